# Optimizing a Trainium2 kernel written in Bass

```python
import math
import jax, jax.numpy as jnp
from jax import lax
import numpy as np

D_MODEL = 1024
BATCH = 32
SEQ = 2048
DEPTH = 4
DEC_BATCH = 32
DEC_SEQ = 16
PAST_LEN = 4096

CHUNK = 64
Q_BLOCK = 128
N_MIX_LAYERS = (DEPTH + 1) // 2
N_POOL_LAYERS = DEPTH // 2

A_DK = 128
A_DV = 128
A_HEADS = (D_MODEL // 2) // A_DV
A_W = A_HEADS * A_DK
B_DQK = 64
B_DV = 2 * B_DQK
B_HEADS = (D_MODEL // 2) // B_DV
B_W = B_HEADS * B_DV
IN_COLS = 4 * A_W + 3 * B_W
MIX_W = A_HEADS * A_DV + B_W
POOL_WINDOWS = (2, 4, 8, 16)
POOL_GROUP = D_MODEL // len(POOL_WINDOWS)
POOL_HIST = max(POOL_WINDOWS) - 1
D_FF = 256 * math.ceil(8 * D_MODEL / 3 / 256)
CONV_W = 3
NUM_BUCKETS = 32
MAX_DISTANCE = 128
DN_ALPHA = (2 * DEPTH) ** 0.25
DN_BETA = (8 * DEPTH) ** -0.25
LN_EPS = 1e-5
RMS_EPS = 1e-6

kernel_name = "hgrn2_diffattn_pool_convffn_stream"


def layer_norm(x, g, b):
    xf = x.astype(jnp.float32)
    mu = jnp.mean(xf, -1, keepdims=True)
    var = jnp.mean(jnp.square(xf - mu), -1, keepdims=True)
    return ((xf - mu) * lax.rsqrt(var + LN_EPS) * g.astype(jnp.float32) + b.astype(jnp.float32)).astype(x.dtype)


def rms_norm(x, g):
    xf = x.astype(jnp.float32)
    return (xf * lax.rsqrt(jnp.mean(xf * xf, -1, keepdims=True) + RMS_EPS) * g.astype(jnp.float32)).astype(x.dtype)


def rel_bucket(rel):
    half = NUM_BUCKETS // 2
    max_exact = half // 2
    base = jnp.where(rel > 0, half, 0)
    n = jnp.abs(rel)
    nf = jnp.maximum(n, 1).astype(jnp.float32)
    large = max_exact + (jnp.log(nf / max_exact) / math.log(MAX_DISTANCE / max_exact)
                         * (half - max_exact)).astype(jnp.int32)
    large = jnp.minimum(large, half - 1)
    return base + jnp.where(n < max_exact, n, large)


def diff_attn(q, k, v, q_pos, k_pos, lam, rel_bias):
    scale = B_DQK ** -0.5
    visible = (k_pos[None, :] // CHUNK) <= (q_pos[:, None] // CHUNK)
    bias = jnp.moveaxis(rel_bias[rel_bucket(k_pos[None, :] - q_pos[:, None])], -1, 0).astype(jnp.float32)
    neg = jnp.finfo(jnp.float32).min

    def probs(qa, ka):
        s = jnp.einsum('bqhd,bkhd->bhqk', qa, ka, preferred_element_type=jnp.float32) * scale + bias
        return jax.nn.softmax(jnp.where(visible, s, neg), axis=-1)

    attn = probs(q[..., :B_DQK], k[..., :B_DQK]) - lam * probs(q[..., B_DQK:], k[..., B_DQK:])
    return jnp.einsum('bhqk,bkhd->bqhd', attn.astype(v.dtype), v)


def hgrn2_chunk(S0, q, log_f, k, v):
    L = q.shape[1]
    b = jnp.cumsum(log_f, axis=1)
    causal = jnp.tril(jnp.ones((L, L), bool))[None, :, :, None, None]
    decay = jnp.where(causal, jnp.exp(jnp.minimum(b[:, :, None] - b[:, None, :], 0.0)), 0.0)
    scores = jnp.einsum('bthk,btshk,bshk->bhts', q, decay, k)
    o = (jnp.einsum('bhts,bshv->bthv', scores, v)
         + jnp.einsum('bthk,bhkv->bthv', q * jnp.exp(b), S0))
    b_last = b[:, -1]
    S = S0 * jnp.exp(b_last)[..., None] + jnp.einsum('bshk,bshv->bhkv', k * jnp.exp(b_last[:, None] - b), v)
    return S, o


def hgrn2_scan(S0, q, log_f, k, v):
    Bsz, T = q.shape[:2]
    L = min(T, CHUNK)
    n = T // L
    to_chunks = lambda t: jnp.moveaxis(t.reshape(Bsz, n, L, *t.shape[2:]), 1, 0)
    S, o = lax.scan(lambda S, c: hgrn2_chunk(S, *c), S0, (to_chunks(q), to_chunks(log_f), to_chunks(k), to_chunks(v)))
    return S, jnp.moveaxis(o, 0, 1).reshape(Bsz, T, *o.shape[3:])


def mixer_ab(x, k_hist, v_hist, S0, lb, lam_init, w_in, w_out, a_norm_g, lam_p, b_norm_g, rel_bias):
    Bsz, T, _ = x.shape
    P = k_hist.shape[1]
    h = x @ w_in
    aq, af, ai, ag, bq, bk, bv = jnp.split(h, np.cumsum([A_W] * 4 + [B_W] * 2).tolist(), axis=-1)
    zf = af.astype(jnp.float32)
    log_f = jnp.log(lb + (1.0 - lb) * jax.nn.sigmoid(zf))
    k_in = (1.0 - lb) * jax.nn.sigmoid(-zf)
    ah = lambda t: t.reshape(Bsz, T, A_HEADS, -1)
    S, oa = hgrn2_scan(S0.astype(jnp.float32), ah(aq.astype(jnp.float32)), ah(log_f), ah(k_in),
                       ah(ai.astype(jnp.float32)))
    oa = rms_norm(oa, a_norm_g.reshape(A_HEADS, A_DV)).reshape(Bsz, T, A_W).astype(x.dtype) * jax.nn.silu(ag)
    lp = lam_p.astype(jnp.float32)
    lam = jnp.exp(jnp.sum(lp[0] * lp[1])) - jnp.exp(jnp.sum(lp[2] * lp[3])) + lam_init
    q = bq.reshape(Bsz, T, B_HEADS, 2 * B_DQK)
    k_new = bk.reshape(Bsz, T, B_HEADS, 2 * B_DQK)
    v_new = bv.reshape(Bsz, T, B_HEADS, B_DV)
    k_all = jnp.concatenate([k_hist.astype(x.dtype), k_new], axis=1)
    v_all = jnp.concatenate([v_hist.astype(x.dtype), v_new], axis=1)
    q_pos = P + jnp.arange(T, dtype=jnp.int32)
    k_pos = jnp.arange(P + T, dtype=jnp.int32)
    if T > Q_BLOCK:
        nb = T // Q_BLOCK
        qb = jnp.moveaxis(q.reshape(Bsz, nb, Q_BLOCK, B_HEADS, 2 * B_DQK), 1, 0)
        ob = lax.map(lambda c: diff_attn(c[0], k_all, v_all, c[1], k_pos, lam, rel_bias),
                     (qb, q_pos.reshape(nb, Q_BLOCK)))
        ob = jnp.moveaxis(ob, 0, 1).reshape(Bsz, T, B_HEADS, B_DV)
    else:
        ob = diff_attn(q, k_all, v_all, q_pos, k_pos, lam, rel_bias)
    ob = (rms_norm(ob, b_norm_g) * (1.0 - lam_init)).reshape(Bsz, T, B_W)
    y = jnp.concatenate([oa, ob], axis=-1) @ w_out
    return y, k_new, v_new, S.astype(x.dtype)


def mixer_pool(x, pos0, hist, w_groups, scale):
    Bsz, T, _ = x.shape
    xh = jnp.concatenate([hist.astype(x.dtype), x], axis=1)
    cs = jnp.cumsum(jnp.pad(xh.astype(jnp.float32), ((0, 0), (1, 0), (0, 0))), axis=1)
    pos = pos0 + jnp.arange(T, dtype=jnp.int32)
    outs = []
    for g, w in enumerate(POOL_WINDOWS):
        sl = slice(g * POOL_GROUP, (g + 1) * POOL_GROUP)
        s = cs[:, POOL_HIST + 1:, sl] - cs[:, POOL_HIST + 1 - w:POOL_HIST + 1 - w + T, sl]
        cnt = jnp.minimum(w, pos + 1).astype(jnp.float32)[None, :, None]
        pooled = (s / cnt).astype(x.dtype) - x[..., sl]
        outs.append(pooled @ w_groups[g])
    y = jnp.concatenate(outs, axis=-1) * scale
    return y, xh[:, -POOL_HIST:]


def conv_ffn(x, hist, w_up, conv_w, conv_b, w_down):
    T = x.shape[1]
    a, b = jnp.split(x @ w_up, 2, axis=-1)
    ah = jnp.concatenate([hist.astype(x.dtype), a], axis=1)
    c = conv_b
    for j in range(CONV_W):
        c = c + ah[:, j:j + T] * conv_w[j]
    return (jax.nn.silu(c) * b) @ w_down, ah[:, -(CONV_W - 1):]


def trunk(x, k_hist, v_hist, s_hist, pool_hist, conv_hist, lower_bounds, w_in_ab, w_out_ab, hgrn_norm_g,
          diff_lambda, diff_norm_g, rel_bias, pool_w, pool_scale, ffn_w_up, ffn_conv_w, ffn_conv_b,
          ffn_w_down, ln1_g, ln1_b, ln2_g, ln2_b):
    pos0 = k_hist[0].shape[1]
    ks, vs, ss, ps, cs = [], [], [], [], []
    for l in range(DEPTH):
        if l % 2 == 0:
            m = l // 2
            lam_init = 0.8 - 0.6 * math.exp(-0.3 * l)
            y, kr, vr, S = mixer_ab(x, k_hist[m], v_hist[m], s_hist[m], lower_bounds[m], lam_init,
                                    w_in_ab[m], w_out_ab[m], hgrn_norm_g[m], diff_lambda[m],
                                    diff_norm_g[m], rel_bias)
            ks.append(kr); vs.append(vr); ss.append(S)
        else:
            p = l // 2
            y, ph = mixer_pool(x, pos0, pool_hist[p], pool_w[p], pool_scale[p])
            ps.append(ph)
        x = layer_norm(DN_ALPHA * x + y, ln1_g[l], ln1_b[l])
        f, ch = conv_ffn(x, conv_hist[l], ffn_w_up[l], ffn_conv_w[l], ffn_conv_b[l], ffn_w_down[l])
        cs.append(ch)
        x = layer_norm(DN_ALPHA * x + f, ln2_g[l], ln2_b[l])
    return x, jnp.stack(ks), jnp.stack(vs), jnp.stack(ss), jnp.stack(ps), jnp.stack(cs)


def setup_inputs(seed: int = 0) -> dict:
    key = jax.random.key(seed)
    ks = jax.random.split(key, 24)
    nrm = lambda k, shape, s: jax.random.normal(k, shape, jnp.float32) * s
    return {
        "x_prompt": nrm(ks[0], (BATCH, SEQ, D_MODEL), 1.0),
        "x_sample": nrm(ks[1], (DEC_BATCH, DEC_SEQ, D_MODEL), 1.0),
        "cache_k": nrm(ks[2], (N_MIX_LAYERS, DEC_BATCH, PAST_LEN, B_HEADS, 2 * B_DQK), 1.0),
        "cache_v": nrm(ks[3], (N_MIX_LAYERS, DEC_BATCH, PAST_LEN, B_HEADS, B_DV), 1.0),
        "state_hgrn": nrm(ks[4], (N_MIX_LAYERS, DEC_BATCH, A_HEADS, A_DK, A_DV), 0.5),
        "state_pool": nrm(ks[5], (N_POOL_LAYERS, DEC_BATCH, POOL_HIST, D_MODEL), 1.0),
        "state_ffn_conv": nrm(ks[6], (DEPTH, DEC_BATCH, CONV_W - 1, D_FF), 1.0),
        "w_in_ab": nrm(ks[7], (N_MIX_LAYERS, D_MODEL, IN_COLS), D_MODEL ** -0.5),
        "w_out_ab": nrm(ks[8], (N_MIX_LAYERS, MIX_W, D_MODEL), DN_BETA * MIX_W ** -0.5),
        "lb_logits": nrm(ks[9], (N_MIX_LAYERS, A_W), 0.5),
        "hgrn_norm_g": 1.0 + nrm(ks[10], (N_MIX_LAYERS, A_W), 0.02),
        "diff_lambda": nrm(ks[11], (N_MIX_LAYERS, 4, B_DQK), 0.1),
        "diff_norm_g": 1.0 + nrm(ks[12], (N_MIX_LAYERS, B_DV), 0.02),
        "rel_bias": nrm(ks[13], (NUM_BUCKETS, B_HEADS), 0.2),
        "pool_w": nrm(ks[14], (N_POOL_LAYERS, len(POOL_WINDOWS), POOL_GROUP, POOL_GROUP), DN_BETA * POOL_GROUP ** -0.5),
        "pool_scale": 1.0 + nrm(ks[15], (N_POOL_LAYERS, D_MODEL), 0.02),
        "ffn_w_up": nrm(ks[16], (DEPTH, D_MODEL, 2 * D_FF), D_MODEL ** -0.5),
        "ffn_conv_w": nrm(ks[17], (DEPTH, CONV_W, D_FF), CONV_W ** -0.5),
        "ffn_conv_b": nrm(ks[18], (DEPTH, D_FF), 0.01),
        "ffn_w_down": nrm(ks[19], (DEPTH, D_FF, D_MODEL), DN_BETA * D_FF ** -0.5),
        "ln1_g": 1.0 + nrm(ks[20], (DEPTH, D_MODEL), 0.02),
        "ln1_b": nrm(ks[21], (DEPTH, D_MODEL), 0.01),
        "ln2_g": 1.0 + nrm(ks[22], (DEPTH, D_MODEL), 0.02),
        "ln2_b": nrm(ks[23], (DEPTH, D_MODEL), 0.01),
    }


def reference(x_prompt, x_sample, cache_k, cache_v, state_hgrn, state_pool, state_ffn_conv, w_in_ab, w_out_ab,
              lb_logits, hgrn_norm_g, diff_lambda, diff_norm_g, rel_bias, pool_w, pool_scale, ffn_w_up,
              ffn_conv_w, ffn_conv_b, ffn_w_down, ln1_g, ln1_b, ln2_g, ln2_b):
    sm = jax.nn.softmax(lb_logits.astype(jnp.float32), axis=0)
    lower_bounds = jnp.cumsum(sm, axis=0) - sm[0]
    weights = (lower_bounds, w_in_ab, w_out_ab, hgrn_norm_g, diff_lambda, diff_norm_g, rel_bias, pool_w,
               pool_scale, ffn_w_up, ffn_conv_w, ffn_conv_b, ffn_w_down, ln1_g, ln1_b, ln2_g, ln2_b)
    Bp, dt = x_prompt.shape[0], x_prompt.dtype
    y_prompt, k_p, v_p, s_p, pool_p, conv_p = trunk(
        x_prompt,
        [jnp.zeros((Bp, 0, B_HEADS, 2 * B_DQK), dt)] * N_MIX_LAYERS,
        [jnp.zeros((Bp, 0, B_HEADS, B_DV), dt)] * N_MIX_LAYERS,
        [jnp.zeros((Bp, A_HEADS, A_DK, A_DV), dt)] * N_MIX_LAYERS,
        [jnp.zeros((Bp, POOL_HIST, D_MODEL), dt)] * N_POOL_LAYERS,
        [jnp.zeros((Bp, CONV_W - 1, D_FF), dt)] * DEPTH,
        *weights)
    y_sample, k_s, v_s, s_s, pool_s, conv_s = trunk(
        x_sample, cache_k, cache_v, state_hgrn, state_pool, state_ffn_conv, *weights)
    return (y_prompt, y_sample, k_p, v_p, s_p, pool_p, conv_p, k_s, v_s, s_s, pool_s, conv_s)
```

```python
import math
from contextlib import ExitStack
import numpy as np
import concourse.bass as bass
import concourse.mybir as mybir
from concourse.bass_utils import run_bass_kernel_spmd

F32, BF16 = mybir.dt.float32, mybir.dt.bfloat16
AF = mybir.ActivationFunctionType
ALU = mybir.AluOpType

D = 1024
NCH = 8
DFF = 2816
NJ = 22
INC = 3584
DEPTH = 4
ALPHA = (2 * DEPTH) ** 0.25
INV_ALPHA = 1.0 / ALPHA
LN_EPS_S = 1e-5 / (ALPHA * ALPHA)
RMS_EPS = 1e-6
NREL = 383
POOL_W = (2, 4, 8, 16)


class Op:
    __slots__ = ("eng", "fn", "deps", "dma", "needed", "tok")


class Prog:
    NS = {"sp": 24, "pool": 12}

    def __init__(self):
        self.ops = []
        self.lw = {}
        self.rd = {}

    def add(self, eng, fn, reads=(), writes=(), dma=False):
        op = Op()
        op.eng, op.fn, op.dma, op.needed, op.tok = eng, fn, dma, False, None
        deps = {}
        reads = list(reads) + ["EPOCH"]
        writes = list(writes)
        if eng != "pe":
            writes += [k for k in reads if isinstance(k, tuple) and k[0] == "ps" and k not in writes]
        if any(isinstance(k, tuple) and k[0] == "A" for k in reads + writes):
            reads.append("ARENA")
        for k in reads:
            w = self.lw.get(k)
            if w is not None:
                deps[id(w)] = (w, "raw")
        for k in writes:
            w = self.lw.get(k)
            if w is not None and id(w) not in deps:
                deps[id(w)] = (w, "waw")
            for r in self.rd.get(k, ()):
                if id(r) not in deps:
                    deps[id(r)] = (r, "war")
        out = []
        for d, kind in deps.values():
            if d is op:
                continue
            if (not d.dma) and (not dma) and d.eng == eng:
                if eng == "pe" or kind != "raw":
                    continue
            out.append(d)
        op.deps = out
        for k in reads:
            self.rd.setdefault(k, []).append(op)
        for k in writes:
            self.lw[k] = op
            self.rd[k] = []
        self.ops.append(op)
        return op

    def finalize(self, sems):
        cnt = {}
        dcount = {"sp": 0, "pool": 0}
        dops = {"sp": [], "pool": []}
        for op in self.ops:
            for d in op.deps:
                d.needed = True
        for op in self.ops:
            if op.dma:
                q = op.eng
                i = dcount[q]
                ns = self.NS[q]
                op.tok = (sems["dma_" + q][i % ns], 16 * (i // ns + 1))
                if i >= ns:
                    op.deps.append(dops[q][i - ns])
                dops[q].append(op)
                dcount[q] = i + 1
            elif op.needed:
                cnt[op.eng] = cnt.get(op.eng, 0) + 1
                op.tok = (sems[op.eng], cnt[op.eng])

    def run_stream(self, name, e):
        waited = {}
        for op in self.ops:
            if op.eng != name:
                continue
            req = {}
            for d in op.deps:
                s, v = d.tok
                k = id(s)
                if k not in req or req[k][1] < v:
                    req[k] = (s, v)
            for k, (s, v) in req.items():
                if waited.get(k, 0) < v:
                    e.wait_ge(s, v)
                    waited[k] = v
            ins = op.fn(e)
            if op.dma:
                ins.then_inc(op.tok[0], 16)
            elif op.needed:
                ins.then_inc(op.tok[0], 1)


def rel_bucket_np(rel):
    half, max_exact = 16, 8
    base = np.where(rel > 0, half, 0)
    n = np.abs(rel)
    nf = np.maximum(n, 1).astype(np.float32)
    large = max_exact + (np.log(nf / np.float32(max_exact)) / np.float32(math.log(128 / max_exact))
                         * (half - max_exact)).astype(np.int32)
    large = np.minimum(large, half - 1)
    return base + np.where(n < max_exact, n, large)


def host_consts(nss, ds):
    c = {}
    c["c_ident"] = np.eye(128, dtype=np.float32)
    s = np.arange(128)[:, None]
    t = np.arange(128)[None, :]
    c["c_maskp"] = ((s // 64 == t // 64) & (s <= t)).astype(np.float32)
    s = np.arange(ds)[:, None]
    t = np.arange(ds)[None, :]
    c["c_masks"] = (s <= t).astype(np.float32)
    c["c_resetp"] = (np.arange(512) % 64 != 0).astype(np.float32)
    c["c_resets"] = (np.arange(nss * ds) % ds != 0).astype(np.float32)
    k = np.arange(128)[:, None]
    q = np.arange(128)[None, :]
    c["c_vis0"] = (k // 64 <= q // 64).astype(np.float32)
    rel = 127 - np.arange(NREL)
    b = rel_bucket_np(rel.astype(np.int32))
    oh = np.zeros((32, NREL), np.float32)
    oh[b, np.arange(NREL)] = 1.0
    c["c_oht"] = oh
    invc = np.zeros((4, 16), np.float32)
    for g, w in enumerate(POOL_W):
        invc[g] = 1.0 / np.minimum(w, np.arange(16) + 1)
    c["c_invc"] = invc
    return c


def build(NPS, SEQ, NSS, DS, PAST):
    nc = bass.Bass("TRN2", target_bir_lowering=False)
    P = Prog()

    def din(name, shape):
        return nc.dram_tensor(name, list(shape), F32, kind="ExternalInput")

    def dout(name, shape):
        return nc.dram_tensor(name, list(shape), F32, kind="ExternalOutput")

    def dint(name, shape, dt):
        return nc.dram_tensor(name, list(shape), dt, kind="Internal")

    NTS = NSS * DS
    x_p = din("x_p", [NPS, SEQ, D]).ap()
    x_s = din("x_s", [NSS, DS, D]).ap()
    cache_k = din("cache_k", [2, NSS, PAST, 512]).ap()
    cache_v = din("cache_v", [2, NSS, PAST, 512]).ap()
    state_hgrn = din("state_hgrn", [2, NSS, 4, 128, 128]).ap()
    state_pool = din("state_pool", [2, NSS, 15, D]).ap()
    state_conv = din("state_conv", [4, NSS, 2, DFF]).ap()
    w_in = din("w_in_ab", [2, D, INC]).ap()
    w_out = din("w_out_ab", [2, D, D]).ap()
    lb_logits = din("lb_logits", [2, 512]).ap()
    hgrn_norm_g = din("hgrn_norm_g", [2, 512]).ap()
    diff_lambda = din("diff_lambda", [2, 256]).ap()
    diff_norm_g = din("diff_norm_g", [2, 128]).ap()
    rel_bias = din("rel_bias", [32, 4]).ap()
    pool_w = din("pool_w", [2, 4, 256, 256]).ap()
    pool_scale = din("pool_scale", [2, D]).ap()
    w_up = din("ffn_w_up", [4, D, 2 * DFF]).ap()
    conv_w = din("ffn_conv_w", [4, 3, DFF]).ap()
    conv_b = din("ffn_conv_b", [4, DFF]).ap()
    w_down = din("ffn_w_down", [4, DFF, D]).ap()
    ln1_g = din("ln1_g", [4, D]).ap()
    ln1_b = din("ln1_b", [4, D]).ap()
    ln2_g = din("ln2_g", [4, D]).ap()
    ln2_b = din("ln2_b", [4, D]).ap()
    c_ident = din("c_ident", [128, 128]).ap()
    c_maskp = din("c_maskp", [128, 128]).ap()
    c_masks = din("c_masks", [DS, DS]).ap()
    c_resetp = din("c_resetp", [512]).ap()
    c_resets = din("c_resets", [NTS]).ap()
    c_vis0 = din("c_vis0", [128, 128]).ap()
    c_oht = din("c_oht", [32, NREL]).ap()
    c_invc = din("c_invc", [4, 16]).ap()
    y_p = dout("y_p", [NPS, SEQ, D]).ap()
    y_s = dout("y_s", [NSS, DS, D]).ap()
    k_p = dout("k_p", [2, NPS, SEQ, 512]).ap()
    v_p = dout("v_p", [2, NPS, SEQ, 512]).ap()
    h_p = dout("h_p", [2, NPS, 4, 128, 128]).ap()
    pl_p = dout("pl_p", [2, NPS, 15, D]).ap()
    cv_p = dout("cv_p", [4, NPS, 2, DFF]).ap()
    k_s = dout("k_s", [2, NSS, DS, 512]).ap()
    v_s = dout("v_s", [2, NSS, DS, 512]).ap()
    h_s = dout("h_s", [2, NSS, 4, 128, 128]).ap()
    pl_s = dout("pl_s", [2, NSS, 15, D]).ap()
    cv_s = dout("cv_s", [4, NSS, 2, DFF]).ap()
    win_bf = dint("win_bf", [2, 14, 128, 8, 256], BF16).ap()
    wout_bf = dint("wout_bf", [2, 4, 128, 8, 256], BF16).ap()
    poolw_bf = dint("poolw_bf", [2, 128, 2, 4, 256], BF16).ap()
    wup_bf = dint("wup_bf", [4, NJ, 128, 8, 256], BF16).ap()
    wdn_bf = dint("wdn_bf", [4, 4, 128, NJ, 256], BF16).ap()
    evm_t = dint("evm", [4, 128, NREL], F32)
    evm = evm_t.ap()

    es = ExitStack()
    ARENA_BYTES = 207 * 1024
    arena = es.enter_context(nc.sbuf_tensor("arena", [128, ARENA_BYTES // 4], F32))
    psb = [es.enter_context(nc.psum_tensor(f"ps{i}", [128, 512], F32)) for i in range(8)]
    sems = {}
    for n in ("pe", "act", "dve", "pool", "sp"):
        sems[n] = es.enter_context(nc.semaphore("sem_" + n))
    for q in ("sp", "pool"):
        sems["dma_" + q] = [es.enter_context(nc.semaphore(f"dsem_{q}{i}")) for i in range(Prog.NS[q])]

    top = [0]

    def carve(nbytes):
        o = top[0]
        top[0] = (o + nbytes + 63) // 64 * 64
        assert top[0] <= ARENA_BYTES, ("arena overflow", top[0])
        return o

    def view(off, dt, shape, parts=128):
        n = int(np.prod(shape))
        esz = 4 if dt == F32 else 2
        nb = n * esz
        assert off % 4 == 0
        a = arena[0:parts, off // 4: off // 4 + (nb + 3) // 4]
        if dt == BF16:
            a = a.bitcast(BF16)
            if a.shape[1] != n:
                a = a[:, 0:n]
        if len(shape) == 2:
            a = a.rearrange("p (a b) -> p a b", a=shape[0])
        elif len(shape) == 3:
            a = a.rearrange("p (a b c) -> p a b c", a=shape[0], b=shape[1])
        return a

    o_ident = carve(512); ident = view(o_ident, F32, (128,))
    o_identb = carve(256); identb = view(o_identb, BF16, (128,))
    o_onesb = carve(256); onesb = view(o_onesb, BF16, (128,))
    o_maskp = carve(256); maskp = view(o_maskp, BF16, (128,))
    o_masks = carve(64); masks = view(o_masks, BF16, (DS,))
    o_resetp = carve(1024); resetp = view(o_resetp, BF16, (512,))
    o_resets = carve(2 * NTS + 4); resets = view(o_resets, BF16, (NTS,))
    o_Ep = carve(2 * 4 * 256); Ep = view(o_Ep, BF16, (2, 4, 128))
    o_Esh = carve(4 * DS * 2); Esh = view(o_Esh, BF16, (4, DS))
    o_Esn = carve(4 * DS * 2); Esn = view(o_Esn, BF16, (4, DS))
    o_lng = carve(4 * 2 * 8 * 4); lng = view(o_lng, F32, (4, 2, 8))
    o_lnb = carve(4 * 2 * 8 * 4); lnb = view(o_lnb, F32, (4, 2, 8))
    o_cw = carve(4 * 3 * NJ * 4); cw = view(o_cw, F32, (4, 3, NJ))
    o_cb = carve(4 * NJ * 4); cbias = view(o_cb, F32, (4, NJ))
    o_lb = carve(2 * 4 * 4); lbv = view(o_lb, F32, (2, 4))
    o_oml = carve(2 * 4 * 4); omlv = view(o_oml, F32, (2, 4))
    o_noml = carve(2 * 4 * 4); nomlv = view(o_noml, F32, (2, 4))
    o_hg = carve(2 * 4 * 4); hgv = view(o_hg, F32, (2, 4))
    o_dg = carve(2 * 4); dgv = view(o_dg, F32, (2,))
    o_nl = carve(2 * 4); neglam = view(o_nl, F32, (2,))
    o_psc = carve(2 * 8 * 4); pscv = view(o_psc, F32, (2, 8))
    o_invc = carve(4 * 16 * 4); invc = view(o_invc, F32, (4, 16))
    o_epsc = carve(8); epsc = view(o_epsc, F32, (2,))
    o_tmpc = carve(2304); tmpc = view(o_tmpc, F32, (576,))
    NSM = max(1, NSS)
    o_S = carve(NSM * 4 * 128 * 4); Sst = view(o_S, F32, (NSM, 4, 128))
    o_Sb = carve(256); Sbf = view(o_Sb, BF16, (128,))
    o_atail = carve(NJ * NSM * 2 * 4); atail = view(o_atail, F32, (NJ, NSM, 2))
    o_small = carve(12 * 8 * 4); smallv = view(o_small, F32, (12, 8))
    o_hout = carve(4 * 512 * 2); hout = view(o_hout, BF16, (4, 512))
    NF, NB = 7, 8
    FW = 528
    o_sf = [carve(FW * 4) for _ in range(NF)]
    o_sb = [carve(512 * 2) for _ in range(NB)]
    o_stg = [carve(4096) for _ in range(2)]
    NR = 2
    o_ring = [carve(4096) for _ in range(NR)]
    base_top = top[0]
    print("base_top", base_top)

    sf_i, sb_i, stg_i, ring_i = [0], [0], [0], [0]

    def sF():
        i = sf_i[0] % NF; sf_i[0] += 1
        return view(o_sf[i], F32, (FW,)), ("sf", i)

    def sB():
        i = sb_i[0] % NB; sb_i[0] += 1
        return view(o_sb[i], BF16, (512,)), ("sb", i)

    def FX(i):
        return view(o_sf[i], F32, (FW,)), ("sf", i)

    def BX(i):
        return view(o_sb[i], BF16, (512,)), ("sb", i)

    def sFr(lo, hi):
        n = hi - lo
        i = lo + sf_i[0] % n; sf_i[0] += 1
        return view(o_sf[i], F32, (FW,)), ("sf", i)

    def sStage():
        i = stg_i[0] % 2; stg_i[0] += 1
        return view(o_stg[i], F32, (1024,)), ("stg", i)

    def sRing():
        i = ring_i[0] % NR; ring_i[0] += 1
        return view(o_ring[i], BF16, (8, 256)), ("ring", i)

    ps_i = [0]

    def PS(lo=0, hi=8):
        n = hi - lo
        i = lo + ps_i[0] % n; ps_i[0] += 1
        return psb[i], ("ps", i)

    def act(out, in_, func, reads, writes, bias=None, scale=None, accum=None):
        kw = {}
        if bias is not None: kw["bias"] = bias
        if scale is not None: kw["scale"] = scale
        if accum is not None: kw["accum_out"] = accum
        P.add("act", lambda e: e.activation(out=out, in_=in_, func=func, **kw), reads, writes)

    def tt(out, in0, in1, op, reads, writes, eng="dve"):
        P.add(eng, lambda e: e.tensor_tensor(out=out, in0=in0, in1=in1, op=op), reads, writes)

    def ts(out, in0, s1, s2, op0, op1, reads, writes, eng="dve"):
        if op1 is None:
            P.add(eng, lambda e: e.tensor_scalar(out=out, in0=in0, scalar1=s1, scalar2=None, op0=op0), reads, writes)
        else:
            P.add(eng, lambda e: e.tensor_scalar(out=out, in0=in0, scalar1=s1, scalar2=s2, op0=op0, op1=op1), reads, writes)

    def stt(out, in0, scalar, in1, op0, op1, reads, writes):
        P.add("dve", lambda e: e.scalar_tensor_tensor(out=out, in0=in0, scalar=scalar, in1=in1, op0=op0, op1=op1), reads, writes)

    def cp(out, in_, reads, writes, eng="dve"):
        P.add(eng, lambda e: e.tensor_copy(out=out, in_=in_), reads, writes)

    def mm(out, lhsT, rhs, start, stop, reads, writes):
        P.add("pe", lambda e: e.matmul(out, lhsT=lhsT, rhs=rhs, start=start, stop=stop), reads, writes)

    def tr(out, in_, idn, reads, writes):
        P.add("pe", lambda e: e.transpose(out, in_, idn), reads, writes)

    def dma(q, out, in_, reads, writes, slow=False):
        if slow:
            P.add(q, lambda e: e.dma_start(out=out, in_=in_, allow_slow_non_contiguous=True), reads, writes, dma=True)
        else:
            P.add(q, lambda e: e.dma_start(out=out, in_=in_), reads, writes, dma=True)

    def memset(eng, ap, val, writes):
        P.add(eng, lambda e: e.memset(ap, val), (), writes)

    def rsqrt(out, in_, which, reads, writes):
        act(out, in_, AF.Sqrt, list(reads) + ["epsc"], writes, bias=epsc[:, which:which + 1])
        P.add("dve", lambda e: e.reciprocal(out=out, in_=out), writes, writes)

    def barrier():
        P.add("pool", lambda e: e.memset(tmpc[0:1, 0:1], 0.0), (), ["EPOCH", "ARENA"])

    def arena_barrier():
        P.add("pool", lambda e: e.memset(tmpc[0:1, 0:1], 0.0), (), ["ARENA"])

    def cast_win(m):
        for kc in range(8):
            src = w_in[m, kc * 128:(kc + 1) * 128, :].rearrange("p (g c) -> p g c", g=14)
            dst = win_bf[m, :, :, kc, :].rearrange("g p c -> p g c")
            dma("pool", dst, src, (), [("win", m)])

    def cast_wout(m):
        for kc in range(8):
            dma("pool", wout_bf[m, :, :, kc, :].rearrange("q p c -> p q c"),
                w_out[m, kc * 128:(kc + 1) * 128, :].rearrange("p (q c) -> p q c", q=4), (), [("wout", m)])

    def cast_poolw(p):
        for g in range(4):
            for kc in range(2):
                dma("pool", poolw_bf[p, :, kc, g, :], pool_w[p, g, kc * 128:(kc + 1) * 128, :], (), [("poolw", p)])

    def cast_wup(l):
        for kc in range(8):
            for ab in range(2):
                src = w_up[l, kc * 128:(kc + 1) * 128, ab * DFF:(ab + 1) * DFF].rearrange("p (j c) -> p j c", j=NJ)
                dst = wup_bf[l, :, :, kc, ab * 128:(ab + 1) * 128].rearrange("j p c -> p j c")
                dma("pool", dst, src, (), [("wup", l)])

    def cast_wdn(l):
        for j in range(NJ):
            src = w_down[l, j * 128:(j + 1) * 128, :].rearrange("p (h c) -> p h c", h=4)
            dst = wdn_bf[l, :, :, j, :].rearrange("h p c -> p h c")
            dma("pool", dst, src, (), [("wdn", l)])

    def setup():
        dma("sp", ident, c_ident, (), ["ident"])
        t1, k1 = sF()
        dma("sp", t1[:, 0:128], c_maskp, (), [k1])
        cp(maskp, t1[:, 0:128], [k1], ["maskp"])
        cp(identb, ident, ["ident"], ["identb"])
        memset("dve", onesb, 1.0, ["onesb"])
        memset("dve", epsc[:, 0:1], LN_EPS_S, ["epsc"])
        memset("dve", epsc[:, 1:2], 128.0 * RMS_EPS, ["epsc"])
        t2, k2 = sF()
        dma("sp", t2[0:DS, 0:DS], c_masks, (), [k2])
        cp(masks[0:DS, :], t2[0:DS, 0:DS], [k2], ["masks"])
        t3, k3 = sF()
        dma("sp", t3[:, 0:512], c_resetp.partition_broadcast(128), (), [k3])
        cp(resetp, t3[:, 0:512], [k3], ["resetp"])
        t4, k4 = sF()
        dma("sp", t4[:, 0:NTS], c_resets.partition_broadcast(128), (), [k4])
        cp(resets, t4[:, 0:NTS], [k4], ["resets"])
        dma("sp", invc, bass.AP(c_invc.tensor, 0, [[0, 128], [16, 4], [1, 16]]), (), ["invc"])
        for l in range(4):
            for w, (g_, b_) in enumerate(((ln1_g, ln1_b), (ln2_g, ln2_b))):
                dma("sp", lng[:, l, w, :], g_[l].rearrange("(c p) -> p c", p=128), (), ["lnp"], slow=True)
                dma("sp", lnb[:, l, w, :], b_[l].rearrange("(c p) -> p c", p=128), (), ["lnp"], slow=True)
            for r in range(3):
                dma("sp", cw[:, l, r, :], conv_w[l, r].rearrange("(j p) -> p j", p=128), (), ["cw"], slow=True)
            dma("sp", cbias[:, l, :], conv_b[l].rearrange("(j p) -> p j", p=128), (), ["cw"], slow=True)
        for m in range(2):
            dma("sp", hgv[:, m, :], hgrn_norm_g[m].rearrange("(h p) -> p h", p=128), (), ["hg"], slow=True)
            dma("sp", dgv[:, m:m + 1], diff_norm_g[m].rearrange("(p o) -> p o", o=1), (), ["dg"], slow=True)
            dma("sp", pscv[:, m, :], pool_scale[m].rearrange("(c p) -> p c", p=128), (), ["psc"], slow=True)
        ts(hgv, hgv, math.sqrt(128.0), None, ALU.mult, None, ["hg"], ["hg"])
        for m in range(2):
            lam_init = 0.8 - 0.6 * math.exp(-0.3 * (2 * m))
            ts(dgv[:, m:m + 1], dgv[:, m:m + 1], math.sqrt(128.0) * (1.0 - lam_init), None, ALU.mult, None, ["dg"], ["dg"])
        ts(pscv, pscv, INV_ALPHA, None, ALU.mult, None, ["psc"], ["psc"])
        l0 = tmpc[:, 0:4]; l1 = tmpc[:, 4:8]
        dma("sp", l0, lb_logits[0].rearrange("(h p) -> p h", p=128), (), ["tmpc"], slow=True)
        dma("sp", l1, lb_logits[1].rearrange("(h p) -> p h", p=128), (), ["tmpc"], slow=True)
        tt(tmpc[:, 8:12], l1, l0, ALU.subtract, ["tmpc"], ["tmpc2"])
        memset("dve", lbv[:, 0, :], 0.0, ["lb"])
        act(lbv[:, 1, :], tmpc[:, 8:12], AF.Sigmoid, ["tmpc2", "lb"], ["lb"])
        ts(omlv, lbv, -1.0, 1.0, ALU.mult, ALU.add, ["lb"], ["oml"])
        ts(nomlv, omlv, -1.0, None, ALU.mult, None, ["oml"], ["noml"])
        for m in range(2):
            lam_init = 0.8 - 0.6 * math.exp(-0.3 * (2 * m))
            lp = tmpc[:, 16:16 + 256]
            dma("sp", lp, diff_lambda[m].partition_broadcast(128), ["tmpcl"], ["tmpcl"])
            pr = tmpc[:, 300:300 + 128]
            tt(pr.rearrange("p (a b) -> p a b", a=2), lp.rearrange("p (a t b) -> p a t b", a=2, t=2)[:, :, 0, :],
               lp.rearrange("p (a t b) -> p a t b", a=2, t=2)[:, :, 1, :], ALU.mult, ["tmpcl"], ["tmpcp"])
            sm_ = tmpc[:, 440:442]
            P.add("dve", lambda e, o=sm_, i=pr: e.tensor_reduce(out=o, in_=i.rearrange("p (a b) -> p a b", a=2),
                                                                 axis=mybir.AxisListType.X, op=ALU.add), ["tmpcp"], ["tmpcs"])
            ex = tmpc[:, 444:446]
            act(ex, sm_, AF.Exp, ["tmpcs"], ["tmpce"])
            tt(neglam[:, m:m + 1], ex[:, 1:2], ex[:, 0:1], ALU.subtract, ["tmpce"], ["nl"])
            ts(neglam[:, m:m + 1], neglam[:, m:m + 1], -lam_init, None, ALU.add, None, ["nl"], ["nl", "tmpcl"])
        rb = tmpc[0:32, 500:504]
        oht, ko = sStage()
        dma("sp", rb, rel_bias, (), ["rb"])
        dma("sp", oht[0:32, 0:NREL], c_oht, (), [ko])
        for h in range(4):
            rbb, kr = sF()
            cp(rbb[0:32, 0:128], rb[:, h:h + 1].to_broadcast([32, 128]), ["rb"], [kr])
            pt, kp = PS()
            mm(pt[:, 0:NREL], rbb[0:32, 0:128], oht[0:32, 0:NREL], True, True, [kr, ko], [kp])
            ngc = tmpc[:, 510 + h:511 + h]
            ts(ngc, pt[:, NREL - 1:NREL], -1.0, None, ALU.mult, None, [kp], [("ngc", h)])
            ev, ke = sF()
            act(ev[:, 0:NREL], pt[:, 0:NREL], AF.Exp, [kp, ("ngc", h)], [ke], bias=ngc)
            dma("sp", evm[h], ev[:, 0:NREL], [ke], [("evm", h)])
        vis, kv = sStage()
        dma("sp", vis[:, 0:128], c_vis0, (), [kv])
        for h in range(4):
            for di, d in enumerate((0, -128)):
                et, ke = sF()
                src = bass.AP(evm_t, h * 128 * NREL + (127 - d), [[NREL - 1, 128], [1, 128]])
                dma("sp", et[:, 0:128], src, [("evm", h)], [ke])
                if d == 0:
                    tt(Ep[:, di, h, :], et[:, 0:128], vis[:, 0:128], ALU.mult, [ke, kv], ["Ep"])
                else:
                    cp(Ep[:, di, h, :], et[:, 0:128], [ke], ["Ep"])
            et, ke = sF()
            src = bass.AP(evm_t, h * 128 * NREL + (127 + 128), [[NREL - 1, 128], [1, DS]])
            dma("sp", et[:, 0:DS], src, [("evm", h)], [ke])
            cp(Esh[:, h, :], et[:, 0:DS], [ke], ["Es"])
            et, ke = sF()
            src = bass.AP(evm_t, h * 128 * NREL + 127, [[NREL - 1, DS], [1, DS]])
            dma("sp", et[0:DS, 0:DS], src, [("evm", h)], [ke])
            cp(Esn[0:DS, h, :], et[0:DS, 0:DS], [ke], ["Es"])

    class Grp:
        pass

    def make_group(kind, seq_ids):
        G = Grp()
        G.kind = kind
        G.seqs = list(seq_ids)
        G.nseq = len(G.seqs)
        G.T = SEQ if kind == "p" else DS
        G.P = 0 if kind == "p" else PAST
        G.NT = G.nseq * G.T
        G.W = min(512, G.NT)
        G.nt = G.NT // G.W
        G.L = min(64, G.T)
        G.blocks = []
        for ti in range(G.nt):
            bl = []
            if kind == "p":
                for j in range(G.W // 128):
                    bl.append((128 * j, 128, 0, ti * G.W + 128 * j))
            else:
                for s in range(G.nseq):
                    bl.append((G.T * s, G.T, s, 0))
            G.blocks.append(bl)
        G.ns_t = 1 if kind == "p" else G.nseq
        G.Ts = G.W // G.ns_t
        top[0] = base_top
        G.o_xres = carve(8 * G.NT * 4)
        G.o_xbf = carve(8 * G.NT * 2)
        KT = G.P + G.T
        G.KT = KT
        G.nkh = G.P // 128
        if kind == "p":
            a0 = top[0]
            G.o_qT = carve(4 * G.NT * 2); G.o_kT = carve(4 * KT * 2); G.o_V = carve((G.T // 128) * 512 * 2)
            G.o_poolw = G.o_qT
            a1 = top[0]
            top[0] = a0
            G.o_gT = carve(NJ * G.W * 2); G.o_wdn = carve(2 * NJ * 256 * 2)
            top[0] = max(top[0], a1)
            G.alias = True
        else:
            G.o_qT = carve(4 * G.NT * 2); G.o_kT = carve(4 * KT * 2); G.o_V = carve((G.nkh + 1) * 512 * 2)
            G.o_poolw = carve(2 * 4 * 256 * 2)
            G.o_gT = carve(NJ * G.W * 2); G.o_wdn = carve(2 * NJ * 256 * 2)
            G.alias = False
        print("group", kind, "arena top", top[0], "of", ARENA_BYTES)
        G.xres = view(G.o_xres, F32, (8, G.NT))
        G.xbf = view(G.o_xbf, BF16, (8, G.NT))
        G.qT = view(G.o_qT, BF16, (4, G.NT))
        G.kT = view(G.o_kT, BF16, (4, KT))
        G.V = view(G.o_V, BF16, ((G.T // 128) if kind == "p" else (G.nkh + 1), 512))
        G.poolw = view(G.o_poolw, BF16, (2, 4, 256))
        G.gT = view(G.o_gT, BF16, (NJ, G.W))
        G.wdn = view(G.o_wdn, BF16, (2, NJ, 256))
        if kind == "p":
            G.x_in, G.y_out, G.k_out, G.v_out, G.h_out, G.pl_out, G.cv_out = x_p, y_p, k_p, v_p, h_p, pl_p, cv_p
            G.maskH, G.reset = maskp, resetp
        else:
            G.x_in, G.y_out, G.k_out, G.v_out, G.h_out, G.pl_out, G.cv_out = x_s, y_s, k_s, v_s, h_s, pl_s, cv_s
            G.maskH, G.reset = masks, resets
        return G

    def kx(oc, ti): return ("xres", oc, ti)
    def kb(oc, ti): return ("A", "xbf", oc, ti)

    def load_x(G):
        import os
        KSUB = int(os.environ.get("KSUB", "9"))
        for ti in range(G.nt):
            c0 = ti * G.W
            banks = [PS() for _ in range(8)]
            for (co, nb, s, t0) in G.blocks[ti]:
                st, ks_ = sStage()
                dma("sp", st[0:nb, :], G.x_in[G.seqs[s], t0:t0 + nb, :], (), [ks_])
                for oc in range(8 if KSUB >= 1 else 0):
                    pb_, kp = banks[oc]
                    tr(pb_[:, co:co + nb], st[0:nb, oc * 128:(oc + 1) * 128], ident[0:nb, 0:nb], [ks_, "ident"], [kp])
            for oc in range(8 if KSUB >= 2 else 0):
                pb_, kp = banks[oc]
                act(G.xres[:, oc, c0:c0 + G.W], pb_[:, 0:G.W], AF.Copy, [kp], [kx(oc, ti)])
                if KSUB == 3:
                    cp(G.xbf[:, oc, c0:c0 + G.W], G.xres[:, oc, c0:c0 + G.W], [kx(oc, ti)], [kb(oc, ti)])
                elif KSUB == 5:
                    cp(G.xbf[:, oc, c0:c0 + G.W], pb_[:, 0:G.W], [kp], [("xbftest", oc, ti)])
                elif KSUB == 6:
                    cp(G.xbf[:, oc, c0:c0 + G.W], pb_[:, 0:G.W], [kp, kx(oc, ti)], [kb(oc, ti)])
                elif KSUB == 7:
                    t_, k_ = sF()
                    cp(t_[:, 0:G.W], pb_[:, 0:G.W], [kp], [k_])
                elif KSUB > 7:
                    cp(G.xbf[:, oc, c0:c0 + G.W], pb_[:, 0:G.W], [kp], [kb(oc, ti)])

    def store_y(G):
        import os
        KSUB = int(os.environ.get("KSUB", "9"))
        if KSUB < 4:
            return
        for ti in range(G.nt):
            c0 = ti * G.W
            for (co, nb, s, t0) in G.blocks[ti]:
                st, ks_ = sStage()
                for half in range(2):
                    pb_, kp = PS()
                    for o4 in range(4):
                        oc = half * 4 + o4
                        tr(pb_[0:nb, o4 * 128:(o4 + 1) * 128], G.xres[:, oc, c0 + co:c0 + co + nb], ident, [kx(oc, ti), "ident"], [kp])
                    if half == 0:
                        act(st[0:nb, 0:512], pb_[0:nb, :], AF.Copy, [kp], [ks_])
                    else:
                        cp(st[0:nb, 512:1024], pb_[0:nb, :], [kp], [ks_])
                dma("pool", G.y_out[G.seqs[s], t0:t0 + nb, :], st[0:nb, :], [ks_], [("yout",)])

    def ln_tile(G, l, which, ti, ychunk, yscale=None):
        W = G.W
        c0 = ti * W
        s1, k1 = psb[6], ("ps", 6)
        s2, k2 = psb[7], ("ps", 7)
        for oc in range(8):
            yp, ky = ychunk(oc)
            xr = G.xres[:, oc, c0:c0 + W]
            sc = INV_ALPHA if yscale is None else yscale(oc)
            stt(xr, yp[:, 0:W], sc, xr, ALU.mult, ALU.add, [ky, kx(oc, ti)] + ([] if yscale is None else ["psc"]), [kx(oc, ti)])
            rb_, kr = sB()
            act(rb_[:, 0:W], xr, AF.Copy, [kx(oc, ti)], [kr])
            sq_, kq = sB()
            act(sq_[:, 0:W], xr, AF.Square, [kx(oc, ti)], [kq])
            mm(s1[:, 0:W], onesb, rb_[:, 0:W], oc == 0, oc == 7, [kr, "onesb"], [k1])
            mm(s2[:, 0:W], onesb, sq_[:, 0:W], oc == 0, oc == 7, [kq, "onesb"], [k2])
        mean, km = FX(0)
        ts(mean[:, 0:W], s1[:, 0:W], 1.0 / D, None, ALU.mult, None, [k1], [km])
        msq, kq2 = FX(1)
        tt(msq[:, 0:W], mean[:, 0:W], mean[:, 0:W], ALU.mult, [km], [kq2])
        rstd, krs = FX(2)
        stt(rstd[:, 0:W], s2[:, 0:W], 1.0 / D, msq[:, 0:W], ALU.mult, ALU.subtract, [k2, kq2], [krs])
        rsqrt(rstd[:, 0:W], rstd[:, 0:W], 0, [krs], [krs])
        for oc in range(8):
            xr = G.xres[:, oc, c0:c0 + W]
            t_, kt = sFr(3, NF)
            tt(t_[:, 0:W], xr, mean[:, 0:W], ALU.subtract, [kx(oc, ti), km], [kt])
            tt(t_[:, 0:W], t_[:, 0:W], rstd[:, 0:W], ALU.mult, [kt, krs], [kt])
            act(xr, t_[:, 0:W], AF.Identity, [kt, "lnp"], [kx(oc, ti)], bias=lnb[:, l, which, oc:oc + 1], scale=lng[:, l, which, oc:oc + 1])
            cp(G.xbf[:, oc, c0:c0 + W], xr, [kx(oc, ti)], [kb(oc, ti)], eng="pool")

    def ffn(G, l):
        W, ns, Ts = G.W, G.ns_t, G.Ts
        if G.kind == "s":
            for s in range(G.nseq):
                for r in range(2):
                    dma("sp", atail[:, :, s, r], state_conv[l, G.seqs[s], r].rearrange("(j p) -> p j", p=128), (), ["atail"], slow=True)
        else:
            memset("pool", atail, 0.0, ["atail"])
        for ti in range(G.nt):
            c0 = ti * W
            for j in range(NJ):
                wr, kw = sRing()
                dma("sp", wr, wup_bf[l, j], [("wup", l)], [kw])
                pa, ka = PS(0, 4)
                pb_, kpb = PS(0, 4)
                for kc in range(8):
                    mm(pa[:, 0:W], wr[:, kc, 0:128], G.xbf[:, kc, c0:c0 + W], kc == 0, kc == 7, [kw, kb(kc, ti)], [ka])
                for kc in range(8):
                    mm(pb_[:, 0:W], wr[:, kc, 128:256], G.xbf[:, kc, c0:c0 + W], kc == 0, kc == 7, [kw, kb(kc, ti)], [kpb])
                a_sb, kas = sF()
                a3 = a_sb[:, 0:ns * (Ts + 2)].rearrange("p (s t) -> p s t", s=ns)
                cp(a3[:, :, 0:2], atail[:, j, 0:ns, :], ["atail"], [kas], eng="pool")
                act(a3[:, :, 2:2 + Ts], pa[:, 0:W].rearrange("p (s t) -> p s t", s=ns), AF.Copy, [ka, kas], [kas])
                t0_, k0 = sF()
                t03 = t0_[:, 0:W].rearrange("p (s t) -> p s t", s=ns)
                act(t0_[:, 0:W], pa[:, 0:W], AF.Identity, [ka, "cw"], [k0], bias=cbias[:, l, j:j + 1], scale=cw[:, l, 2, j:j + 1])
                stt(t03, a3[:, :, 1:1 + Ts], cw[:, l, 1, j:j + 1], t03, ALU.mult, ALU.add, [kas, k0, "cw"], [k0])
                stt(t03, a3[:, :, 0:Ts], cw[:, l, 0, j:j + 1], t03, ALU.mult, ALU.add, [kas, k0, "cw"], [k0])
                cp(atail[:, j, 0:ns, :], a3[:, :, Ts:Ts + 2], [kas, "atail"], ["atail"], eng="pool")
                act(t0_[:, 0:W], t0_[:, 0:W], AF.Silu, [k0], [k0])
                tt(G.gT[:, j, 0:W], t0_[:, 0:W], pb_[:, 0:W], ALU.mult, [k0, kpb], [("A", "gT", j)])
            def ychunk(oc, l=l):
                py, kpy = PS(4, 6)
                qd, o2 = oc // 2, oc % 2
                if o2 == 0:
                    dma("sp", G.wdn[:, qd % 2], wdn_bf[l, qd], [("wdn", l)], [("A", "wdn", qd % 2)])
                for j in range(NJ):
                    mm(py[:, 0:W], G.wdn[:, qd % 2, j, o2 * 128:(o2 + 1) * 128], G.gT[:, j, 0:W], j == 0, j == NJ - 1,
                       [("A", "wdn", qd % 2), ("A", "gT", j)], [kpy])
                return py, kpy
            ln_tile(G, l, 1, ti, ychunk)
        for s in range(G.nseq):
            for r in range(2):
                dma("pool", G.cv_out[l, G.seqs[s], r].rearrange("(j p) -> p j", p=128), atail[:, :, s, r], ["atail"], [("cvout",)], slow=True)

    def mixer_pool(G, l):
        p = l // 2
        W = G.W
        for s in range(G.nseq):
            cend = (s + 1) * G.T
            tl = (cend - 1) // W
            for oc in range(8):
                dst = G.pl_out[p, G.seqs[s], :, oc * 128:(oc + 1) * 128].rearrange("r p -> p r")
                dma("pool", dst, G.xres[:, oc, cend - 15:cend], [kx(oc, tl)], [("plout",)], slow=True)
        dma("sp", G.poolw, poolw_bf[p], [("poolw", p)], [("A", "poolw")])
        for ti in range(G.nt):
            c0 = ti * W
            for s in range(G.ns_t):
                sc0 = c0 + s * G.Ts
                Ts = G.Ts
                first = (G.kind == "s") or (ti == 0)
                for oc in range(8):
                    g = oc // 2
                    w = POOL_W[g]
                    if first:
                        src, ksrc = sF()
                        if G.kind == "s":
                            dma("sp", src[:, 1:16], state_pool[p, G.seqs[s], :, oc * 128:(oc + 1) * 128].rearrange("r p -> p r"), (), [ksrc], slow=True)
                        else:
                            memset("pool", src[:, 0:16], 0.0, [ksrc])
                        cp(src[:, 16:16 + Ts], G.xres[:, oc, sc0:sc0 + Ts], [kx(oc, ti)], [ksrc], eng="pool")
                        cur = src[:, 16 - 15:16 + Ts]
                        kcur = [ksrc]
                    else:
                        cur = G.xres[:, oc, sc0 - 15:sc0 + Ts]
                        kcur = [kx(oc, ti), kx(oc, ti - 1)]
                    xt = cur[:, 15:15 + Ts]
                    lo = 15
                    step = 1
                    while step < w:
                        nxt, kn = sF()
                        n = lo - step + Ts
                        tt(nxt[:, 0:n], cur[:, step:step + n], cur[:, 0:n], ALU.add, kcur, [kn], eng="pool")
                        cur, kcur, lo = nxt[:, 0:n], [kn], lo - step
                        step *= 2
                    sw = cur[:, lo:lo + Ts]
                    outp = G.xbf[:, oc, sc0:sc0 + Ts]
                    stt(outp, sw, 1.0 / w, xt, ALU.mult, ALU.subtract, kcur + [kx(oc, ti)] + ([ksrc] if first else []), [kb(oc, ti)])
                    if G.kind == "p" and ti == 0:
                        fx, kf = sF()
                        tt(fx[:, 0:16], sw[:, 0:16], invc[:, g, :], ALU.mult, kcur + ["invc"], [kf])
                        tt(outp[:, 0:16], fx[:, 0:16], xt[:, 0:16], ALU.subtract, [kf, kx(oc, ti)] + ([ksrc] if first else []), [kb(oc, ti)])

        for ti in range(G.nt):
            c0 = ti * W

            def ychunk(oc):
                g = oc // 2
                py, kpy = PS(4, 6)
                for kc in range(2):
                    mm(py[:, 0:W], G.poolw[:, kc, g, (oc % 2) * 128:(oc % 2 + 1) * 128], G.xbf[:, 2 * g + kc, c0:c0 + W],
                       kc == 0, kc == 1, [("A", "poolw"), kb(2 * g + kc, ti)], [kpy])
                return py, kpy
            ln_tile(G, l, 0, ti, ychunk, yscale=lambda oc: pscv[:, p, oc:oc + 1])

    def mixer_ab(G, l):
        m = l // 2
        W, L = G.W, G.L
        nch = W // L
        mid = L // 2 - 1
        if G.kind == "s":
            for s in range(G.nseq):
                dma("sp", Sst[:, s, :, :], state_hgrn[m, G.seqs[s]].rearrange("h k v -> k h v"), (), [("S", s)])
        else:
            memset("pool", Sst[:, 0, :, :], 0.0, [("S", 0)])
        if G.kind == "s":
            for s in range(G.nseq):
                b = G.seqs[s]
                pass
        for ti in range(G.nt):
            c0 = ti * W
            aq_sb = {}; sig_sb = {}; sg_sb = {}; vtok = {}
            order = [0, 2, 6, 4, 1, 3, 7, 5, 8, 9, 10, 11, 12, 13]
            for g in order:
                wr, kw = sRing()
                dma("sp", wr, win_bf[m, g], [("win", m)], [kw])
                sec, hh0 = g // 2, (g % 2) * 2
                if sec in (0, 1, 3, 4, 5):
                    for hi in range(2):
                        h = hh0 + hi
                        pt, kp = PS()
                        for kc in range(8):
                            mm(pt[:, 0:W], wr[:, kc, hi * 128:(hi + 1) * 128], G.xbf[:, kc, c0:c0 + W], kc == 0, kc == 7, [kw, kb(kc, ti)], [kp])
                        if sec == 0:
                            t_, k_ = FX(hi); act(t_[:, 0:W], pt[:, 0:W], AF.Copy, [kp], [k_]); aq_sb[h] = (t_, k_)
                        elif sec == 1:
                            t_, k_ = FX(2 + hi); act(t_[:, 0:W], pt[:, 0:W], AF.Sigmoid, [kp], [k_]); sig_sb[h] = (t_, k_)
                        elif sec == 3:
                            t_, k_ = BX(hi); act(t_[:, 0:W], pt[:, 0:W], AF.Silu, [kp], [k_]); sg_sb[h] = (t_, k_)
                        elif sec == 4:
                            cp(G.qT[:, h, c0:c0 + W], pt[:, 0:W], [kp], [("A", "qT", h, ti)])
                        elif G.kind == "p":
                            act(G.kT[:, h, c0:c0 + W], pt[:, 0:W], AF.Copy, [kp], [("A", "kT", h, ti)])
                        else:
                            cp(G.knew[:, h, 0:W], pt[:, 0:W], [kp], [("A", "knew", h)])
                if sec in (2, 5, 6):
                    for bi, (co, nb, s, t0) in enumerate(G.blocks[ti]):
                        pt, kp = PS()
                        for kc in range(8):
                            mm(pt[0:nb, 0:256], G.xbf[:, kc, c0 + co:c0 + co + nb], wr[:, kc, :], kc == 0, kc == 7, [kw, kb(kc, ti)], [kp])
                        if sec == 2:
                            t_, k_ = BX(2 + bi // 2)
                            tv = t_[:, (bi % 2) * 256:(bi % 2 + 1) * 256]
                            vtok[(g % 2, bi)] = (tv, k_)
                            act(tv[0:nb, :], pt[0:nb, 0:256], AF.Copy, [kp], [k_])
                        else:
                            st, ks_ = sStage()
                            act(st[0:nb, 0:256], pt[0:nb, 0:256], AF.Copy, [kp], [ks_])
                            outt = G.k_out if sec == 5 else G.v_out
                            dma("pool", outt[m, G.seqs[s], t0:t0 + nb, (g % 2) * 256:(g % 2 + 1) * 256], st[0:nb, 0:256], [ks_], [("kvout",)])
                            if sec == 6:
                                if G.kind == "p":
                                    vt = t0 // 128
                                    cp(G.V[0:nb, vt, (g % 2) * 256:(g % 2 + 1) * 256], st[0:nb, 0:256], [ks_], [("A", "V", vt)])
                                else:
                                    cp(G.vnew[0:nb, s, (g % 2) * 256:(g % 2 + 1) * 256], st[0:nb, 0:256], [ks_], [("A", "vnew", s)])
                if g in (4, 5):
                    for hi in range(2):
                        h = (g % 2) * 2 + hi
                        hgrn_tile_head(G, m, ti, h, aq_sb[h], sig_sb[h], sg_sb[h], vtok, hi, g % 2)
            for h in range(4):
                cp(G.xbf[:, h, c0:c0 + W], hout[:, h, 0:W], [("hout", h)], [kb(h, ti)], eng="pool")
        for s in range(G.nseq):
            dma("pool", G.h_out[m, G.seqs[s]].rearrange("h k v -> k h v"), Sst[:, s, :, :], [("S", s)], [("hout",)])
        attention(G, m)
        for ti in range(G.nt):
            c0 = ti * W

            wst = {}

            def ychunk(oc, c0=c0, ti=ti, wst=wst):
                py, kpy = PS(4, 6)
                if oc % 2 == 0:
                    wst["w"] = sRing()
                    dma("sp", wst["w"][0], wout_bf[m, oc // 2], [("wout", m)], [wst["w"][1]])
                wr, kw = wst["w"]
                for kc in range(8):
                    mm(py[:, 0:W], wr[:, kc, (oc % 2) * 128:(oc % 2 + 1) * 128], G.xbf[:, kc, c0:c0 + W], kc == 0, kc == 7,
                       [kw, kb(kc, ti)], [kpy])
                return py, kpy
            ln_tile(G, l, 0, ti, ychunk)

    def hgrn_tile_head(G, m, ti, h, aq, sig, sg, vtok, hi, gp):
        W, L = G.W, G.L
        nch = W // L
        mid = L // 2 - 1
        c0 = ti * W
        aq_t, kaq = aq
        sig_t, ksig = sig
        sg_t, ksg = sg
        lf, klf = FX(4)
        act(lf[:, 0:W], sig_t[:, 0:W], AF.Ln, [ksig, "lb", "oml"], [klf], bias=lbv[:, m, h:h + 1], scale=omlv[:, m, h:h + 1])
        kin = sig_t
        ts(kin[:, 0:W], sig_t[:, 0:W], nomlv[:, m, h:h + 1], omlv[:, m, h:h + 1], ALU.mult, ALU.add, [ksig, "oml", "noml"], [ksig])
        bt, kbt = FX(5)
        P.add("dve", lambda e: e.tensor_tensor_scan(out=bt[:, 0:W], data0=G.reset[:, 0:W], data1=lf[:, 0:W], initial=0.0,
                                                    op0=ALU.mult, op1=ALU.add), [klf, "resetp", "resets"], [kbt])
        b3 = bt[:, 0:W].rearrange("p (c t) -> p c t", c=nch)
        dd = lf
        d3 = dd[:, 0:W].rearrange("p (c t) -> p c t", c=nch)
        tt(d3, b3, b3[:, :, mid:mid + 1].to_broadcast([128, nch, L]), ALU.subtract, [kbt, klf], [klf])
        sidx = hgrn_tile_head.cnt % 4; hgrn_tile_head.cnt += 1
        em = smallv[:, 3 * sidx + 0, 0:nch]; el = smallv[:, 3 * sidx + 1, 0:nch]; eb = smallv[:, 3 * sidx + 2, 0:nch]
        ksm = ("small", sidx)
        act(em, b3[:, :, mid], AF.Exp, [kbt], [ksm])
        act(el, d3[:, :, L - 1], AF.Exp, [klf, ksm], [ksm])
        act(eb, b3[:, :, L - 1], AF.Exp, [kbt, ksm], [ksm])
        e1 = bt
        act(e1[:, 0:W], dd[:, 0:W], AF.Exp, [klf, kbt, ksm], [kbt])
        qt, kqt = BX(4)
        tt(qt[:, 0:W], aq_t[:, 0:W], e1[:, 0:W], ALU.mult, [kaq, kbt], [kqt])
        act(e1[:, 0:W], dd[:, 0:W], AF.Exp, [klf, kbt, kqt], [kbt], scale=-1.0)
        kt, kkt = BX(5)
        tt(kt[:, 0:W], kin[:, 0:W], e1[:, 0:W], ALU.mult, [ksig, kbt], [kkt])
        oacc = aq_t
        koacc = kaq
        for bi, (co, nb, s, t0) in enumerate(G.blocks[ti]):
            vt_t, kvt = vtok[(gp, bi)]
            vv = vt_t[0:nb, hi * 128:(hi + 1) * 128]
            ptk, kptk = PS()
            ptkb = ptk[:, 0:256].bitcast(BF16)
            tr(ptkb[0:nb, 0:128], kt[:, co:co + nb], identb, [kkt, "identb"], [kptk])
            b7, kktok = BX(7)
            ktok = b7[:, (bi % 2) * 256:(bi % 2) * 256 + 128]
            cp(ktok[0:nb, 0:128], ptkb[0:nb, 0:128], [kptk], [kktok])
            psc, kpsc = PS()
            mm(psc[0:nb, 0:nb], kt[:, co:co + nb], qt[:, co:co + nb], True, True, [kkt, kqt], [kpsc])
            scm, kscm = b7[:, (bi % 2) * 256 + 128:(bi % 2) * 256 + 256], kktok
            tt(scm[0:nb, 0:nb], psc[0:nb, 0:nb], G.maskH[0:nb, 0:nb], ALU.mult, [kpsc, "maskp", "masks"], [kscm])
            po, kpo = PS()
            mm(po[:, 0:nb], vv, scm[0:nb, 0:nb], True, True, [kvt, kscm], [kpo])
            act(oacc[:, co:co + nb], po[:, 0:nb], AF.Copy, [kpo, kqt], [koacc])
            for ci in range(nb // L):
                c = (co + ci * L) // L
                S = Sst[:, s, h, :]
                kS = ("S", s)
                act(Sbf, S, AF.Copy, [kS, ksm], ["Sbf"], scale=em[:, c:c + 1])
                px, kpx = PS()
                mm(px[:, 0:L], Sbf, qt[:, co + ci * L:co + (ci + 1) * L], True, True, ["Sbf", kqt], [kpx])
                tt(oacc[:, co + ci * L:co + (ci + 1) * L], oacc[:, co + ci * L:co + (ci + 1) * L], px[:, 0:L], ALU.add, [koacc, kpx], [koacc])
                pu, kpu = PS()
                mm(pu[:, 0:128], ktok[ci * L:(ci + 1) * L, 0:128], vv[ci * L:(ci + 1) * L, :], True, True, [kktok, kvt], [kpu])
                tmp, ktmp = FX(6)
                ts(tmp[:, 0:128], pu[:, 0:128], el[:, c:c + 1], None, ALU.mult, None, [kpu, ksm], [ktmp])
                stt(S, S, eb[:, c:c + 1], tmp[:, 0:128], ALU.mult, ALU.add, [kS, ksm, ktmp], [kS])
        sq, ksq = BX(6)
        act(sq[:, 0:W], oacc[:, 0:W], AF.Square, [koacc], [ksq])
        pss, kpss = PS()
        mm(pss[:, 0:W], onesb, sq[:, 0:W], True, True, [ksq, "onesb"], [kpss])
        rs, krs = FX(6)
        rsqrt(rs[:, 0:W], pss[:, 0:W], 1, [kpss], [krs])
        stt(rs[:, 0:W], oacc[:, 0:W], hgv[:, m, h:h + 1], rs[:, 0:W], ALU.mult, ALU.mult, [koacc, krs, "hg"], [krs])
        tt(hout[:, h, 0:W], rs[:, 0:W], sg_t[:, 0:W], ALU.mult, [krs, ksg], [("hout", h)])
    hgrn_tile_head.cnt = 0

    def attention(G, m):
        T, Pn = G.T, G.P
        QW = min(512, T)
        nq = T // QW
        for s in range(G.nseq):
            b = G.seqs[s]
            scol = s * T
            if G.kind == "s":
                for kt_ in range(G.nkh):
                    st, ks_ = sStage()
                    dma("sp", st[:, 0:512], cache_k[m, b, kt_ * 128:(kt_ + 1) * 128, :], (), [ks_])
                    pt, kp = PS(4, 8)
                    for h in range(4):
                        tr(pt[:, h * 128:(h + 1) * 128], st[:, h * 128:(h + 1) * 128], ident, [ks_, "ident"], [kp])
                    act(G.kT[:, :, kt_ * 128:(kt_ + 1) * 128], pt[:, :].rearrange("p (h k) -> p h k", h=4), AF.Copy, [kp], [("A", "kTh", kt_)])
                    st2, ks2 = sStage()
                    dma("sp", st2[:, 0:512], cache_v[m, b, kt_ * 128:(kt_ + 1) * 128, :], (), [ks2])
                    cp(G.V[:, kt_, :], st2[:, 0:512], [ks2], [("A", "Vh", kt_)], eng="pool")
                for h in range(4):
                    cp(G.kT[:, h, Pn:Pn + T], G.knew[:, h, scol:scol + T], [("A", "knew", h)], [("A", "kTn", h)], eng="pool")
                cp(G.V[0:T, G.nkh, :], G.vnew[0:T, s, :], [("A", "vnew", s)], [("A", "Vn")], eng="pool")
            ktiles = []
            for kt_ in range(G.nkh):
                ktiles.append((kt_ * 128, 128, kt_, True))
            if G.kind == "p":
                for kt_ in range(T // 128):
                    ktiles.append((kt_ * 128, 128, kt_, False))
            else:
                ktiles.append((Pn, T, G.nkh, False))
            for h in range(4):
                for qi in range(nq):
                    q0 = qi * QW
                    ti = (scol + q0) // G.W
                    accs = [(psb[i], ("ps", i)) for i in range(4)]
                    vis = []
                    for (k0, nk, vt, hist) in ktiles:
                        if hist:
                            vis.append((k0, nk, vt, hist, 0))
                        else:
                            krel = k0 - Pn
                            if krel >= q0 + QW:
                                continue
                            vis.append((k0, nk, vt, hist, max(0, krel - q0)))
                    for idx, (k0, nk, vt, hist, qs) in enumerate(vis):
                        first, last = idx == 0, idx == len(vis) - 1
                        nqc = QW - qs
                        qsl = slice(scol + q0 + qs, scol + q0 + QW)
                        if G.kind == "p":
                            kkeys = [("A", "kT", h, (k0) // G.W)]
                            vkeys = [("A", "V", vt)]
                        else:
                            kkeys = [("A", "kTh", vt)] if hist else [("A", "kTn", h)]
                            vkeys = [("A", "Vh", vt)] if hist else [("A", "Vn")]
                        pp = []
                        for half in range(2):
                            psc, kpsc = PS(4, 8)
                            pr = slice(half * 64, (half + 1) * 64)
                            mm(psc[0:nk, 0:nqc], G.kT[pr, h, k0:k0 + nk], G.qT[pr, h, qsl], True, True,
                               kkeys + [("A", "qT", h, ti)], [kpsc])
                            pt_, kpt = sB()
                            act(pt_[0:nk, 0:nqc], psc[0:nk, 0:nqc], AF.Exp, [kpsc], [kpt], scale=0.125)
                            if G.kind == "p":
                                if not hist:
                                    krel = k0 - Pn
                                    for qb in range(qs, QW, 128):
                                        d = krel - (q0 + qb)
                                        if d in (0, -128):
                                            di = 0 if d == 0 else 1
                                            sl = slice(qb - qs, qb - qs + 128)
                                            tt(pt_[0:nk, sl], pt_[0:nk, sl], Ep[0:nk, di, h, :], ALU.mult, [kpt, "Ep"], [kpt])
                            else:
                                if hist and k0 == Pn - 128:
                                    tt(pt_[0:nk, 0:nqc], pt_[0:nk, 0:nqc], Esh[0:nk, h, :], ALU.mult, [kpt, "Es"], [kpt])
                                if not hist:
                                    tt(pt_[0:nk, 0:nqc], pt_[0:nk, 0:nqc], Esn[0:nk, h, :], ALU.mult, [kpt, "Es"], [kpt])
                            pp.append((pt_, kpt))
                        for half in range(2):
                            pt_, kpt = pp[half]
                            (po, kpo), (psm, kpsm) = accs[2 * half], accs[2 * half + 1]
                            mm(po[:, qs:QW], G.V[0:nk, vt, h * 128:(h + 1) * 128], pt_[0:nk, 0:nqc], first, last, vkeys + [kpt], [kpo])
                            mm(psm[:, qs:QW], onesb[0:nk, :], pt_[0:nk, 0:nqc], first, last, ["onesb", kpt], [kpsm])
                    t1, k1 = FX(0); t2, k2 = FX(1)
                    P.add("dve", lambda e, o=t1, i=accs[1][0]: e.reciprocal(out=o[:, 0:QW], in_=i[:, 0:QW]), [accs[1][1]], [k1])
                    tt(t1[:, 0:QW], accs[0][0][:, 0:QW], t1[:, 0:QW], ALU.mult, [accs[0][1], k1], [k1])
                    P.add("dve", lambda e, o=t2, i=accs[3][0]: e.reciprocal(out=o[:, 0:QW], in_=i[:, 0:QW]), [accs[3][1]], [k2])
                    tt(t2[:, 0:QW], accs[2][0][:, 0:QW], t2[:, 0:QW], ALU.mult, [accs[2][1], k2], [k2])
                    stt(t1[:, 0:QW], t2[:, 0:QW], neglam[:, m:m + 1], t1[:, 0:QW], ALU.mult, ALU.add, [k1, k2, "nl"], [k1])
                    sq, ksq = sB()
                    act(sq[:, 0:QW], t1[:, 0:QW], AF.Square, [k1], [ksq])
                    pss, kpss = PS(4, 8)
                    mm(pss[:, 0:QW], onesb, sq[:, 0:QW], True, True, [ksq, "onesb"], [kpss])
                    rsqrt(t2[:, 0:QW], pss[:, 0:QW], 1, [kpss, k2], [k2])
                    P.add("dve", lambda e, o=G.xbf[:, 4 + h, scol + q0:scol + q0 + QW], a=t1, r=t2, g_=dgv[:, m:m + 1]:
                          e.scalar_tensor_tensor(out=o, in0=a[:, 0:QW], scalar=g_, in1=r[:, 0:QW], op0=ALU.mult, op1=ALU.mult),
                          [k1, k2, "dg"], [kb(4 + h, ti)])

    import os
    LEVEL = int(os.environ.get("KLEVEL", "99"))
    KL = int(os.environ.get("KLAYERS", "4"))
    KG = os.environ.get("KGROUPS", "ps")
    setup()
    cast_order = [("win", 0), ("wout", 0), ("wup", 0), ("wdn", 0), ("poolw", 0), ("wup", 1), ("wdn", 1),
                  ("win", 1), ("wout", 1), ("wup", 2), ("wdn", 2), ("poolw", 1), ("wup", 3), ("wdn", 3)]
    castf = {"win": cast_win, "wout": cast_wout, "wup": cast_wup, "wdn": cast_wdn, "poolw": cast_poolw}
    if LEVEL >= 1:
        for nm, i in cast_order[:5]:
            castf[nm](i)
    cast_later = {0: cast_order[5:7], 1: cast_order[7:12], 2: cast_order[12:]}

    groups = [("p", [i]) for i in range(NPS)]
    if NSS > 0:
        groups.append(("s", list(range(NSS))))
    groups = [g_ for g_ in groups if g_[0] in KG]
    for gi, (kind, ids) in enumerate(groups):
        if LEVEL < 2:
            break
        barrier()
        G = make_group(kind, ids)
        if kind == "s":
            G.o_knew = carve(4 * G.NT * 2); G.knew = view(G.o_knew, BF16, (4, G.NT))
            G.o_vnew = carve(G.nseq * 512 * 2); G.vnew = view(G.o_vnew, BF16, (G.nseq, 512))
        load_x(G)
        for l in range(DEPTH if LEVEL >= 3 else 0):
            if l >= KL:
                break
            if gi == 0:
                for nm, i in cast_later.get(l, []):
                    castf[nm](i)
            if G.alias:
                arena_barrier()
            if l % 2 == 0:
                mixer_ab(G, l)
            else:
                mixer_pool(G, l)
            if G.alias:
                arena_barrier()
            ffn(G, l)
        store_y(G)
    barrier()

    P.finalize(sems)
    with nc.Block() as block:
        @block.sync
        def _(e):
            P.run_stream("sp", e)

        @block.scalar
        def _(e):
            P.run_stream("act", e)

        @block.vector
        def _(e):
            P.run_stream("dve", e)

        @block.gpsimd
        def _(e):
            P.run_stream("pool", e)

        @block.tensor
        def _(e):
            P.run_stream("pe", e)
    es.close()
    return nc, len(P.ops)


_CACHE = {}


def kernel(**inp):
    import os
    NCORES = int(os.environ.get("KCORES", "8"))
    x_prompt = np.asarray(inp["x_prompt"], np.float32)
    x_sample = np.asarray(inp["x_sample"], np.float32)
    B, SEQ, _ = x_prompt.shape
    DB, DS, _ = x_sample.shape
    PAST = inp["cache_k"].shape[2]
    NPS, NSS = B // NCORES, DB // NCORES
    key = (NPS, SEQ, NSS, DS, PAST)
    if key not in _CACHE:
        _CACHE[key] = build(*key)
    nc, nops = _CACHE[key]
    consts = host_consts(NSS, DS)
    ck = np.asarray(inp["cache_k"], np.float32).reshape(2, DB, PAST, 512)
    cv = np.asarray(inp["cache_v"], np.float32).reshape(2, DB, PAST, 512)
    sh = np.asarray(inp["state_hgrn"], np.float32)
    sp_ = np.asarray(inp["state_pool"], np.float32)
    sc = np.asarray(inp["state_ffn_conv"], np.float32)
    shared = {k: np.ascontiguousarray(np.asarray(inp[k], np.float32)) for k in
              ("w_in_ab", "w_out_ab", "lb_logits", "hgrn_norm_g", "diff_norm_g", "rel_bias", "pool_w", "pool_scale",
               "ffn_w_up", "ffn_conv_w", "ffn_conv_b", "ffn_w_down", "ln1_g", "ln1_b", "ln2_g", "ln2_b")}
    shared["diff_lambda"] = np.ascontiguousarray(np.asarray(inp["diff_lambda"], np.float32).reshape(2, 256))
    shared.update(consts)
    in_maps = []
    for c in range(NCORES):
        ps, ss = slice(c * NPS, (c + 1) * NPS), slice(c * NSS, (c + 1) * NSS)
        d = dict(shared)
        d["x_p"] = np.ascontiguousarray(x_prompt[ps])
        d["x_s"] = np.ascontiguousarray(x_sample[ss])
        d["cache_k"] = np.ascontiguousarray(ck[:, ss])
        d["cache_v"] = np.ascontiguousarray(cv[:, ss])
        d["state_hgrn"] = np.ascontiguousarray(sh[:, ss])
        d["state_pool"] = np.ascontiguousarray(sp_[:, ss])
        d["state_conv"] = np.ascontiguousarray(sc[:, ss])
        in_maps.append(d)
    res = run_bass_kernel_spmd(nc, in_maps, core_ids=list(range(NCORES)))
    R = res.results

    def cat(name, axis):
        return np.concatenate([np.asarray(r[name], np.float32) for r in R], axis=axis)
    y_p = cat("y_p", 0)
    y_s = cat("y_s", 0)
    k_p = cat("k_p", 1).reshape(2, B, SEQ, 4, 128)
    v_p = cat("v_p", 1).reshape(2, B, SEQ, 4, 128)
    h_p = cat("h_p", 1)
    pl_p = cat("pl_p", 1)
    cv_p = cat("cv_p", 1)
    k_s = cat("k_s", 1).reshape(2, DB, DS, 4, 128)
    v_s = cat("v_s", 1).reshape(2, DB, DS, 4, 128)
    h_s = cat("h_s", 1)
    pl_s = cat("pl_s", 1)
    cv_s = cat("cv_s", 1)
    return (y_p, y_s, k_p, v_p, h_p, pl_p, cv_p, k_s, v_s, h_s, pl_s, cv_s)
```

```python
import math
from contextlib import ExitStack
import numpy as np
import concourse.bass as bass
import concourse.mybir as mybir
from concourse.bass_utils import run_bass_kernel_spmd

F32, BF16 = mybir.dt.float32, mybir.dt.bfloat16
AF = mybir.ActivationFunctionType
ALU = mybir.AluOpType

D = 1024
NCH = 8
DFF = 2816
NJ = 22
INC = 3584
DEPTH = 4
ALPHA = (2 * DEPTH) ** 0.25
INV_ALPHA = 1.0 / ALPHA
LN_EPS_S = 1e-5 / (ALPHA * ALPHA)
RMS_EPS = 1e-6
NREL = 383
POOL_W = (2, 4, 8, 16)


class Op:
    __slots__ = ("eng", "fn", "deps", "dma", "needed", "tok")


class Prog:
    NS = {"sp": 24, "pool": 12}

    def __init__(self):
        self.ops = []
        self.lw = {}
        self.rd = {}

    def add(self, eng, fn, reads=(), writes=(), dma=False):
        op = Op()
        op.eng, op.fn, op.dma, op.needed, op.tok = eng, fn, dma, False, None
        deps = {}
        reads = list(reads) + ["EPOCH"]
        writes = list(writes)
        if eng != "pe":
            writes += [k for k in reads if isinstance(k, tuple) and k[0] == "ps" and k not in writes]
        if any(isinstance(k, tuple) and k[0] == "A" for k in reads + writes):
            reads.append("ARENA")
        for k in reads:
            w = self.lw.get(k)
            if w is not None:
                deps[id(w)] = (w, "raw")
        for k in writes:
            w = self.lw.get(k)
            if w is not None and id(w) not in deps:
                deps[id(w)] = (w, "waw")
            for r in self.rd.get(k, ()):
                if id(r) not in deps:
                    deps[id(r)] = (r, "war")
        out = []
        for d, kind in deps.values():
            if d is op:
                continue
            if (not d.dma) and (not dma) and d.eng == eng:
                if eng == "pe" or kind != "raw":
                    continue
            out.append(d)
        op.deps = out
        for k in reads:
            self.rd.setdefault(k, []).append(op)
        for k in writes:
            self.lw[k] = op
            self.rd[k] = []
        self.ops.append(op)
        return op

    def finalize(self, sems):
        cnt = {}
        dcount = {"sp": 0, "pool": 0}
        dops = {"sp": [], "pool": []}
        for op in self.ops:
            for d in op.deps:
                d.needed = True
        for op in self.ops:
            if op.dma:
                q = op.eng
                i = dcount[q]
                ns = self.NS[q]
                op.tok = (sems["dma_" + q][i % ns], 16 * (i // ns + 1))
                if i >= ns:
                    op.deps.append(dops[q][i - ns])
                dops[q].append(op)
                dcount[q] = i + 1
            elif op.needed:
                cnt[op.eng] = cnt.get(op.eng, 0) + 1
                op.tok = (sems[op.eng], cnt[op.eng])

    def run_stream(self, name, e):
        waited = {}
        for op in self.ops:
            if op.eng != name:
                continue
            req = {}
            for d in op.deps:
                s, v = d.tok
                k = id(s)
                if k not in req or req[k][1] < v:
                    req[k] = (s, v)
            for k, (s, v) in req.items():
                if waited.get(k, 0) < v:
                    e.wait_ge(s, v)
                    waited[k] = v
            ins = op.fn(e)
            if op.dma:
                ins.then_inc(op.tok[0], 16)
            elif op.needed:
                ins.then_inc(op.tok[0], 1)


def rel_bucket_np(rel):
    half, max_exact = 16, 8
    base = np.where(rel > 0, half, 0)
    n = np.abs(rel)
    nf = np.maximum(n, 1).astype(np.float32)
    large = max_exact + (np.log(nf / np.float32(max_exact)) / np.float32(math.log(128 / max_exact))
                         * (half - max_exact)).astype(np.int32)
    large = np.minimum(large, half - 1)
    return base + np.where(n < max_exact, n, large)


def host_consts(nss, ds):
    c = {}
    c["c_ident"] = np.eye(128, dtype=np.float32)
    s = np.arange(128)[:, None]
    t = np.arange(128)[None, :]
    c["c_maskp"] = ((s // 64 == t // 64) & (s <= t)).astype(np.float32)
    s = np.arange(ds)[:, None]
    t = np.arange(ds)[None, :]
    c["c_masks"] = (s <= t).astype(np.float32)
    c["c_resetp"] = (np.arange(512) % 64 != 0).astype(np.float32)
    c["c_resets"] = (np.arange(nss * ds) % ds != 0).astype(np.float32)
    k = np.arange(128)[:, None]
    q = np.arange(128)[None, :]
    c["c_vis0"] = (k // 64 <= q // 64).astype(np.float32)
    rel = 127 - np.arange(NREL)
    b = rel_bucket_np(rel.astype(np.int32))
    oh = np.zeros((32, NREL), np.float32)
    oh[b, np.arange(NREL)] = 1.0
    c["c_oht"] = oh
    invc = np.zeros((4, 16), np.float32)
    for g, w in enumerate(POOL_W):
        invc[g] = 1.0 / np.minimum(w, np.arange(16) + 1)
    c["c_invc"] = invc
    return c


def build(NPS, SEQ, NSS, DS, PAST):
    nc = bass.Bass("TRN2", target_bir_lowering=False)
    P = Prog()

    def din(name, shape):
        return nc.dram_tensor(name, list(shape), F32, kind="ExternalInput")

    def dout(name, shape):
        return nc.dram_tensor(name, list(shape), F32, kind="ExternalOutput")

    def dint(name, shape, dt):
        return nc.dram_tensor(name, list(shape), dt, kind="Internal")

    NTS = NSS * DS
    x_p = din("x_p", [NPS, SEQ, D]).ap()
    x_s = din("x_s", [NSS, DS, D]).ap()
    cache_k = din("cache_k", [2, NSS, PAST, 512]).ap()
    cache_v = din("cache_v", [2, NSS, PAST, 512]).ap()
    state_hgrn = din("state_hgrn", [2, NSS, 4, 128, 128]).ap()
    state_pool = din("state_pool", [2, NSS, 15, D]).ap()
    state_conv = din("state_conv", [4, NSS, 2, DFF]).ap()
    w_in = din("w_in_ab", [2, D, INC]).ap()
    w_out = din("w_out_ab", [2, D, D]).ap()
    lb_logits = din("lb_logits", [2, 512]).ap()
    hgrn_norm_g = din("hgrn_norm_g", [2, 512]).ap()
    diff_lambda = din("diff_lambda", [2, 256]).ap()
    diff_norm_g = din("diff_norm_g", [2, 128]).ap()
    rel_bias = din("rel_bias", [32, 4]).ap()
    pool_w = din("pool_w", [2, 4, 256, 256]).ap()
    pool_scale = din("pool_scale", [2, D]).ap()
    w_up = din("ffn_w_up", [4, D, 2 * DFF]).ap()
    conv_w = din("ffn_conv_w", [4, 3, DFF]).ap()
    conv_b = din("ffn_conv_b", [4, DFF]).ap()
    w_down = din("ffn_w_down", [4, DFF, D]).ap()
    ln1_g = din("ln1_g", [4, D]).ap()
    ln1_b = din("ln1_b", [4, D]).ap()
    ln2_g = din("ln2_g", [4, D]).ap()
    ln2_b = din("ln2_b", [4, D]).ap()
    c_ident = din("c_ident", [128, 128]).ap()
    c_maskp = din("c_maskp", [128, 128]).ap()
    c_masks = din("c_masks", [DS, DS]).ap()
    c_resetp = din("c_resetp", [512]).ap()
    c_resets = din("c_resets", [NTS]).ap()
    c_vis0 = din("c_vis0", [128, 128]).ap()
    c_oht = din("c_oht", [32, NREL]).ap()
    c_invc = din("c_invc", [4, 16]).ap()
    y_p = dout("y_p", [NPS, SEQ, D]).ap()
    y_s = dout("y_s", [NSS, DS, D]).ap()
    k_p = dout("k_p", [2, NPS, SEQ, 512]).ap()
    v_p = dout("v_p", [2, NPS, SEQ, 512]).ap()
    h_p = dout("h_p", [2, NPS, 4, 128, 128]).ap()
    pl_p = dout("pl_p", [2, NPS, 15, D]).ap()
    cv_p = dout("cv_p", [4, NPS, 2, DFF]).ap()
    k_s = dout("k_s", [2, NSS, DS, 512]).ap()
    v_s = dout("v_s", [2, NSS, DS, 512]).ap()
    h_s = dout("h_s", [2, NSS, 4, 128, 128]).ap()
    pl_s = dout("pl_s", [2, NSS, 15, D]).ap()
    cv_s = dout("cv_s", [4, NSS, 2, DFF]).ap()
    win_bf = dint("win_bf", [2, 14, 128, 8, 256], BF16).ap()
    wout_bf = dint("wout_bf", [2, 4, 128, 8, 256], BF16).ap()
    poolw_bf = dint("poolw_bf", [2, 128, 2, 4, 256], BF16).ap()
    wup_bf = dint("wup_bf", [4, NJ, 128, 8, 256], BF16).ap()
    wdn_bf = dint("wdn_bf", [4, 4, 128, NJ, 256], BF16).ap()
    evm_t = dint("evm", [4, 128, NREL], F32)
    evm = evm_t.ap()

    es = ExitStack()
    ARENA_BYTES = 207 * 1024
    arena = es.enter_context(nc.sbuf_tensor("arena", [128, ARENA_BYTES // 4], F32))
    psb = [es.enter_context(nc.psum_tensor(f"ps{i}", [128, 512], F32)) for i in range(8)]
    sems = {}
    for n in ("pe", "act", "dve", "pool", "sp"):
        sems[n] = es.enter_context(nc.semaphore("sem_" + n))
    for q in ("sp", "pool"):
        sems["dma_" + q] = [es.enter_context(nc.semaphore(f"dsem_{q}{i}")) for i in range(Prog.NS[q])]

    top = [0]

    def carve(nbytes):
        o = top[0]
        top[0] = (o + nbytes + 63) // 64 * 64
        assert top[0] <= ARENA_BYTES, ("arena overflow", top[0])
        return o

    def view(off, dt, shape, parts=128):
        n = int(np.prod(shape))
        esz = 4 if dt == F32 else 2
        nb = n * esz
        assert off % 4 == 0
        a = arena[0:parts, off // 4: off // 4 + (nb + 3) // 4]
        if dt == BF16:
            a = a.bitcast(BF16)
            if a.shape[1] != n:
                a = a[:, 0:n]
        if len(shape) == 2:
            a = a.rearrange("p (a b) -> p a b", a=shape[0])
        elif len(shape) == 3:
            a = a.rearrange("p (a b c) -> p a b c", a=shape[0], b=shape[1])
        return a

    o_ident = carve(512); ident = view(o_ident, F32, (128,))
    o_identb = carve(256); identb = view(o_identb, BF16, (128,))
    o_onesb = carve(256); onesb = view(o_onesb, BF16, (128,))
    o_maskp = carve(256); maskp = view(o_maskp, BF16, (128,))
    o_masks = carve(64); masks = view(o_masks, BF16, (DS,))
    o_resetp = carve(1024); resetp = view(o_resetp, BF16, (512,))
    o_resets = carve(2 * NTS + 4); resets = view(o_resets, BF16, (NTS,))
    o_Ep = carve(2 * 4 * 256); Ep = view(o_Ep, BF16, (2, 4, 128))
    o_Esh = carve(4 * DS * 2); Esh = view(o_Esh, BF16, (4, DS))
    o_Esn = carve(4 * DS * 2); Esn = view(o_Esn, BF16, (4, DS))
    o_lng = carve(4 * 2 * 8 * 4); lng = view(o_lng, F32, (4, 2, 8))
    o_lnb = carve(4 * 2 * 8 * 4); lnb = view(o_lnb, F32, (4, 2, 8))
    o_cw = carve(4 * 3 * NJ * 4); cw = view(o_cw, F32, (4, 3, NJ))
    o_cb = carve(4 * NJ * 4); cbias = view(o_cb, F32, (4, NJ))
    o_lb = carve(2 * 4 * 4); lbv = view(o_lb, F32, (2, 4))
    o_oml = carve(2 * 4 * 4); omlv = view(o_oml, F32, (2, 4))
    o_noml = carve(2 * 4 * 4); nomlv = view(o_noml, F32, (2, 4))
    o_hg = carve(2 * 4 * 4); hgv = view(o_hg, F32, (2, 4))
    o_dg = carve(2 * 4); dgv = view(o_dg, F32, (2,))
    o_nl = carve(2 * 4); neglam = view(o_nl, F32, (2,))
    o_psc = carve(2 * 8 * 4); pscv = view(o_psc, F32, (2, 8))
    o_invc = carve(4 * 16 * 4); invc = view(o_invc, F32, (4, 16))
    o_epsc = carve(8); epsc = view(o_epsc, F32, (2,))
    o_tmpc = carve(2304); tmpc = view(o_tmpc, F32, (576,))
    NSM = max(1, NSS)
    Sst_box = [None]
    o_Sb = carve(256); Sbf = view(o_Sb, BF16, (128,))
    o_atail = carve(NJ * NSM * 2 * 4); atail = view(o_atail, F32, (NJ, NSM, 2))
    o_small = carve(12 * 8 * 4); smallv = view(o_small, F32, (12, 8))
    o_hout = carve(4 * 512 * 2); hout = view(o_hout, BF16, (4, 512))
    NF, NB = 7, 8
    FW = 528
    o_sf = [carve(FW * 4) for _ in range(NF)]
    o_sb = [carve(512 * 2) for _ in range(NB)]
    o_stg = [carve(4096) for _ in range(2)]
    NR = 3
    o_ring = [carve(4096) for _ in range(NR)]
    base_top = top[0]
    print("base_top", base_top)

    sf_i, sb_i, stg_i, ring_i = [0], [0], [0], [0]

    def sF():
        i = sf_i[0] % NF; sf_i[0] += 1
        return view(o_sf[i], F32, (FW,)), ("sf", i)

    def sB():
        i = sb_i[0] % NB; sb_i[0] += 1
        return view(o_sb[i], BF16, (512,)), ("sb", i)

    def FX(i):
        return view(o_sf[i], F32, (FW,)), ("sf", i)

    def BX(i):
        return view(o_sb[i], BF16, (512,)), ("sb", i)

    def sFr(lo, hi):
        n = hi - lo
        i = lo + sf_i[0] % n; sf_i[0] += 1
        return view(o_sf[i], F32, (FW,)), ("sf", i)

    def sStage():
        i = stg_i[0] % 2; stg_i[0] += 1
        return view(o_stg[i], F32, (1024,)), ("stg", i)

    def sRing():
        i = ring_i[0] % NR; ring_i[0] += 1
        return view(o_ring[i], BF16, (8, 256)), ("ring", i)

    ps_i = [0]

    def PS(lo=0, hi=8):
        n = hi - lo
        i = lo + ps_i[0] % n; ps_i[0] += 1
        return psb[i], ("ps", i)

    def act(out, in_, func, reads, writes, bias=None, scale=None, accum=None):
        kw = {}
        if bias is not None: kw["bias"] = bias
        if scale is not None: kw["scale"] = scale
        if accum is not None: kw["accum_out"] = accum
        P.add("act", lambda e: e.activation(out=out, in_=in_, func=func, **kw), reads, writes)

    def tt(out, in0, in1, op, reads, writes, eng="dve"):
        P.add(eng, lambda e: e.tensor_tensor(out=out, in0=in0, in1=in1, op=op), reads, writes)

    def ts(out, in0, s1, s2, op0, op1, reads, writes, eng="dve"):
        if op1 is None:
            P.add(eng, lambda e: e.tensor_scalar(out=out, in0=in0, scalar1=s1, scalar2=None, op0=op0), reads, writes)
        else:
            P.add(eng, lambda e: e.tensor_scalar(out=out, in0=in0, scalar1=s1, scalar2=s2, op0=op0, op1=op1), reads, writes)

    def stt(out, in0, scalar, in1, op0, op1, reads, writes):
        P.add("dve", lambda e: e.scalar_tensor_tensor(out=out, in0=in0, scalar=scalar, in1=in1, op0=op0, op1=op1), reads, writes)

    def cp(out, in_, reads, writes, eng="dve"):
        P.add(eng, lambda e: e.tensor_copy(out=out, in_=in_), reads, writes)

    def mm(out, lhsT, rhs, start, stop, reads, writes):
        P.add("pe", lambda e: e.matmul(out, lhsT=lhsT, rhs=rhs, start=start, stop=stop), reads, writes)

    def tr(out, in_, idn, reads, writes):
        P.add("pe", lambda e: e.transpose(out, in_, idn), reads, writes)

    def dma(q, out, in_, reads, writes, slow=False):
        if slow:
            P.add(q, lambda e: e.dma_start(out=out, in_=in_, allow_slow_non_contiguous=True), reads, writes, dma=True)
        else:
            P.add(q, lambda e: e.dma_start(out=out, in_=in_), reads, writes, dma=True)

    def memset(eng, ap, val, writes):
        P.add(eng, lambda e: e.memset(ap, val), (), writes)

    def wkeys(nm, i):
        n = {"win": 8, "wout": 8, "poolw": 8, "wup": 16, "wdn": NJ}[nm]
        return [(nm, i, k) for k in range(n)]

    def rsqrt(out, in_, which, reads, writes):
        act(out, in_, AF.Sqrt, list(reads) + ["epsc"], writes, bias=epsc[:, which:which + 1])
        P.add("dve", lambda e: e.reciprocal(out=out, in_=out), writes, writes)

    def barrier():
        P.add("pool", lambda e: e.memset(tmpc[0:1, 0:1], 0.0), (), ["EPOCH", "ARENA"])

    def arena_barrier():
        P.add("pool", lambda e: e.memset(tmpc[0:1, 0:1], 0.0), (), ["ARENA"])

    def cast_win(m):
        for kc in range(8):
            src = w_in[m, kc * 128:(kc + 1) * 128, :].rearrange("p (g c) -> p g c", g=14)
            dst = win_bf[m, :, :, kc, :].rearrange("g p c -> p g c")
            dma("pool", dst, src, (), [("win", m, kc)])

    def cast_wout(m):
        for kc in range(8):
            dma("pool", wout_bf[m, :, :, kc, :].rearrange("q p c -> p q c"),
                w_out[m, kc * 128:(kc + 1) * 128, :].rearrange("p (q c) -> p q c", q=4), (), [("wout", m, kc)])

    def cast_poolw(p):
        for g in range(4):
            for kc in range(2):
                dma("pool", poolw_bf[p, :, kc, g, :], pool_w[p, g, kc * 128:(kc + 1) * 128, :], (), [("poolw", p, g * 2 + kc)])

    def cast_wup(l):
        for kc in range(8):
            for ab in range(2):
                src = w_up[l, kc * 128:(kc + 1) * 128, ab * DFF:(ab + 1) * DFF].rearrange("p (j c) -> p j c", j=NJ)
                dst = wup_bf[l, :, :, kc, ab * 128:(ab + 1) * 128].rearrange("j p c -> p j c")
                dma("pool", dst, src, (), [("wup", l, kc * 2 + ab)])

    def cast_wdn(l):
        for j in range(NJ):
            src = w_down[l, j * 128:(j + 1) * 128, :].rearrange("p (h c) -> p h c", h=4)
            dst = wdn_bf[l, :, :, j, :].rearrange("h p c -> p h c")
            dma("pool", dst, src, (), [("wdn", l, j)])

    def setup():
        dma("sp", ident, c_ident, (), ["ident"])
        t1, k1 = sF()
        dma("sp", t1[:, 0:128], c_maskp, (), [k1])
        cp(maskp, t1[:, 0:128], [k1], ["maskp"])
        cp(identb, ident, ["ident"], ["identb"])
        memset("dve", onesb, 1.0, ["onesb"])
        memset("dve", epsc[:, 0:1], LN_EPS_S, ["epsc"])
        memset("dve", epsc[:, 1:2], 128.0 * RMS_EPS, ["epsc"])
        t2, k2 = sF()
        dma("sp", t2[0:DS, 0:DS], c_masks, (), [k2])
        cp(masks[0:DS, :], t2[0:DS, 0:DS], [k2], ["masks"])
        t3, k3 = sF()
        dma("sp", t3[:, 0:512], c_resetp.partition_broadcast(128), (), [k3])
        cp(resetp, t3[:, 0:512], [k3], ["resetp"])
        t4, k4 = sF()
        dma("sp", t4[:, 0:NTS], c_resets.partition_broadcast(128), (), [k4])
        cp(resets, t4[:, 0:NTS], [k4], ["resets"])
        dma("sp", invc, bass.AP(c_invc.tensor, 0, [[0, 128], [16, 4], [1, 16]]), (), ["invc"])
        for l in range(4):
            for w, (g_, b_) in enumerate(((ln1_g, ln1_b), (ln2_g, ln2_b))):
                dma("sp", lng[:, l, w, :], g_[l].rearrange("(c p) -> p c", p=128), (), ["lnp"], slow=True)
                dma("sp", lnb[:, l, w, :], b_[l].rearrange("(c p) -> p c", p=128), (), ["lnp"], slow=True)
            for r in range(3):
                dma("sp", cw[:, l, r, :], conv_w[l, r].rearrange("(j p) -> p j", p=128), (), ["cw"], slow=True)
            dma("sp", cbias[:, l, :], conv_b[l].rearrange("(j p) -> p j", p=128), (), ["cw"], slow=True)
        for m in range(2):
            dma("sp", hgv[:, m, :], hgrn_norm_g[m].rearrange("(h p) -> p h", p=128), (), ["hg"], slow=True)
            dma("sp", dgv[:, m:m + 1], diff_norm_g[m].rearrange("(p o) -> p o", o=1), (), ["dg"], slow=True)
            dma("sp", pscv[:, m, :], pool_scale[m].rearrange("(c p) -> p c", p=128), (), ["psc"], slow=True)
        ts(hgv, hgv, math.sqrt(128.0), None, ALU.mult, None, ["hg"], ["hg"])
        for m in range(2):
            lam_init = 0.8 - 0.6 * math.exp(-0.3 * (2 * m))
            ts(dgv[:, m:m + 1], dgv[:, m:m + 1], math.sqrt(128.0) * (1.0 - lam_init), None, ALU.mult, None, ["dg"], ["dg"])
        ts(pscv, pscv, INV_ALPHA, None, ALU.mult, None, ["psc"], ["psc"])
        l0 = tmpc[:, 0:4]; l1 = tmpc[:, 4:8]
        dma("sp", l0, lb_logits[0].rearrange("(h p) -> p h", p=128), (), ["tmpc"], slow=True)
        dma("sp", l1, lb_logits[1].rearrange("(h p) -> p h", p=128), (), ["tmpc"], slow=True)
        tt(tmpc[:, 8:12], l1, l0, ALU.subtract, ["tmpc"], ["tmpc2"])
        memset("dve", lbv[:, 0, :], 0.0, ["lb"])
        act(lbv[:, 1, :], tmpc[:, 8:12], AF.Sigmoid, ["tmpc2", "lb"], ["lb"])
        ts(omlv, lbv, -1.0, 1.0, ALU.mult, ALU.add, ["lb"], ["oml"])
        ts(nomlv, omlv, -1.0, None, ALU.mult, None, ["oml"], ["noml"])
        for m in range(2):
            lam_init = 0.8 - 0.6 * math.exp(-0.3 * (2 * m))
            lp = tmpc[:, 16:16 + 256]
            dma("sp", lp, diff_lambda[m].partition_broadcast(128), ["tmpcl"], ["tmpcl"])
            pr = tmpc[:, 300:300 + 128]
            tt(pr.rearrange("p (a b) -> p a b", a=2), lp.rearrange("p (a t b) -> p a t b", a=2, t=2)[:, :, 0, :],
               lp.rearrange("p (a t b) -> p a t b", a=2, t=2)[:, :, 1, :], ALU.mult, ["tmpcl"], ["tmpcp"])
            sm_ = tmpc[:, 440:442]
            P.add("dve", lambda e, o=sm_, i=pr: e.tensor_reduce(out=o, in_=i.rearrange("p (a b) -> p a b", a=2),
                                                                 axis=mybir.AxisListType.X, op=ALU.add), ["tmpcp"], ["tmpcs"])
            ex = tmpc[:, 444:446]
            act(ex, sm_, AF.Exp, ["tmpcs"], ["tmpce"])
            tt(neglam[:, m:m + 1], ex[:, 1:2], ex[:, 0:1], ALU.subtract, ["tmpce"], ["nl"])
            ts(neglam[:, m:m + 1], neglam[:, m:m + 1], -lam_init, None, ALU.add, None, ["nl"], ["nl", "tmpcl"])
        rb = tmpc[0:32, 500:504]
        oht, ko = sStage()
        dma("sp", rb, rel_bias, (), ["rb"])
        dma("sp", oht[0:32, 0:NREL], c_oht, (), [ko])
        for h in range(4):
            rbb, kr = sF()
            cp(rbb[0:32, 0:128], rb[:, h:h + 1].to_broadcast([32, 128]), ["rb"], [kr])
            pt, kp = PS()
            mm(pt[:, 0:NREL], rbb[0:32, 0:128], oht[0:32, 0:NREL], True, True, [kr, ko], [kp])
            ngc = tmpc[:, 510 + h:511 + h]
            ts(ngc, pt[:, NREL - 1:NREL], -1.0, None, ALU.mult, None, [kp], [("ngc", h)])
            ev, ke = sF()
            act(ev[:, 0:NREL], pt[:, 0:NREL], AF.Exp, [kp, ("ngc", h)], [ke], bias=ngc)
            dma("sp", evm[h], ev[:, 0:NREL], [ke], [("evm", h)])
        vis, kv = sStage()
        dma("sp", vis[:, 0:128], c_vis0, (), [kv])
        for h in range(4):
            for di, d in enumerate((0, -128)):
                et, ke = sF()
                src = bass.AP(evm_t, h * 128 * NREL + (127 - d), [[NREL - 1, 128], [1, 128]])
                dma("sp", et[:, 0:128], src, [("evm", h)], [ke])
                if d == 0:
                    tt(Ep[:, di, h, :], et[:, 0:128], vis[:, 0:128], ALU.mult, [ke, kv], ["Ep"])
                else:
                    cp(Ep[:, di, h, :], et[:, 0:128], [ke], ["Ep"])
            et, ke = sF()
            src = bass.AP(evm_t, h * 128 * NREL + (127 + 128), [[NREL - 1, 128], [1, DS]])
            dma("sp", et[:, 0:DS], src, [("evm", h)], [ke])
            cp(Esh[:, h, :], et[:, 0:DS], [ke], ["Es"])
            et, ke = sF()
            src = bass.AP(evm_t, h * 128 * NREL + 127, [[NREL - 1, DS], [1, DS]])
            dma("sp", et[0:DS, 0:DS], src, [("evm", h)], [ke])
            cp(Esn[0:DS, h, :], et[0:DS, 0:DS], [ke], ["Es"])

    class Grp:
        pass

    def make_group(kind, seq_ids):
        G = Grp()
        G.pending = []
        G.kind = kind
        G.seqs = list(seq_ids)
        G.nseq = len(G.seqs)
        G.T = SEQ if kind == "p" else DS
        G.P = 0 if kind == "p" else PAST
        G.NT = G.nseq * G.T
        G.W = min(512, G.NT)
        G.nt = G.NT // G.W
        G.L = min(64, G.T)
        G.blocks = []
        for ti in range(G.nt):
            bl = []
            if kind == "p":
                for j in range(G.W // 128):
                    bl.append((128 * j, 128, 0, ti * G.W + 128 * j))
            else:
                for s in range(G.nseq):
                    bl.append((G.T * s, G.T, s, 0))
            G.blocks.append(bl)
        G.ns_t = 1 if kind == "p" else G.nseq
        G.Ts = G.W // G.ns_t
        top[0] = base_top
        G.o_xres = carve(8 * G.NT * 4)
        G.o_xbf = carve(8 * G.NT * 2)
        G.o_S = carve(G.nseq * 4 * 128 * 4)
        Sst_box[0] = view(G.o_S, F32, (G.nseq, 4, 128))
        KT = G.P + G.T
        G.KT = KT
        G.nkh = G.P // 128
        if kind == "p":
            a0 = top[0]
            G.o_qT = carve(4 * G.NT * 2); G.o_kT = carve(4 * KT * 2); G.o_V = carve((G.T // 128) * 512 * 2)
            G.o_poolw = G.o_qT
            a1 = top[0]
            top[0] = a0
            G.o_gT = carve(NJ * G.W * 2); G.o_wdn = carve(2 * NJ * 256 * 2)
            top[0] = max(top[0], a1)
            G.alias = True
        else:
            G.o_qT = carve(4 * G.NT * 2); G.o_kT = carve(4 * KT * 2); G.o_V = carve((G.nkh + 1) * 512 * 2)
            G.o_poolw = carve(2 * 4 * 256 * 2)
            G.o_gT = carve(NJ * G.W * 2); G.o_wdn = carve(2 * NJ * 256 * 2)
            G.alias = False
        print("group", kind, "arena top", top[0], "of", ARENA_BYTES)
        G.xres = view(G.o_xres, F32, (8, G.NT))
        G.xbf = view(G.o_xbf, BF16, (8, G.NT))
        G.qT = view(G.o_qT, BF16, (4, G.NT))
        G.kT = view(G.o_kT, BF16, (4, KT))
        G.V = view(G.o_V, BF16, ((G.T // 128) if kind == "p" else (G.nkh + 1), 512))
        G.poolw = view(G.o_poolw, BF16, (2, 4, 256))
        G.gT = view(G.o_gT, BF16, (NJ, G.W))
        G.wdn = view(G.o_wdn, BF16, (2, NJ, 256))
        if kind == "p":
            G.x_in, G.y_out, G.k_out, G.v_out, G.h_out, G.pl_out, G.cv_out = x_p, y_p, k_p, v_p, h_p, pl_p, cv_p
            G.maskH, G.reset = maskp, resetp
        else:
            G.x_in, G.y_out, G.k_out, G.v_out, G.h_out, G.pl_out, G.cv_out = x_s, y_s, k_s, v_s, h_s, pl_s, cv_s
            G.maskH, G.reset = masks, resets
        return G

    def kx(oc, ti): return ("xres", oc, ti)
    def kb(oc, ti): return ("A", "xbf", oc, ti)

    def load_x(G):
        import os
        KSUB = int(os.environ.get("KSUB", "9"))
        for ti in range(G.nt):
            c0 = ti * G.W
            banks = [PS() for _ in range(8)]
            for (co, nb, s, t0) in G.blocks[ti]:
                st, ks_ = sStage()
                dma("sp", st[0:nb, :], G.x_in[G.seqs[s], t0:t0 + nb, :], (), [ks_])
                for oc in range(8 if KSUB >= 1 else 0):
                    pb_, kp = banks[oc]
                    tr(pb_[:, co:co + nb], st[0:nb, oc * 128:(oc + 1) * 128], ident[0:nb, 0:nb], [ks_, "ident"], [kp])
            for oc in range(8 if KSUB >= 2 else 0):
                pb_, kp = banks[oc]
                act(G.xres[:, oc, c0:c0 + G.W], pb_[:, 0:G.W], AF.Copy, [kp], [kx(oc, ti)])
                if KSUB == 3:
                    cp(G.xbf[:, oc, c0:c0 + G.W], G.xres[:, oc, c0:c0 + G.W], [kx(oc, ti)], [kb(oc, ti)])
                elif KSUB == 5:
                    cp(G.xbf[:, oc, c0:c0 + G.W], pb_[:, 0:G.W], [kp], [("xbftest", oc, ti)])
                elif KSUB == 6:
                    cp(G.xbf[:, oc, c0:c0 + G.W], pb_[:, 0:G.W], [kp, kx(oc, ti)], [kb(oc, ti)])
                elif KSUB == 7:
                    t_, k_ = sF()
                    cp(t_[:, 0:G.W], pb_[:, 0:G.W], [kp], [k_])
                elif KSUB > 7:
                    cp(G.xbf[:, oc, c0:c0 + G.W], pb_[:, 0:G.W], [kp], [kb(oc, ti)])

    def store_y(G):
        import os
        KSUB = int(os.environ.get("KSUB", "9"))
        if KSUB < 4:
            return
        flush_pending(G)
        for ti in range(G.nt):
            c0 = ti * G.W
            for (co, nb, s, t0) in G.blocks[ti]:
                st, ks_ = sStage()
                for half in range(2):
                    pb_, kp = PS()
                    for o4 in range(4):
                        oc = half * 4 + o4
                        tr(pb_[0:nb, o4 * 128:(o4 + 1) * 128], G.xres[:, oc, c0 + co:c0 + co + nb], ident, [kx(oc, ti), "ident"], [kp])
                    if half == 0:
                        act(st[0:nb, 0:512], pb_[0:nb, :], AF.Copy, [kp], [ks_])
                    else:
                        cp(st[0:nb, 512:1024], pb_[0:nb, :], [kp], [ks_])
                dma("pool", G.y_out[G.seqs[s], t0:t0 + nb, :], st[0:nb, :], [ks_], [("yout",)])

    def pop_pending(G, n=1):
        for _ in range(n):
            if G.pending:
                G.pending.pop(0)[1]()

    def flush_pending(G, tile=None):
        keep = []
        for (t_id, fn) in G.pending:
            if tile is None or t_id == tile:
                fn()
            else:
                keep.append((t_id, fn))
        G.pending = keep

    def ln_tile(G, l, which, ti, ychunk, yscale=None):
        W = G.W
        c0 = ti * W
        s1, k1 = psb[6], ("ps", 6)
        s2, k2 = psb[7], ("ps", 7)
        for oc in range(8):
            yp, ky = ychunk(oc)
            xr = G.xres[:, oc, c0:c0 + W]
            sc = INV_ALPHA if yscale is None else yscale(oc)
            stt(xr, yp[:, 0:W], sc, xr, ALU.mult, ALU.add, [ky, kx(oc, ti)] + ([] if yscale is None else ["psc"]), [kx(oc, ti)])
            rb_, kr = sB()
            act(rb_[:, 0:W], xr, AF.Copy, [kx(oc, ti)], [kr])
            sq_, kq = sB()
            act(sq_[:, 0:W], xr, AF.Square, [kx(oc, ti)], [kq])
            mm(s1[:, 0:W], onesb, rb_[:, 0:W], oc == 0, oc == 7, [kr, "onesb"], [k1])
            mm(s2[:, 0:W], onesb, sq_[:, 0:W], oc == 0, oc == 7, [kq, "onesb"], [k2])
            pop_pending(G, 1)
        flush_pending(G)
        mean, km = FX(0)
        ts(mean[:, 0:W], s1[:, 0:W], 1.0 / D, None, ALU.mult, None, [k1], [km])
        msq, kq2 = FX(1)
        tt(msq[:, 0:W], mean[:, 0:W], mean[:, 0:W], ALU.mult, [km], [kq2])
        rstd, krs = FX(2)
        stt(rstd[:, 0:W], s2[:, 0:W], 1.0 / D, msq[:, 0:W], ALU.mult, ALU.subtract, [k2, kq2], [krs])
        rsqrt(rstd[:, 0:W], rstd[:, 0:W], 0, [krs], [krs])
        for oc in range(8):
            def partB(oc=oc, mean=mean, rstd=rstd, km=km, krs=krs):
                xr = G.xres[:, oc, c0:c0 + W]
                t_, kt = sFr(3, NF)
                tt(t_[:, 0:W], xr, mean[:, 0:W], ALU.subtract, [kx(oc, ti), km], [kt])
                tt(t_[:, 0:W], t_[:, 0:W], rstd[:, 0:W], ALU.mult, [kt, krs], [kt])
                act(xr, t_[:, 0:W], AF.Identity, [kt, "lnp"], [kx(oc, ti)], bias=lnb[:, l, which, oc:oc + 1], scale=lng[:, l, which, oc:oc + 1])
                act(G.xbf[:, oc, c0:c0 + W], t_[:, 0:W], AF.Identity, [kt, "lnp"], [kb(oc, ti)], bias=lnb[:, l, which, oc:oc + 1], scale=lng[:, l, which, oc:oc + 1])
            G.pending.append((ti, partB))

    def ffn(G, l):
        W, ns, Ts = G.W, G.ns_t, G.Ts
        if G.kind == "s":
            for s in range(G.nseq):
                st, ks_ = sStage()
                for r in range(2):
                    dma("sp", st[r * NJ:(r + 1) * NJ, 0:128], state_conv[l, G.seqs[s], r].rearrange("(j p) -> j p", p=128), (), [ks_])
                pt, kp = PS(4, 6)
                tr(pt[:, 0:2 * NJ], st[0:2 * NJ, 0:128], ident[0:2 * NJ, 0:2 * NJ], [ks_, "ident"], [kp])
                cp(atail[:, :, s, :].rearrange("p j r -> p r j"), pt[:, 0:2 * NJ].rearrange("p (r j) -> p r j", r=2), [kp], ["atail"])
        else:
            memset("pool", atail, 0.0, ["atail"])
        for ti in range(G.nt):
            c0 = ti * W
            flush_pending(G, tile=ti)
            for j in range(NJ):
                wr, kw = sRing()
                dma("sp", wr, wup_bf[l, j], wkeys("wup", l), [kw])
                pa, ka = PS(0, 4)
                pb_, kpb = PS(0, 4)
                for kc in range(8):
                    mm(pa[:, 0:W], wr[:, kc, 0:128], G.xbf[:, kc, c0:c0 + W], kc == 0, kc == 7, [kw, kb(kc, ti)], [ka])
                for kc in range(8):
                    mm(pb_[:, 0:W], wr[:, kc, 128:256], G.xbf[:, kc, c0:c0 + W], kc == 0, kc == 7, [kw, kb(kc, ti)], [kpb])
                a_sb, kas = sFr(3, NF)
                a3 = a_sb[:, 0:ns * (Ts + 2)].rearrange("p (s t) -> p s t", s=ns)
                cp(a3[:, :, 0:2], atail[:, j, 0:ns, :], ["atail"], [kas], eng="pool")
                act(a3[:, :, 2:2 + Ts], pa[:, 0:W].rearrange("p (s t) -> p s t", s=ns), AF.Copy, [ka, kas], [kas])
                t0_, k0 = sFr(3, NF)
                t03 = t0_[:, 0:W].rearrange("p (s t) -> p s t", s=ns)
                act(t0_[:, 0:W], pa[:, 0:W], AF.Identity, [ka, "cw"], [k0], bias=cbias[:, l, j:j + 1], scale=cw[:, l, 2, j:j + 1])
                stt(t03, a3[:, :, 1:1 + Ts], cw[:, l, 1, j:j + 1], t03, ALU.mult, ALU.add, [kas, k0, "cw"], [k0])
                stt(t03, a3[:, :, 0:Ts], cw[:, l, 0, j:j + 1], t03, ALU.mult, ALU.add, [kas, k0, "cw"], [k0])
                cp(atail[:, j, 0:ns, :], a3[:, :, Ts:Ts + 2], [kas, "atail"], ["atail"], eng="pool")
                act(t0_[:, 0:W], t0_[:, 0:W], AF.Silu, [k0], [k0])
                tt(G.gT[:, j, 0:W], t0_[:, 0:W], pb_[:, 0:W], ALU.mult, [k0, kpb], [("A", "gT", j)])
                pop_pending(G, 1)
            def ychunk(oc, l=l):
                py, kpy = PS(4, 6)
                qd, o2 = oc // 2, oc % 2
                if o2 == 0:
                    dma("sp", G.wdn[:, qd % 2], wdn_bf[l, qd], wkeys("wdn", l), [("A", "wdn", qd % 2)])
                for j in range(NJ):
                    mm(py[:, 0:W], G.wdn[:, qd % 2, j, o2 * 128:(o2 + 1) * 128], G.gT[:, j, 0:W], j == 0, j == NJ - 1,
                       [("A", "wdn", qd % 2), ("A", "gT", j)], [kpy])
                return py, kpy
            ln_tile(G, l, 1, ti, ychunk)
        for s in range(G.nseq):
            t44, k44 = sFr(3, NF)
            cp(t44[:, 0:2 * NJ].rearrange("p (r j) -> p r j", r=2), atail[:, :, s, :].rearrange("p j r -> p r j"), ["atail"], [k44])
            pt, kp = PS(4, 6)
            tr(pt[0:2 * NJ, 0:128], t44[:, 0:2 * NJ], ident, [k44, "ident"], [kp])
            st, ks_ = sStage()
            act(st[0:2 * NJ, 0:128], pt[0:2 * NJ, 0:128], AF.Copy, [kp], [ks_])
            for r in range(2):
                dma("pool", G.cv_out[l, G.seqs[s], r].rearrange("(j p) -> j p", p=128), st[r * NJ:(r + 1) * NJ, 0:128], [ks_], [("cvout",)])

    def mixer_pool(G, l):
        p = l // 2
        W = G.W
        flush_pending(G)
        for s in range(G.nseq):
            cend = (s + 1) * G.T
            tl = (cend - 1) // W
            st, ks_ = sStage()
            for half in range(2):
                pb_, kp = PS(4, 6)
                for o4 in range(4):
                    oc = half * 4 + o4
                    tr(pb_[0:15, o4 * 128:(o4 + 1) * 128], G.xres[:, oc, cend - 15:cend], ident, [kx(oc, tl), "ident"], [kp])
                if half == 0:
                    act(st[0:15, 0:512], pb_[0:15, :], AF.Copy, [kp], [ks_])
                else:
                    cp(st[0:15, 512:1024], pb_[0:15, :], [kp], [ks_])
            dma("pool", G.pl_out[p, G.seqs[s]], st[0:15, :], [ks_], [("plout",)])
        dma("sp", G.poolw, poolw_bf[p], wkeys("poolw", p), [("A", "poolw")])
        for ti in range(G.nt):
            c0 = ti * W
            for s in range(G.ns_t):
                sc0 = c0 + s * G.Ts
                Ts = G.Ts
                first = (G.kind == "s") or (ti == 0)
                if G.kind == "s":
                    sth, ksh = sStage()
                    dma("sp", sth[0:15, :], state_pool[p, G.seqs[s]], (), [ksh])
                    phh, kph = PS(6, 8)
                    for oc in range(8):
                        tr(phh[:, oc * 16:oc * 16 + 15], sth[0:15, oc * 128:(oc + 1) * 128], ident[0:15, 0:15], [ksh, "ident"], [kph])
                for oc in range(8):
                    g = oc // 2
                    w = POOL_W[g]
                    if first:
                        src, ksrc = sF()
                        if G.kind == "s":
                            cp(src[:, 1:16], phh[:, oc * 16:oc * 16 + 15], [kph], [ksrc])
                        else:
                            memset("pool", src[:, 0:16], 0.0, [ksrc])
                        cp(src[:, 16:16 + Ts], G.xres[:, oc, sc0:sc0 + Ts], [kx(oc, ti)], [ksrc], eng="pool")
                        cur = src[:, 16 - 15:16 + Ts]
                        kcur = [ksrc]
                    else:
                        cur = G.xres[:, oc, sc0 - 15:sc0 + Ts]
                        kcur = [kx(oc, ti), kx(oc, ti - 1)]
                    xt = cur[:, 15:15 + Ts]
                    lo = 15
                    step = 1
                    while step < w:
                        nxt, kn = sF()
                        n = lo - step + Ts
                        tt(nxt[:, 0:n], cur[:, step:step + n], cur[:, 0:n], ALU.add, kcur, [kn])
                        cur, kcur, lo = nxt[:, 0:n], [kn], lo - step
                        step *= 2
                    sw = cur[:, lo:lo + Ts]
                    outp = G.xbf[:, oc, sc0:sc0 + Ts]
                    stt(outp, sw, 1.0 / w, xt, ALU.mult, ALU.subtract, kcur + [kx(oc, ti)] + ([ksrc] if first else []), [kb(oc, ti)])
                    if G.kind == "p" and ti == 0:
                        fx, kf = sF()
                        tt(fx[:, 0:16], sw[:, 0:16], invc[:, g, :], ALU.mult, kcur + ["invc"], [kf])
                        tt(outp[:, 0:16], fx[:, 0:16], xt[:, 0:16], ALU.subtract, [kf, kx(oc, ti)] + ([ksrc] if first else []), [kb(oc, ti)])

        for ti in range(G.nt):
            c0 = ti * W

            def ychunk(oc):
                g = oc // 2
                py, kpy = PS(4, 6)
                for kc in range(2):
                    mm(py[:, 0:W], G.poolw[:, kc, g, (oc % 2) * 128:(oc % 2 + 1) * 128], G.xbf[:, 2 * g + kc, c0:c0 + W],
                       kc == 0, kc == 1, [("A", "poolw"), kb(2 * g + kc, ti)], [kpy])
                return py, kpy
            ln_tile(G, l, 0, ti, ychunk, yscale=lambda oc: pscv[:, p, oc:oc + 1])

    def mixer_ab(G, l):
        m = l // 2
        W, L = G.W, G.L
        flush_pending(G)
        nch = W // L
        mid = L // 2 - 1
        if G.kind == "s":
            for s in range(G.nseq):
                dma("sp", Sst_box[0][:, s, :, :], state_hgrn[m, G.seqs[s]].rearrange("h k v -> k h v"), (), [("S", s)])
        else:
            memset("pool", Sst_box[0][:, 0, :, :], 0.0, [("S", 0)])
        if G.kind == "s":
            for s in range(G.nseq):
                b = G.seqs[s]
                pass
        for ti in range(G.nt):
            c0 = ti * W
            aq_sb = {}; sig_sb = {}; sg_sb = {}; vtok = {}
            order = [0, 2, 6, 4, 1, 3, 7, 5, 8, 9, 10, 11, 12, 13]
            for g in order:
                wr, kw = sRing()
                dma("sp", wr, win_bf[m, g], wkeys("win", m), [kw])
                sec, hh0 = g // 2, (g % 2) * 2
                if sec in (0, 1, 3, 4, 5):
                    for hi in range(2):
                        h = hh0 + hi
                        pt, kp = PS()
                        for kc in range(8):
                            mm(pt[:, 0:W], wr[:, kc, hi * 128:(hi + 1) * 128], G.xbf[:, kc, c0:c0 + W], kc == 0, kc == 7, [kw, kb(kc, ti)], [kp])
                        if sec == 0:
                            t_, k_ = FX(hi); act(t_[:, 0:W], pt[:, 0:W], AF.Copy, [kp], [k_]); aq_sb[h] = (t_, k_)
                        elif sec == 1:
                            t_, k_ = FX(2 + hi); act(t_[:, 0:W], pt[:, 0:W], AF.Sigmoid, [kp], [k_]); sig_sb[h] = (t_, k_)
                        elif sec == 3:
                            t_, k_ = BX(hi); act(t_[:, 0:W], pt[:, 0:W], AF.Silu, [kp], [k_]); sg_sb[h] = (t_, k_)
                        elif sec == 4:
                            cp(G.qT[:, h, c0:c0 + W], pt[:, 0:W], [kp], [("A", "qT", h, ti)])
                        elif G.kind == "p":
                            act(G.kT[:, h, c0:c0 + W], pt[:, 0:W], AF.Copy, [kp], [("A", "kT", h, ti)])
                        else:
                            cp(G.knew[:, h, 0:W], pt[:, 0:W], [kp], [("A", "knew", h)])
                if sec in (2, 5, 6):
                    for bi, (co, nb, s, t0) in enumerate(G.blocks[ti]):
                        pt, kp = PS()
                        for kc in range(8):
                            mm(pt[0:nb, 0:256], G.xbf[:, kc, c0 + co:c0 + co + nb], wr[:, kc, :], kc == 0, kc == 7, [kw, kb(kc, ti)], [kp])
                        if sec == 2:
                            t_, k_ = BX(2 + bi // 2)
                            tv = t_[:, (bi % 2) * 256:(bi % 2 + 1) * 256]
                            vtok[(g % 2, bi)] = (tv, k_)
                            act(tv[0:nb, :], pt[0:nb, 0:256], AF.Copy, [kp], [k_])
                        else:
                            st, ks_ = sStage()
                            act(st[0:nb, 0:256], pt[0:nb, 0:256], AF.Copy, [kp], [ks_])
                            outt = G.k_out if sec == 5 else G.v_out
                            dma("pool", outt[m, G.seqs[s], t0:t0 + nb, (g % 2) * 256:(g % 2 + 1) * 256], st[0:nb, 0:256], [ks_], [("kvout",)])
                            if sec == 6:
                                if G.kind == "p":
                                    vt = t0 // 128
                                    cp(G.V[0:nb, vt, (g % 2) * 256:(g % 2 + 1) * 256], st[0:nb, 0:256], [ks_], [("A", "V", vt)])
                                else:
                                    cp(G.vnew[0:nb, s, (g % 2) * 256:(g % 2 + 1) * 256], st[0:nb, 0:256], [ks_], [("A", "vnew", s)])
                if g in (4, 5):
                    for hi in range(2):
                        h = (g % 2) * 2 + hi
                        hgrn_tile_head(G, m, ti, h, aq_sb[h], sig_sb[h], sg_sb[h], vtok, hi, g % 2)
            for h in range(4):
                cp(G.xbf[:, h, c0:c0 + W], hout[:, h, 0:W], [("hout", h)], [kb(h, ti)], eng="pool")
        for s in range(G.nseq):
            dma("pool", G.h_out[m, G.seqs[s]].rearrange("h k v -> k h v"), Sst_box[0][:, s, :, :], [("S", s)], [("hout",)])
        attention(G, m)
        for ti in range(G.nt):
            c0 = ti * W

            wst = {}

            def ychunk(oc, c0=c0, ti=ti, wst=wst):
                py, kpy = PS(4, 6)
                if oc % 2 == 0:
                    wst["w"] = sRing()
                    dma("sp", wst["w"][0], wout_bf[m, oc // 2], wkeys("wout", m), [wst["w"][1]])
                wr, kw = wst["w"]
                for kc in range(8):
                    mm(py[:, 0:W], wr[:, kc, (oc % 2) * 128:(oc % 2 + 1) * 128], G.xbf[:, kc, c0:c0 + W], kc == 0, kc == 7,
                       [kw, kb(kc, ti)], [kpy])
                return py, kpy
            ln_tile(G, l, 0, ti, ychunk)

    def hgrn_tile_head(G, m, ti, h, aq, sig, sg, vtok, hi, gp):
        W, L = G.W, G.L
        nch = W // L
        mid = L // 2 - 1
        c0 = ti * W
        aq_t, kaq = aq
        sig_t, ksig = sig
        sg_t, ksg = sg
        lf, klf = FX(4)
        act(lf[:, 0:W], sig_t[:, 0:W], AF.Ln, [ksig, "lb", "oml"], [klf], bias=lbv[:, m, h:h + 1], scale=omlv[:, m, h:h + 1])
        kin = sig_t
        ts(kin[:, 0:W], sig_t[:, 0:W], nomlv[:, m, h:h + 1], omlv[:, m, h:h + 1], ALU.mult, ALU.add, [ksig, "oml", "noml"], [ksig])
        bt, kbt = FX(5)
        P.add("dve", lambda e: e.tensor_tensor_scan(out=bt[:, 0:W], data0=G.reset[:, 0:W], data1=lf[:, 0:W], initial=0.0,
                                                    op0=ALU.mult, op1=ALU.add), [klf, "resetp", "resets"], [kbt])
        b3 = bt[:, 0:W].rearrange("p (c t) -> p c t", c=nch)
        dd = lf
        d3 = dd[:, 0:W].rearrange("p (c t) -> p c t", c=nch)
        tt(d3, b3, b3[:, :, mid:mid + 1].to_broadcast([128, nch, L]), ALU.subtract, [kbt, klf], [klf])
        sidx = hgrn_tile_head.cnt % 4; hgrn_tile_head.cnt += 1
        em = smallv[:, 3 * sidx + 0, 0:nch]; el = smallv[:, 3 * sidx + 1, 0:nch]; eb = smallv[:, 3 * sidx + 2, 0:nch]
        ksm = ("small", sidx)
        act(em, b3[:, :, mid], AF.Exp, [kbt], [ksm])
        act(el, d3[:, :, L - 1], AF.Exp, [klf, ksm], [ksm])
        act(eb, b3[:, :, L - 1], AF.Exp, [kbt, ksm], [ksm])
        e1 = bt
        act(e1[:, 0:W], dd[:, 0:W], AF.Exp, [klf, kbt, ksm], [kbt])
        qt, kqt = BX(4)
        tt(qt[:, 0:W], aq_t[:, 0:W], e1[:, 0:W], ALU.mult, [kaq, kbt], [kqt])
        act(e1[:, 0:W], dd[:, 0:W], AF.Exp, [klf, kbt, kqt], [kbt], scale=-1.0)
        kt, kkt = BX(5)
        tt(kt[:, 0:W], kin[:, 0:W], e1[:, 0:W], ALU.mult, [ksig, kbt], [kkt])
        oacc = aq_t
        koacc = kaq
        for bi, (co, nb, s, t0) in enumerate(G.blocks[ti]):
            vt_t, kvt = vtok[(gp, bi)]
            vv = vt_t[0:nb, hi * 128:(hi + 1) * 128]
            ptk, kptk = PS()
            ptkb = ptk[:, 0:256].bitcast(BF16)
            tr(ptkb[0:nb, 0:128], kt[:, co:co + nb], identb, [kkt, "identb"], [kptk])
            b7, kktok = BX(7)
            ktok = b7[:, (bi % 2) * 256:(bi % 2) * 256 + 128]
            cp(ktok[0:nb, 0:128], ptkb[0:nb, 0:128], [kptk], [kktok])
            psc, kpsc = PS()
            mm(psc[0:nb, 0:nb], kt[:, co:co + nb], qt[:, co:co + nb], True, True, [kkt, kqt], [kpsc])
            scm, kscm = b7[:, (bi % 2) * 256 + 128:(bi % 2) * 256 + 256], kktok
            tt(scm[0:nb, 0:nb], psc[0:nb, 0:nb], G.maskH[0:nb, 0:nb], ALU.mult, [kpsc, "maskp", "masks"], [kscm])
            po, kpo = PS()
            mm(po[:, 0:nb], vv, scm[0:nb, 0:nb], True, True, [kvt, kscm], [kpo])
            act(oacc[:, co:co + nb], po[:, 0:nb], AF.Copy, [kpo, kqt], [koacc])
            for ci in range(nb // L):
                c = (co + ci * L) // L
                S = Sst_box[0][:, s, h, :]
                kS = ("S", s)
                act(Sbf, S, AF.Copy, [kS, ksm], ["Sbf"], scale=em[:, c:c + 1])
                px, kpx = PS()
                mm(px[:, 0:L], Sbf, qt[:, co + ci * L:co + (ci + 1) * L], True, True, ["Sbf", kqt], [kpx])
                tt(oacc[:, co + ci * L:co + (ci + 1) * L], oacc[:, co + ci * L:co + (ci + 1) * L], px[:, 0:L], ALU.add, [koacc, kpx], [koacc])
                pu, kpu = PS()
                mm(pu[:, 0:128], ktok[ci * L:(ci + 1) * L, 0:128], vv[ci * L:(ci + 1) * L, :], True, True, [kktok, kvt], [kpu])
                tmp, ktmp = FX(6)
                ts(tmp[:, 0:128], pu[:, 0:128], el[:, c:c + 1], None, ALU.mult, None, [kpu, ksm], [ktmp])
                stt(S, S, eb[:, c:c + 1], tmp[:, 0:128], ALU.mult, ALU.add, [kS, ksm, ktmp], [kS])
        sq, ksq = BX(6)
        act(sq[:, 0:W], oacc[:, 0:W], AF.Square, [koacc], [ksq])
        pss, kpss = PS()
        mm(pss[:, 0:W], onesb, sq[:, 0:W], True, True, [ksq, "onesb"], [kpss])
        rs, krs = FX(6)
        rsqrt(rs[:, 0:W], pss[:, 0:W], 1, [kpss], [krs])
        stt(rs[:, 0:W], oacc[:, 0:W], hgv[:, m, h:h + 1], rs[:, 0:W], ALU.mult, ALU.mult, [koacc, krs, "hg"], [krs])
        tt(hout[:, h, 0:W], rs[:, 0:W], sg_t[:, 0:W], ALU.mult, [krs, ksg], [("hout", h)])
    hgrn_tile_head.cnt = 0

    def attention(G, m):
        T, Pn = G.T, G.P
        QW = min(512, T)
        nq = T // QW
        for s in range(G.nseq):
            b = G.seqs[s]
            scol = s * T
            if G.kind == "s":
                for kt_ in range(G.nkh):
                    st, ks_ = sStage()
                    dma("sp", st[:, 0:512], cache_k[m, b, kt_ * 128:(kt_ + 1) * 128, :], (), [ks_])
                    pt, kp = PS(4, 8)
                    for h in range(4):
                        tr(pt[:, h * 128:(h + 1) * 128], st[:, h * 128:(h + 1) * 128], ident, [ks_, "ident"], [kp])
                    act(G.kT[:, :, kt_ * 128:(kt_ + 1) * 128], pt[:, :].rearrange("p (h k) -> p h k", h=4), AF.Copy, [kp], [("A", "kTh", kt_)])
                    st2, ks2 = sStage()
                    dma("sp", st2[:, 0:512], cache_v[m, b, kt_ * 128:(kt_ + 1) * 128, :], (), [ks2])
                    cp(G.V[:, kt_, :], st2[:, 0:512], [ks2], [("A", "Vh", kt_)], eng="pool")
                for h in range(4):
                    cp(G.kT[:, h, Pn:Pn + T], G.knew[:, h, scol:scol + T], [("A", "knew", h)], [("A", "kTn", h)], eng="pool")
                cp(G.V[0:T, G.nkh, :], G.vnew[0:T, s, :], [("A", "vnew", s)], [("A", "Vn")], eng="pool")
            ktiles = []
            for kt_ in range(G.nkh):
                ktiles.append((kt_ * 128, 128, kt_, True))
            if G.kind == "p":
                for kt_ in range(T // 128):
                    ktiles.append((kt_ * 128, 128, kt_, False))
            else:
                ktiles.append((Pn, T, G.nkh, False))
            for h in range(4):
                for qi in range(nq):
                    q0 = qi * QW
                    ti = (scol + q0) // G.W
                    accs = [(psb[i], ("ps", i)) for i in range(4)]
                    vis = []
                    for (k0, nk, vt, hist) in ktiles:
                        if hist:
                            vis.append((k0, nk, vt, hist, 0))
                        else:
                            krel = k0 - Pn
                            if krel >= q0 + QW:
                                continue
                            vis.append((k0, nk, vt, hist, max(0, krel - q0)))
                    for idx, (k0, nk, vt, hist, qs) in enumerate(vis):
                        first, last = idx == 0, idx == len(vis) - 1
                        nqc = QW - qs
                        qsl = slice(scol + q0 + qs, scol + q0 + QW)
                        if G.kind == "p":
                            kkeys = [("A", "kT", h, (k0) // G.W)]
                            vkeys = [("A", "V", vt)]
                        else:
                            kkeys = [("A", "kTh", vt)] if hist else [("A", "kTn", h)]
                            vkeys = [("A", "Vh", vt)] if hist else [("A", "Vn")]
                        pp = []
                        for half in range(2):
                            psc, kpsc = PS(4, 8)
                            pr = slice(half * 64, (half + 1) * 64)
                            mm(psc[0:nk, 0:nqc], G.kT[pr, h, k0:k0 + nk], G.qT[pr, h, qsl], True, True,
                               kkeys + [("A", "qT", h, ti)], [kpsc])
                            pt_, kpt = sB()
                            act(pt_[0:nk, 0:nqc], psc[0:nk, 0:nqc], AF.Exp, [kpsc], [kpt], scale=0.125)
                            if G.kind == "p":
                                if not hist:
                                    krel = k0 - Pn
                                    for qb in range(qs, QW, 128):
                                        d = krel - (q0 + qb)
                                        if d in (0, -128):
                                            di = 0 if d == 0 else 1
                                            sl = slice(qb - qs, qb - qs + 128)
                                            tt(pt_[0:nk, sl], pt_[0:nk, sl], Ep[0:nk, di, h, :], ALU.mult, [kpt, "Ep"], [kpt])
                            else:
                                if hist and k0 == Pn - 128:
                                    tt(pt_[0:nk, 0:nqc], pt_[0:nk, 0:nqc], Esh[0:nk, h, :], ALU.mult, [kpt, "Es"], [kpt])
                                if not hist:
                                    tt(pt_[0:nk, 0:nqc], pt_[0:nk, 0:nqc], Esn[0:nk, h, :], ALU.mult, [kpt, "Es"], [kpt])
                            pp.append((pt_, kpt))
                        for half in range(2):
                            pt_, kpt = pp[half]
                            (po, kpo), (psm, kpsm) = accs[2 * half], accs[2 * half + 1]
                            mm(po[:, qs:QW], G.V[0:nk, vt, h * 128:(h + 1) * 128], pt_[0:nk, 0:nqc], first, last, vkeys + [kpt], [kpo])
                            mm(psm[:, qs:QW], onesb[0:nk, :], pt_[0:nk, 0:nqc], first, last, ["onesb", kpt], [kpsm])
                    t1, k1 = FX(0); t2, k2 = FX(1)
                    P.add("dve", lambda e, o=t1, i=accs[1][0]: e.reciprocal(out=o[:, 0:QW], in_=i[:, 0:QW]), [accs[1][1]], [k1])
                    tt(t1[:, 0:QW], accs[0][0][:, 0:QW], t1[:, 0:QW], ALU.mult, [accs[0][1], k1], [k1])
                    P.add("dve", lambda e, o=t2, i=accs[3][0]: e.reciprocal(out=o[:, 0:QW], in_=i[:, 0:QW]), [accs[3][1]], [k2])
                    tt(t2[:, 0:QW], accs[2][0][:, 0:QW], t2[:, 0:QW], ALU.mult, [accs[2][1], k2], [k2])
                    stt(t1[:, 0:QW], t2[:, 0:QW], neglam[:, m:m + 1], t1[:, 0:QW], ALU.mult, ALU.add, [k1, k2, "nl"], [k1])
                    sq, ksq = sB()
                    act(sq[:, 0:QW], t1[:, 0:QW], AF.Square, [k1], [ksq])
                    pss, kpss = PS(4, 8)
                    mm(pss[:, 0:QW], onesb, sq[:, 0:QW], True, True, [ksq, "onesb"], [kpss])
                    rsqrt(t2[:, 0:QW], pss[:, 0:QW], 1, [kpss, k2], [k2])
                    P.add("dve", lambda e, o=G.xbf[:, 4 + h, scol + q0:scol + q0 + QW], a=t1, r=t2, g_=dgv[:, m:m + 1]:
                          e.scalar_tensor_tensor(out=o, in0=a[:, 0:QW], scalar=g_, in1=r[:, 0:QW], op0=ALU.mult, op1=ALU.mult),
                          [k1, k2, "dg"], [kb(4 + h, ti)])

    import os
    LEVEL = int(os.environ.get("KLEVEL", "99"))
    KL = int(os.environ.get("KLAYERS", "4"))
    KG = os.environ.get("KGROUPS", "ps")
    setup()
    cast_order = [("win", 0), ("wout", 0), ("wup", 0), ("wdn", 0), ("poolw", 0), ("wup", 1), ("wdn", 1),
                  ("win", 1), ("wout", 1), ("wup", 2), ("wdn", 2), ("poolw", 1), ("wup", 3), ("wdn", 3)]
    castf = {"win": cast_win, "wout": cast_wout, "wup": cast_wup, "wdn": cast_wdn, "poolw": cast_poolw}
    if LEVEL >= 1:
        for nm, i in cast_order[:5]:
            castf[nm](i)
    cast_later = {0: cast_order[5:7], 1: cast_order[7:12], 2: cast_order[12:]}

    groups = [("p", [i]) for i in range(NPS)]
    if NSS > 0:
        groups.append(("s", list(range(NSS))))
    groups = [g_ for g_ in groups if g_[0] in KG]
    for gi, (kind, ids) in enumerate(groups):
        if LEVEL < 2:
            break
        barrier()
        G = make_group(kind, ids)
        if kind == "s":
            G.o_knew = carve(4 * G.NT * 2); G.knew = view(G.o_knew, BF16, (4, G.NT))
            G.o_vnew = carve(G.nseq * 512 * 2); G.vnew = view(G.o_vnew, BF16, (G.nseq, 512))
        load_x(G)
        for l in range(DEPTH if LEVEL >= 3 else 0):
            if l >= KL:
                break
            if gi == 0:
                for nm, i in cast_later.get(l, []):
                    castf[nm](i)
            if G.alias:
                arena_barrier()
            if l % 2 == 0:
                mixer_ab(G, l)
            else:
                mixer_pool(G, l)
            if G.alias:
                arena_barrier()
            ffn(G, l)
        store_y(G)
    barrier()

    P.finalize(sems)
    with nc.Block() as block:
        @block.sync
        def _(e):
            P.run_stream("sp", e)

        @block.scalar
        def _(e):
            P.run_stream("act", e)

        @block.vector
        def _(e):
            P.run_stream("dve", e)

        @block.gpsimd
        def _(e):
            P.run_stream("pool", e)

        @block.tensor
        def _(e):
            P.run_stream("pe", e)
    es.close()
    return nc, len(P.ops)


_CACHE = {}


def kernel(**inp):
    import os
    NCORES = int(os.environ.get("KCORES", "8"))
    x_prompt = np.asarray(inp["x_prompt"], np.float32)
    x_sample = np.asarray(inp["x_sample"], np.float32)
    B, SEQ, _ = x_prompt.shape
    DB, DS, _ = x_sample.shape
    PAST = inp["cache_k"].shape[2]
    NPS, NSS = B // NCORES, DB // NCORES
    key = (NPS, SEQ, NSS, DS, PAST)
    if key not in _CACHE:
        _CACHE[key] = build(*key)
    nc, nops = _CACHE[key]
    consts = host_consts(NSS, DS)
    ck = np.asarray(inp["cache_k"], np.float32).reshape(2, DB, PAST, 512)
    cv = np.asarray(inp["cache_v"], np.float32).reshape(2, DB, PAST, 512)
    sh = np.asarray(inp["state_hgrn"], np.float32)
    sp_ = np.asarray(inp["state_pool"], np.float32)
    sc = np.asarray(inp["state_ffn_conv"], np.float32)
    shared = {k: np.ascontiguousarray(np.asarray(inp[k], np.float32)) for k in
              ("w_in_ab", "w_out_ab", "lb_logits", "hgrn_norm_g", "diff_norm_g", "rel_bias", "pool_w", "pool_scale",
               "ffn_w_up", "ffn_conv_w", "ffn_conv_b", "ffn_w_down", "ln1_g", "ln1_b", "ln2_g", "ln2_b")}
    shared["diff_lambda"] = np.ascontiguousarray(np.asarray(inp["diff_lambda"], np.float32).reshape(2, 256))
    shared.update(consts)
    in_maps = []
    for c in range(NCORES):
        ps, ss = slice(c * NPS, (c + 1) * NPS), slice(c * NSS, (c + 1) * NSS)
        d = dict(shared)
        d["x_p"] = np.ascontiguousarray(x_prompt[ps])
        d["x_s"] = np.ascontiguousarray(x_sample[ss])
        d["cache_k"] = np.ascontiguousarray(ck[:, ss])
        d["cache_v"] = np.ascontiguousarray(cv[:, ss])
        d["state_hgrn"] = np.ascontiguousarray(sh[:, ss])
        d["state_pool"] = np.ascontiguousarray(sp_[:, ss])
        d["state_conv"] = np.ascontiguousarray(sc[:, ss])
        in_maps.append(d)
    res = run_bass_kernel_spmd(nc, in_maps, core_ids=list(range(NCORES)))
    R = res.results

    def cat(name, axis):
        return np.concatenate([np.asarray(r[name], np.float32) for r in R], axis=axis)
    y_p = cat("y_p", 0)
    y_s = cat("y_s", 0)
    k_p = cat("k_p", 1).reshape(2, B, SEQ, 4, 128)
    v_p = cat("v_p", 1).reshape(2, B, SEQ, 4, 128)
    h_p = cat("h_p", 1)
    pl_p = cat("pl_p", 1)
    cv_p = cat("cv_p", 1)
    k_s = cat("k_s", 1).reshape(2, DB, DS, 4, 128)
    v_s = cat("v_s", 1).reshape(2, DB, DS, 4, 128)
    h_s = cat("h_s", 1)
    pl_s = cat("pl_s", 1)
    cv_s = cat("cv_s", 1)
    return (y_p, y_s, k_p, v_p, h_p, pl_p, cv_p, k_s, v_s, h_s, pl_s, cv_s)
```

```python
import math
from contextlib import ExitStack
import numpy as np
import concourse.bass as bass
import concourse.mybir as mybir
from concourse.bass_utils import run_bass_kernel_spmd

F32, BF16 = mybir.dt.float32, mybir.dt.bfloat16
AF = mybir.ActivationFunctionType
ALU = mybir.AluOpType

D = 1024
NCH = 8
DFF = 2816
NJ = 22
INC = 3584
DEPTH = 4
ALPHA = (2 * DEPTH) ** 0.25
INV_ALPHA = 1.0 / ALPHA
LN_EPS_S = 1e-5 / (ALPHA * ALPHA)
RMS_EPS = 1e-6
NREL = 383
POOL_W = (2, 4, 8, 16)


class Op:
    __slots__ = ("eng", "fn", "deps", "dma", "needed", "tok")


class Prog:
    NS = {"sp": 24, "pool": 12}

    def __init__(self):
        self.ops = []
        self.lw = {}
        self.rd = {}

    def add(self, eng, fn, reads=(), writes=(), dma=False):
        op = Op()
        op.eng, op.fn, op.dma, op.needed, op.tok = eng, fn, dma, False, None
        deps = {}
        def _exp(keys):
            out_ = []
            for k in keys:
                if isinstance(k, tuple) and len(k) == 2 and k[0] == "stg":
                    out_ += [("stg", k[1], 0), ("stg", k[1], 1)]
                else:
                    out_.append(k)
            return out_
        reads = _exp(reads) + ["EPOCH"]
        writes = _exp(writes)
        if eng != "pe":
            writes += [k for k in reads if isinstance(k, tuple) and k[0] == "ps" and k not in writes]
        if any(isinstance(k, tuple) and k[0] == "A" for k in reads + writes):
            reads.append("ARENA")
        for k in reads:
            w = self.lw.get(k)
            if w is not None:
                deps[id(w)] = (w, "raw")
        for k in writes:
            w = self.lw.get(k)
            if w is not None and id(w) not in deps:
                deps[id(w)] = (w, "waw")
            for r in self.rd.get(k, ()):
                if id(r) not in deps:
                    deps[id(r)] = (r, "war")
        out = []
        for d, kind in deps.values():
            if d is op:
                continue
            if (not d.dma) and (not dma) and d.eng == eng:
                if eng == "pe" or kind != "raw":
                    continue
            out.append(d)
        op.deps = out
        for k in reads:
            self.rd.setdefault(k, []).append(op)
        for k in writes:
            self.lw[k] = op
            self.rd[k] = []
        self.ops.append(op)
        return op

    def finalize(self, sems):
        cnt = {}
        dcount = {"sp": 0, "pool": 0}
        dops = {"sp": [], "pool": []}
        for op in self.ops:
            for d in op.deps:
                d.needed = True
        for op in self.ops:
            if op.dma:
                q = op.eng
                i = dcount[q]
                ns = self.NS[q]
                op.tok = (sems["dma_" + q][i % ns], 16 * (i // ns + 1))
                if i >= ns:
                    op.deps.append(dops[q][i - ns])
                dops[q].append(op)
                dcount[q] = i + 1
            elif op.needed:
                cnt[op.eng] = cnt.get(op.eng, 0) + 1
                op.tok = (sems[op.eng], cnt[op.eng])

    def run_stream(self, name, e):
        waited = {}
        for op in self.ops:
            if op.eng != name:
                continue
            req = {}
            for d in op.deps:
                s, v = d.tok
                k = id(s)
                if k not in req or req[k][1] < v:
                    req[k] = (s, v)
            for k, (s, v) in req.items():
                if waited.get(k, 0) < v:
                    e.wait_ge(s, v)
                    waited[k] = v
            ins = op.fn(e)
            if op.dma:
                ins.then_inc(op.tok[0], 16)
            elif op.needed:
                ins.then_inc(op.tok[0], 1)


def rel_bucket_np(rel):
    half, max_exact = 16, 8
    base = np.where(rel > 0, half, 0)
    n = np.abs(rel)
    nf = np.maximum(n, 1).astype(np.float32)
    large = max_exact + (np.log(nf / np.float32(max_exact)) / np.float32(math.log(128 / max_exact))
                         * (half - max_exact)).astype(np.int32)
    large = np.minimum(large, half - 1)
    return base + np.where(n < max_exact, n, large)


def host_consts(nss, ds):
    c = {}
    c["c_ident"] = np.eye(128, dtype=np.float32)
    s = np.arange(128)[:, None]
    t = np.arange(128)[None, :]
    c["c_maskp"] = ((s // 64 == t // 64) & (s <= t)).astype(np.float32)
    s = np.arange(ds)[:, None]
    t = np.arange(ds)[None, :]
    c["c_masks"] = (s <= t).astype(np.float32)
    c["c_resetp"] = (np.arange(512) % 64 != 0).astype(np.float32)
    c["c_resets"] = (np.arange(nss * ds) % ds != 0).astype(np.float32)
    k = np.arange(128)[:, None]
    q = np.arange(128)[None, :]
    c["c_vis0"] = (k // 64 <= q // 64).astype(np.float32)
    rel = 127 - np.arange(NREL)
    b = rel_bucket_np(rel.astype(np.int32))
    oh = np.zeros((32, NREL), np.float32)
    oh[b, np.arange(NREL)] = 1.0
    c["c_oht"] = oh
    invc = np.zeros((4, 16), np.float32)
    for g, w in enumerate(POOL_W):
        invc[g] = 1.0 / np.minimum(w, np.arange(16) + 1)
    c["c_invc"] = invc
    return c


def build(NPS, SEQ, NSS, DS, PAST):
    nc = bass.Bass("TRN2", target_bir_lowering=False)
    P = Prog()

    def din(name, shape):
        return nc.dram_tensor(name, list(shape), F32, kind="ExternalInput")

    def dout(name, shape):
        return nc.dram_tensor(name, list(shape), F32, kind="ExternalOutput")

    def dint(name, shape, dt):
        return nc.dram_tensor(name, list(shape), dt, kind="Internal")

    NTS = NSS * DS
    x_p = din("x_p", [NPS, SEQ, D]).ap()
    x_s = din("x_s", [NSS, DS, D]).ap()
    cache_k = din("cache_k", [2, NSS, PAST, 512]).ap()
    cache_v = din("cache_v", [2, NSS, PAST, 512]).ap()
    state_hgrn = din("state_hgrn", [2, NSS, 4, 128, 128]).ap()
    state_pool = din("state_pool", [2, NSS, 15, D]).ap()
    state_conv = din("state_conv", [4, NSS, 2, DFF]).ap()
    w_in = din("w_in_ab", [2, D, INC]).ap()
    w_out = din("w_out_ab", [2, D, D]).ap()
    lb_logits = din("lb_logits", [2, 512]).ap()
    hgrn_norm_g = din("hgrn_norm_g", [2, 512]).ap()
    diff_lambda = din("diff_lambda", [2, 256]).ap()
    diff_norm_g = din("diff_norm_g", [2, 128]).ap()
    rel_bias = din("rel_bias", [32, 4]).ap()
    pool_w = din("pool_w", [2, 4, 256, 256]).ap()
    pool_scale = din("pool_scale", [2, D]).ap()
    w_up = din("ffn_w_up", [4, D, 2 * DFF]).ap()
    conv_w = din("ffn_conv_w", [4, 3, DFF]).ap()
    conv_b = din("ffn_conv_b", [4, DFF]).ap()
    w_down = din("ffn_w_down", [4, DFF, D]).ap()
    ln1_g = din("ln1_g", [4, D]).ap()
    ln1_b = din("ln1_b", [4, D]).ap()
    ln2_g = din("ln2_g", [4, D]).ap()
    ln2_b = din("ln2_b", [4, D]).ap()
    c_ident = din("c_ident", [128, 128]).ap()
    c_maskp = din("c_maskp", [128, 128]).ap()
    c_masks = din("c_masks", [DS, DS]).ap()
    c_resetp = din("c_resetp", [512]).ap()
    c_resets = din("c_resets", [NTS]).ap()
    c_vis0 = din("c_vis0", [128, 128]).ap()
    c_oht = din("c_oht", [32, NREL]).ap()
    c_invc = din("c_invc", [4, 16]).ap()
    y_p = dout("y_p", [NPS, SEQ, D]).ap()
    y_s = dout("y_s", [NSS, DS, D]).ap()
    k_p = dout("k_p", [2, NPS, SEQ, 512]).ap()
    v_p = dout("v_p", [2, NPS, SEQ, 512]).ap()
    h_p = dout("h_p", [2, NPS, 4, 128, 128]).ap()
    pl_p = dout("pl_p", [2, NPS, 15, D]).ap()
    cv_p = dout("cv_p", [4, NPS, 2, DFF]).ap()
    k_s = dout("k_s", [2, NSS, DS, 512]).ap()
    v_s = dout("v_s", [2, NSS, DS, 512]).ap()
    h_s = dout("h_s", [2, NSS, 4, 128, 128]).ap()
    pl_s = dout("pl_s", [2, NSS, 15, D]).ap()
    cv_s = dout("cv_s", [4, NSS, 2, DFF]).ap()
    win_bf = dint("win_bf", [2, 14, 128, 8, 256], BF16).ap()
    wout_bf = dint("wout_bf", [2, 4, 128, 8, 256], BF16).ap()
    poolw_bf = dint("poolw_bf", [2, 128, 2, 4, 256], BF16).ap()
    wup_bf = dint("wup_bf", [4, NJ, 128, 8, 256], BF16).ap()
    wdn_bf = dint("wdn_bf", [4, 4, 128, NJ, 256], BF16).ap()
    evm_t = dint("evm", [4, 128, NREL], F32)
    evm = evm_t.ap()

    es = ExitStack()
    ARENA_BYTES = 207 * 1024
    arena = es.enter_context(nc.sbuf_tensor("arena", [128, ARENA_BYTES // 4], F32))
    psb = [es.enter_context(nc.psum_tensor(f"ps{i}", [128, 512], F32)) for i in range(8)]
    sems = {}
    for n in ("pe", "act", "dve", "pool", "sp"):
        sems[n] = es.enter_context(nc.semaphore("sem_" + n))
    for q in ("sp", "pool"):
        sems["dma_" + q] = [es.enter_context(nc.semaphore(f"dsem_{q}{i}")) for i in range(Prog.NS[q])]

    top = [0]

    def carve(nbytes):
        o = top[0]
        top[0] = (o + nbytes + 63) // 64 * 64
        assert top[0] <= ARENA_BYTES, ("arena overflow", top[0])
        return o

    def view(off, dt, shape, parts=128):
        n = int(np.prod(shape))
        esz = 4 if dt == F32 else 2
        nb = n * esz
        assert off % 4 == 0
        a = arena[0:parts, off // 4: off // 4 + (nb + 3) // 4]
        if dt == BF16:
            a = a.bitcast(BF16)
            if a.shape[1] != n:
                a = a[:, 0:n]
        if len(shape) == 2:
            a = a.rearrange("p (a b) -> p a b", a=shape[0])
        elif len(shape) == 3:
            a = a.rearrange("p (a b c) -> p a b c", a=shape[0], b=shape[1])
        return a

    o_ident = carve(512); ident = view(o_ident, F32, (128,))
    o_identb = carve(256); identb = view(o_identb, BF16, (128,))
    o_onesb = carve(256); onesb = view(o_onesb, BF16, (128,))
    o_maskp = carve(256); maskp = view(o_maskp, BF16, (128,))
    o_masks = carve(64); masks = view(o_masks, BF16, (DS,))
    o_resetp = carve(1024); resetp = view(o_resetp, BF16, (512,))
    o_resets = carve(2 * NTS + 4); resets = view(o_resets, BF16, (NTS,))
    o_Ep = carve(2 * 4 * 256); Ep = view(o_Ep, BF16, (2, 4, 128))
    o_Esh = carve(4 * DS * 2); Esh = view(o_Esh, BF16, (4, DS))
    o_Esn = carve(4 * DS * 2); Esn = view(o_Esn, BF16, (4, DS))
    o_lng = carve(4 * 2 * 8 * 4); lng = view(o_lng, F32, (4, 2, 8))
    o_lnb = carve(4 * 2 * 8 * 4); lnb = view(o_lnb, F32, (4, 2, 8))
    o_cw = carve(4 * 3 * NJ * 4); cw = view(o_cw, F32, (4, 3, NJ))
    o_cb = carve(4 * NJ * 4); cbias = view(o_cb, F32, (4, NJ))
    o_lb = carve(2 * 4 * 4); lbv = view(o_lb, F32, (2, 4))
    o_oml = carve(2 * 4 * 4); omlv = view(o_oml, F32, (2, 4))
    o_noml = carve(2 * 4 * 4); nomlv = view(o_noml, F32, (2, 4))
    o_hg = carve(2 * 4 * 4); hgv = view(o_hg, F32, (2, 4))
    o_dg = carve(2 * 4); dgv = view(o_dg, F32, (2,))
    o_nl = carve(2 * 4); neglam = view(o_nl, F32, (2,))
    o_psc = carve(2 * 8 * 4); pscv = view(o_psc, F32, (2, 8))
    o_invc = carve(4 * 16 * 4); invc = view(o_invc, F32, (4, 16))
    o_epsc = carve(8); epsc = view(o_epsc, F32, (2,))
    o_tmpc = carve(2304); tmpc = view(o_tmpc, F32, (576,))
    NSM = max(1, NSS)
    Sst_box = [None]
    o_Sb = carve(256); Sbf = view(o_Sb, BF16, (128,))
    o_atail = carve(NJ * NSM * 2 * 4); atail = view(o_atail, F32, (NJ, NSM, 2))
    o_small = carve(12 * 8 * 4); smallv = view(o_small, F32, (12, 8))
    o_hout = carve(4 * 512 * 2); hout = view(o_hout, BF16, (4, 512))
    NF, NB = 7, 8
    FW = 528
    o_sf = [carve(FW * 4) for _ in range(NF)]
    o_sb = [carve(512 * 2) for _ in range(NB)]
    o_stg = [carve(4096) for _ in range(2)]
    NR = 3
    o_ring = [carve(4096) for _ in range(NR)]
    base_top = top[0]
    print("base_top", base_top)

    sf_i, sb_i, stg_i, ring_i = [0], [0], [0], [0]

    def sF():
        i = sf_i[0] % NF; sf_i[0] += 1
        return view(o_sf[i], F32, (FW,)), ("sf", i)

    def sB():
        i = sb_i[0] % NB; sb_i[0] += 1
        return view(o_sb[i], BF16, (512,)), ("sb", i)

    def FX(i):
        return view(o_sf[i], F32, (FW,)), ("sf", i)

    def BX(i):
        return view(o_sb[i], BF16, (512,)), ("sb", i)

    def sFr(lo, hi):
        n = hi - lo
        i = lo + sf_i[0] % n; sf_i[0] += 1
        return view(o_sf[i], F32, (FW,)), ("sf", i)

    def sStage():
        i = stg_i[0] % 2; stg_i[0] += 1
        return view(o_stg[i], F32, (1024,)), ("stg", i)

    sh_i = [0]

    def sHalf():
        i = sh_i[0] % 4; sh_i[0] += 1
        return view(o_stg[i // 2] + (i % 2) * 2048, F32, (512,)), ("stg", i // 2, i % 2)

    def sRing():
        i = ring_i[0] % NR; ring_i[0] += 1
        return view(o_ring[i], BF16, (8, 256)), ("ring", i)

    ps_i = [0]

    def PS(lo=0, hi=8):
        n = hi - lo
        i = lo + ps_i[0] % n; ps_i[0] += 1
        return psb[i], ("ps", i)

    def act(out, in_, func, reads, writes, bias=None, scale=None, accum=None):
        kw = {}
        if bias is not None: kw["bias"] = bias
        if scale is not None: kw["scale"] = scale
        if accum is not None: kw["accum_out"] = accum
        P.add("act", lambda e: e.activation(out=out, in_=in_, func=func, **kw), reads, writes)

    def tt(out, in0, in1, op, reads, writes, eng="dve"):
        P.add(eng, lambda e: e.tensor_tensor(out=out, in0=in0, in1=in1, op=op), reads, writes)

    def ts(out, in0, s1, s2, op0, op1, reads, writes, eng="dve"):
        if op1 is None:
            P.add(eng, lambda e: e.tensor_scalar(out=out, in0=in0, scalar1=s1, scalar2=None, op0=op0), reads, writes)
        else:
            P.add(eng, lambda e: e.tensor_scalar(out=out, in0=in0, scalar1=s1, scalar2=s2, op0=op0, op1=op1), reads, writes)

    def stt(out, in0, scalar, in1, op0, op1, reads, writes):
        P.add("dve", lambda e: e.scalar_tensor_tensor(out=out, in0=in0, scalar=scalar, in1=in1, op0=op0, op1=op1), reads, writes)

    def cp(out, in_, reads, writes, eng="dve"):
        P.add(eng, lambda e: e.tensor_copy(out=out, in_=in_), reads, writes)

    def mm(out, lhsT, rhs, start, stop, reads, writes):
        P.add("pe", lambda e: e.matmul(out, lhsT=lhsT, rhs=rhs, start=start, stop=stop), reads, writes)

    def tr(out, in_, idn, reads, writes):
        P.add("pe", lambda e: e.transpose(out, in_, idn), reads, writes)

    def dma(q, out, in_, reads, writes, slow=False):
        if slow:
            P.add(q, lambda e: e.dma_start(out=out, in_=in_, allow_slow_non_contiguous=True), reads, writes, dma=True)
        else:
            P.add(q, lambda e: e.dma_start(out=out, in_=in_), reads, writes, dma=True)

    def memset(eng, ap, val, writes):
        P.add(eng, lambda e: e.memset(ap, val), (), writes)

    def wkeys(nm, i):
        n = {"win": 8, "wout": 8, "poolw": 8, "wup": 16, "wdn": NJ}[nm]
        return [(nm, i, k) for k in range(n)]

    def rsqrt(out, in_, which, reads, writes):
        act(out, in_, AF.Sqrt, list(reads) + ["epsc"], writes, bias=epsc[:, which:which + 1])
        P.add("dve", lambda e: e.reciprocal(out=out, in_=out), writes, writes)

    def barrier():
        P.add("pool", lambda e: e.memset(tmpc[0:1, 0:1], 0.0), (), ["EPOCH", "ARENA"])

    def arena_barrier():
        P.add("pool", lambda e: e.memset(tmpc[0:1, 0:1], 0.0), (), ["ARENA"])

    def cast_win(m):
        for kc in range(8):
            src = w_in[m, kc * 128:(kc + 1) * 128, :].rearrange("p (g c) -> p g c", g=14)
            dst = win_bf[m, :, :, kc, :].rearrange("g p c -> p g c")
            dma("pool", dst, src, (), [("win", m, kc)])

    def cast_wout(m):
        for kc in range(8):
            dma("pool", wout_bf[m, :, :, kc, :].rearrange("q p c -> p q c"),
                w_out[m, kc * 128:(kc + 1) * 128, :].rearrange("p (q c) -> p q c", q=4), (), [("wout", m, kc)])

    def cast_poolw(p):
        for g in range(4):
            for kc in range(2):
                dma("pool", poolw_bf[p, :, kc, g, :], pool_w[p, g, kc * 128:(kc + 1) * 128, :], (), [("poolw", p, g * 2 + kc)])

    def cast_wup(l):
        for kc in range(8):
            for ab in range(2):
                src = w_up[l, kc * 128:(kc + 1) * 128, ab * DFF:(ab + 1) * DFF].rearrange("p (j c) -> p j c", j=NJ)
                dst = wup_bf[l, :, :, kc, ab * 128:(ab + 1) * 128].rearrange("j p c -> p j c")
                dma("pool", dst, src, (), [("wup", l, kc * 2 + ab)])

    def cast_wdn(l):
        for j in range(NJ):
            src = w_down[l, j * 128:(j + 1) * 128, :].rearrange("p (h c) -> p h c", h=4)
            dst = wdn_bf[l, :, :, j, :].rearrange("h p c -> p h c")
            dma("pool", dst, src, (), [("wdn", l, j)])

    def setup():
        dma("sp", ident, c_ident, (), ["ident"])
        t1, k1 = sF()
        dma("sp", t1[:, 0:128], c_maskp, (), [k1])
        cp(maskp, t1[:, 0:128], [k1], ["maskp"])
        cp(identb, ident, ["ident"], ["identb"])
        memset("dve", onesb, 1.0, ["onesb"])
        memset("dve", epsc[:, 0:1], LN_EPS_S, ["epsc"])
        memset("dve", epsc[:, 1:2], 128.0 * RMS_EPS, ["epsc"])
        t2, k2 = sF()
        dma("sp", t2[0:DS, 0:DS], c_masks, (), [k2])
        cp(masks[0:DS, :], t2[0:DS, 0:DS], [k2], ["masks"])
        t3, k3 = sF()
        dma("sp", t3[:, 0:512], c_resetp.partition_broadcast(128), (), [k3])
        cp(resetp, t3[:, 0:512], [k3], ["resetp"])
        t4, k4 = sF()
        dma("sp", t4[:, 0:NTS], c_resets.partition_broadcast(128), (), [k4])
        cp(resets, t4[:, 0:NTS], [k4], ["resets"])
        dma("sp", invc, bass.AP(c_invc.tensor, 0, [[0, 128], [16, 4], [1, 16]]), (), ["invc"])
        for l in range(4):
            for w, (g_, b_) in enumerate(((ln1_g, ln1_b), (ln2_g, ln2_b))):
                dma("sp", lng[:, l, w, :], g_[l].rearrange("(c p) -> p c", p=128), (), ["lnp"], slow=True)
                dma("sp", lnb[:, l, w, :], b_[l].rearrange("(c p) -> p c", p=128), (), ["lnp"], slow=True)
            for r in range(3):
                dma("sp", cw[:, l, r, :], conv_w[l, r].rearrange("(j p) -> p j", p=128), (), ["cw"], slow=True)
            dma("sp", cbias[:, l, :], conv_b[l].rearrange("(j p) -> p j", p=128), (), ["cw"], slow=True)
        for m in range(2):
            dma("sp", hgv[:, m, :], hgrn_norm_g[m].rearrange("(h p) -> p h", p=128), (), ["hg"], slow=True)
            dma("sp", dgv[:, m:m + 1], diff_norm_g[m].rearrange("(p o) -> p o", o=1), (), ["dg"], slow=True)
            dma("sp", pscv[:, m, :], pool_scale[m].rearrange("(c p) -> p c", p=128), (), ["psc"], slow=True)
        ts(hgv, hgv, math.sqrt(128.0), None, ALU.mult, None, ["hg"], ["hg"])
        for m in range(2):
            lam_init = 0.8 - 0.6 * math.exp(-0.3 * (2 * m))
            ts(dgv[:, m:m + 1], dgv[:, m:m + 1], math.sqrt(128.0) * (1.0 - lam_init), None, ALU.mult, None, ["dg"], ["dg"])
        ts(pscv, pscv, INV_ALPHA, None, ALU.mult, None, ["psc"], ["psc"])
        l0 = tmpc[:, 0:4]; l1 = tmpc[:, 4:8]
        dma("sp", l0, lb_logits[0].rearrange("(h p) -> p h", p=128), (), ["tmpc"], slow=True)
        dma("sp", l1, lb_logits[1].rearrange("(h p) -> p h", p=128), (), ["tmpc"], slow=True)
        tt(tmpc[:, 8:12], l1, l0, ALU.subtract, ["tmpc"], ["tmpc2"])
        memset("dve", lbv[:, 0, :], 0.0, ["lb"])
        act(lbv[:, 1, :], tmpc[:, 8:12], AF.Sigmoid, ["tmpc2", "lb"], ["lb"])
        ts(omlv, lbv, -1.0, 1.0, ALU.mult, ALU.add, ["lb"], ["oml"])
        ts(nomlv, omlv, -1.0, None, ALU.mult, None, ["oml"], ["noml"])
        for m in range(2):
            lam_init = 0.8 - 0.6 * math.exp(-0.3 * (2 * m))
            lp = tmpc[:, 16:16 + 256]
            dma("sp", lp, diff_lambda[m].partition_broadcast(128), ["tmpcl"], ["tmpcl"])
            pr = tmpc[:, 300:300 + 128]
            tt(pr.rearrange("p (a b) -> p a b", a=2), lp.rearrange("p (a t b) -> p a t b", a=2, t=2)[:, :, 0, :],
               lp.rearrange("p (a t b) -> p a t b", a=2, t=2)[:, :, 1, :], ALU.mult, ["tmpcl"], ["tmpcp"])
            sm_ = tmpc[:, 440:442]
            P.add("dve", lambda e, o=sm_, i=pr: e.tensor_reduce(out=o, in_=i.rearrange("p (a b) -> p a b", a=2),
                                                                 axis=mybir.AxisListType.X, op=ALU.add), ["tmpcp"], ["tmpcs"])
            ex = tmpc[:, 444:446]
            act(ex, sm_, AF.Exp, ["tmpcs"], ["tmpce"])
            tt(neglam[:, m:m + 1], ex[:, 1:2], ex[:, 0:1], ALU.subtract, ["tmpce"], ["nl"])
            ts(neglam[:, m:m + 1], neglam[:, m:m + 1], -lam_init, None, ALU.add, None, ["nl"], ["nl", "tmpcl"])
        rb = tmpc[0:32, 500:504]
        oht, ko = sStage()
        dma("sp", rb, rel_bias, (), ["rb"])
        dma("sp", oht[0:32, 0:NREL], c_oht, (), [ko])
        for h in range(4):
            rbb, kr = sF()
            cp(rbb[0:32, 0:128], rb[:, h:h + 1].to_broadcast([32, 128]), ["rb"], [kr])
            pt, kp = PS()
            mm(pt[:, 0:NREL], rbb[0:32, 0:128], oht[0:32, 0:NREL], True, True, [kr, ko], [kp])
            ngc = tmpc[:, 510 + h:511 + h]
            ts(ngc, pt[:, NREL - 1:NREL], -1.0, None, ALU.mult, None, [kp], [("ngc", h)])
            ev, ke = sF()
            act(ev[:, 0:NREL], pt[:, 0:NREL], AF.Exp, [kp, ("ngc", h)], [ke], bias=ngc)
            dma("sp", evm[h], ev[:, 0:NREL], [ke], [("evm", h)])
        vis, kv = sStage()
        dma("sp", vis[:, 0:128], c_vis0, (), [kv])
        for h in range(4):
            for di, d in enumerate((0, -128)):
                et, ke = sF()
                src = bass.AP(evm_t, h * 128 * NREL + (127 - d), [[NREL - 1, 128], [1, 128]])
                dma("sp", et[:, 0:128], src, [("evm", h)], [ke])
                if d == 0:
                    tt(Ep[:, di, h, :], et[:, 0:128], vis[:, 0:128], ALU.mult, [ke, kv], ["Ep"])
                else:
                    cp(Ep[:, di, h, :], et[:, 0:128], [ke], ["Ep"])
            et, ke = sF()
            src = bass.AP(evm_t, h * 128 * NREL + (127 + 128), [[NREL - 1, 128], [1, DS]])
            dma("sp", et[:, 0:DS], src, [("evm", h)], [ke])
            cp(Esh[:, h, :], et[:, 0:DS], [ke], ["Es"])
            et, ke = sF()
            src = bass.AP(evm_t, h * 128 * NREL + 127, [[NREL - 1, DS], [1, DS]])
            dma("sp", et[0:DS, 0:DS], src, [("evm", h)], [ke])
            cp(Esn[0:DS, h, :], et[0:DS, 0:DS], [ke], ["Es"])

    class Grp:
        pass

    def make_group(kind, seq_ids):
        G = Grp()
        G.pending = []
        G.kind = kind
        G.seqs = list(seq_ids)
        G.nseq = len(G.seqs)
        G.T = SEQ if kind == "p" else DS
        G.P = 0 if kind == "p" else PAST
        G.NT = G.nseq * G.T
        G.W = min(512, G.NT)
        G.nt = G.NT // G.W
        G.L = min(64, G.T)
        G.blocks = []
        for ti in range(G.nt):
            bl = []
            if kind == "p":
                for j in range(G.W // 128):
                    bl.append((128 * j, 128, 0, ti * G.W + 128 * j))
            else:
                for s in range(G.nseq):
                    bl.append((G.T * s, G.T, s, 0))
            G.blocks.append(bl)
        G.ns_t = 1 if kind == "p" else G.nseq
        G.Ts = G.W // G.ns_t
        top[0] = base_top
        G.o_xres = carve(8 * G.NT * 4)
        G.o_xbf = carve(8 * G.NT * 2)
        G.o_S = carve(G.nseq * 4 * 128 * 4)
        Sst_box[0] = view(G.o_S, F32, (G.nseq, 4, 128))
        KT = G.P + G.T
        G.KT = KT
        G.nkh = G.P // 128
        if kind == "p":
            a0 = top[0]
            G.o_qT = carve(4 * G.NT * 2); G.o_kT = carve(4 * KT * 2); G.o_V = carve((G.T // 128) * 512 * 2)
            G.o_poolw = G.o_qT
            a1 = top[0]
            top[0] = a0
            G.o_gT = carve(NJ * G.W * 2); G.o_wdn = carve(2 * NJ * 256 * 2)
            top[0] = max(top[0], a1)
            G.alias = True
        else:
            G.o_qT = carve(4 * G.NT * 2); G.o_kT = carve(4 * KT * 2); G.o_V = carve((G.nkh + 1) * 512 * 2)
            G.o_poolw = carve(2 * 4 * 256 * 2)
            G.o_gT = carve(NJ * G.W * 2); G.o_wdn = carve(2 * NJ * 256 * 2)
            G.alias = False
        print("group", kind, "arena top", top[0], "of", ARENA_BYTES)
        G.xres = view(G.o_xres, F32, (8, G.NT))
        G.xbf = view(G.o_xbf, BF16, (8, G.NT))
        G.qT = view(G.o_qT, BF16, (4, G.NT))
        G.kT = view(G.o_kT, BF16, (4, KT))
        G.V = view(G.o_V, BF16, ((G.T // 128) if kind == "p" else (G.nkh + 1), 512))
        G.poolw = view(G.o_poolw, BF16, (2, 4, 256))
        G.gT = view(G.o_gT, BF16, (NJ, G.W))
        G.wdn = view(G.o_wdn, BF16, (2, NJ, 256))
        if kind == "p":
            G.x_in, G.y_out, G.k_out, G.v_out, G.h_out, G.pl_out, G.cv_out = x_p, y_p, k_p, v_p, h_p, pl_p, cv_p
            G.maskH, G.reset = maskp, resetp
        else:
            G.x_in, G.y_out, G.k_out, G.v_out, G.h_out, G.pl_out, G.cv_out = x_s, y_s, k_s, v_s, h_s, pl_s, cv_s
            G.maskH, G.reset = masks, resets
        return G

    def kx(oc, ti): return ("xres", oc, ti)
    def kb(oc, ti): return ("A", "xbf", oc, ti)

    def load_x(G):
        import os
        KSUB = int(os.environ.get("KSUB", "9"))
        for ti in range(G.nt):
            c0 = ti * G.W
            banks = [PS() for _ in range(8)]
            for (co, nb, s, t0) in G.blocks[ti]:
                st, ks_ = sStage()
                dma("sp", st[0:nb, :], G.x_in[G.seqs[s], t0:t0 + nb, :], (), [ks_])
                for oc in range(8 if KSUB >= 1 else 0):
                    pb_, kp = banks[oc]
                    tr(pb_[:, co:co + nb], st[0:nb, oc * 128:(oc + 1) * 128], ident[0:nb, 0:nb], [ks_, "ident"], [kp])
            for oc in range(8 if KSUB >= 2 else 0):
                pb_, kp = banks[oc]
                act(G.xres[:, oc, c0:c0 + G.W], pb_[:, 0:G.W], AF.Copy, [kp], [kx(oc, ti)])
                if KSUB == 3:
                    cp(G.xbf[:, oc, c0:c0 + G.W], G.xres[:, oc, c0:c0 + G.W], [kx(oc, ti)], [kb(oc, ti)])
                elif KSUB == 5:
                    cp(G.xbf[:, oc, c0:c0 + G.W], pb_[:, 0:G.W], [kp], [("xbftest", oc, ti)])
                elif KSUB == 6:
                    cp(G.xbf[:, oc, c0:c0 + G.W], pb_[:, 0:G.W], [kp, kx(oc, ti)], [kb(oc, ti)])
                elif KSUB == 7:
                    t_, k_ = sF()
                    cp(t_[:, 0:G.W], pb_[:, 0:G.W], [kp], [k_])
                elif KSUB > 7:
                    cp(G.xbf[:, oc, c0:c0 + G.W], pb_[:, 0:G.W], [kp], [kb(oc, ti)])

    def store_y(G):
        import os
        KSUB = int(os.environ.get("KSUB", "9"))
        if KSUB < 4:
            return
        flush_pending(G)
        for ti in range(G.nt):
            c0 = ti * G.W
            for (co, nb, s, t0) in G.blocks[ti]:
                st, ks_ = sStage()
                for half in range(2):
                    pb_, kp = PS()
                    for o4 in range(4):
                        oc = half * 4 + o4
                        tr(pb_[0:nb, o4 * 128:(o4 + 1) * 128], G.xres[:, oc, c0 + co:c0 + co + nb], ident, [kx(oc, ti), "ident"], [kp])
                    if half == 0:
                        act(st[0:nb, 0:512], pb_[0:nb, :], AF.Copy, [kp], [ks_])
                    else:
                        cp(st[0:nb, 512:1024], pb_[0:nb, :], [kp], [ks_])
                dma("pool", G.y_out[G.seqs[s], t0:t0 + nb, :], st[0:nb, :], [ks_], [("yout",)])

    def pop_pending(G, n=1):
        for _ in range(n):
            if G.pending:
                G.pending.pop(0)[1]()

    def flush_pending(G, tile=None):
        keep = []
        for (t_id, fn) in G.pending:
            if tile is None or t_id == tile:
                fn()
            else:
                keep.append((t_id, fn))
        G.pending = keep

    def ln_tile(G, l, which, ti, ychunk, yscale=None):
        W = G.W
        c0 = ti * W
        s1, k1 = psb[6], ("ps", 6)
        s2, k2 = psb[7], ("ps", 7)
        prev_stats = None
        for oc in range(8):
            yp, ky = ychunk(oc)
            if prev_stats is not None:
                prev_stats()
            xr = G.xres[:, oc, c0:c0 + W]
            sc = INV_ALPHA if yscale is None else yscale(oc)
            stt(xr, yp[:, 0:W], sc, xr, ALU.mult, ALU.add, [ky, kx(oc, ti)] + ([] if yscale is None else ["psc"]), [kx(oc, ti)])
            rb_, kr = sB()
            act(rb_[:, 0:W], xr, AF.Copy, [kx(oc, ti)], [kr])
            sq_, kq = sB()
            act(sq_[:, 0:W], xr, AF.Square, [kx(oc, ti)], [kq])
            def prev_stats(oc=oc, rb_=rb_, sq_=sq_, kr=kr, kq=kq):
                mm(s1[:, 0:W], onesb, rb_[:, 0:W], oc == 0, oc == 7, [kr, "onesb"], [k1])
                mm(s2[:, 0:W], onesb, sq_[:, 0:W], oc == 0, oc == 7, [kq, "onesb"], [k2])
            pop_pending(G, 1)
        prev_stats()
        flush_pending(G)
        mean, km = FX(0)
        ts(mean[:, 0:W], s1[:, 0:W], 1.0 / D, None, ALU.mult, None, [k1], [km])
        msq, kq2 = FX(1)
        tt(msq[:, 0:W], mean[:, 0:W], mean[:, 0:W], ALU.mult, [km], [kq2])
        rstd, krs = msq, kq2
        stt(rstd[:, 0:W], s2[:, 0:W], 1.0 / D, msq[:, 0:W], ALU.mult, ALU.subtract, [k2, kq2], [krs])
        rsqrt(rstd[:, 0:W], rstd[:, 0:W], 0, [krs], [krs])
        for oc in range(8):
            def partB(oc=oc, mean=mean, rstd=rstd, km=km, krs=krs):
                xr = G.xres[:, oc, c0:c0 + W]
                t_, kt = sFr(2, NF)
                tt(t_[:, 0:W], xr, mean[:, 0:W], ALU.subtract, [kx(oc, ti), km], [kt])
                tt(t_[:, 0:W], t_[:, 0:W], rstd[:, 0:W], ALU.mult, [kt, krs], [kt])
                act(xr, t_[:, 0:W], AF.Identity, [kt, "lnp"], [kx(oc, ti)], bias=lnb[:, l, which, oc:oc + 1], scale=lng[:, l, which, oc:oc + 1])
                act(G.xbf[:, oc, c0:c0 + W], t_[:, 0:W], AF.Identity, [kt, "lnp"], [kb(oc, ti)], bias=lnb[:, l, which, oc:oc + 1], scale=lng[:, l, which, oc:oc + 1])
            G.pending.append((ti, partB))

    def ffn(G, l):
        W, ns, Ts = G.W, G.ns_t, G.Ts
        if G.kind == "s":
            for s in range(G.nseq):
                st, ks_ = sStage()
                for r in range(2):
                    dma("sp", st[r * NJ:(r + 1) * NJ, 0:128], state_conv[l, G.seqs[s], r].rearrange("(j p) -> j p", p=128), (), [ks_])
                pt, kp = PS(4, 6)
                tr(pt[:, 0:2 * NJ], st[0:2 * NJ, 0:128], ident[0:2 * NJ, 0:2 * NJ], [ks_, "ident"], [kp])
                cp(atail[:, :, s, :].rearrange("p j r -> p r j"), pt[:, 0:2 * NJ].rearrange("p (r j) -> p r j", r=2), [kp], ["atail"])
        else:
            memset("pool", atail, 0.0, ["atail"])
        for ti in range(G.nt):
            c0 = ti * W
            flush_pending(G, tile=ti)
            for j in range(NJ):
                wr, kw = sRing()
                dma("sp", wr, wup_bf[l, j], wkeys("wup", l), [kw])
                pa, ka = PS(0, 8)
                pb_, kpb = PS(0, 8)
                for kc in range(8):
                    mm(pa[:, 0:W], wr[:, kc, 0:128], G.xbf[:, kc, c0:c0 + W], kc == 0, kc == 7, [kw, kb(kc, ti)], [ka])
                for kc in range(8):
                    mm(pb_[:, 0:W], wr[:, kc, 128:256], G.xbf[:, kc, c0:c0 + W], kc == 0, kc == 7, [kw, kb(kc, ti)], [kpb])
                a_sb, kas = sFr(2, NF)
                a3 = a_sb[:, 0:ns * (Ts + 2)].rearrange("p (s t) -> p s t", s=ns)
                cp(a3[:, :, 0:2], atail[:, j, 0:ns, :], ["atail"], [kas], eng="pool")
                act(a3[:, :, 2:2 + Ts], pa[:, 0:W].rearrange("p (s t) -> p s t", s=ns), AF.Copy, [ka, kas], [kas])
                t0_, k0 = sHalf()
                t03 = t0_[:, 0:W].rearrange("p (s t) -> p s t", s=ns)
                act(t0_[:, 0:W], pa[:, 0:W], AF.Identity, [ka, "cw"], [k0], bias=cbias[:, l, j:j + 1], scale=cw[:, l, 2, j:j + 1])
                stt(t03, a3[:, :, 1:1 + Ts], cw[:, l, 1, j:j + 1], t03, ALU.mult, ALU.add, [kas, k0, "cw"], [k0])
                stt(t03, a3[:, :, 0:Ts], cw[:, l, 0, j:j + 1], t03, ALU.mult, ALU.add, [kas, k0, "cw"], [k0])
                cp(atail[:, j, 0:ns, :], a3[:, :, Ts:Ts + 2], [kas, "atail"], ["atail"], eng="pool")
                act(t0_[:, 0:W], t0_[:, 0:W], AF.Silu, [k0], [k0])
                tt(G.gT[:, j, 0:W], t0_[:, 0:W], pb_[:, 0:W], ALU.mult, [k0, kpb], [("A", "gT", j)])
                pop_pending(G, 1)
            def ychunk(oc, l=l):
                py, kpy = PS(4, 6)
                qd, o2 = oc // 2, oc % 2
                if o2 == 0:
                    dma("sp", G.wdn[:, qd % 2], wdn_bf[l, qd], wkeys("wdn", l), [("A", "wdn", qd % 2)])
                for j in range(NJ):
                    mm(py[:, 0:W], G.wdn[:, qd % 2, j, o2 * 128:(o2 + 1) * 128], G.gT[:, j, 0:W], j == 0, j == NJ - 1,
                       [("A", "wdn", qd % 2), ("A", "gT", j)], [kpy])
                return py, kpy
            ln_tile(G, l, 1, ti, ychunk)
        for s in range(G.nseq):
            t44, k44 = sFr(3, NF)
            cp(t44[:, 0:2 * NJ].rearrange("p (r j) -> p r j", r=2), atail[:, :, s, :].rearrange("p j r -> p r j"), ["atail"], [k44])
            pt, kp = PS(4, 6)
            tr(pt[0:2 * NJ, 0:128], t44[:, 0:2 * NJ], ident, [k44, "ident"], [kp])
            st, ks_ = sStage()
            act(st[0:2 * NJ, 0:128], pt[0:2 * NJ, 0:128], AF.Copy, [kp], [ks_])
            for r in range(2):
                dma("pool", G.cv_out[l, G.seqs[s], r].rearrange("(j p) -> j p", p=128), st[r * NJ:(r + 1) * NJ, 0:128], [ks_], [("cvout",)])

    def mixer_pool(G, l):
        p = l // 2
        W = G.W
        flush_pending(G)
        for s in range(G.nseq):
            cend = (s + 1) * G.T
            tl = (cend - 1) // W
            st, ks_ = sStage()
            for half in range(2):
                pb_, kp = PS(4, 6)
                for o4 in range(4):
                    oc = half * 4 + o4
                    tr(pb_[0:15, o4 * 128:(o4 + 1) * 128], G.xres[:, oc, cend - 15:cend], ident, [kx(oc, tl), "ident"], [kp])
                if half == 0:
                    act(st[0:15, 0:512], pb_[0:15, :], AF.Copy, [kp], [ks_])
                else:
                    cp(st[0:15, 512:1024], pb_[0:15, :], [kp], [ks_])
            dma("pool", G.pl_out[p, G.seqs[s]], st[0:15, :], [ks_], [("plout",)])
        dma("sp", G.poolw, poolw_bf[p], wkeys("poolw", p), [("A", "poolw")])
        for ti in range(G.nt):
            c0 = ti * W
            for s in range(G.ns_t):
                sc0 = c0 + s * G.Ts
                Ts = G.Ts
                first = (G.kind == "s") or (ti == 0)
                if G.kind == "s":
                    sth, ksh = sStage()
                    dma("sp", sth[0:15, :], state_pool[p, G.seqs[s]], (), [ksh])
                    phh, kph = PS(6, 8)
                    for oc in range(8):
                        tr(phh[:, oc * 16:oc * 16 + 15], sth[0:15, oc * 128:(oc + 1) * 128], ident[0:15, 0:15], [ksh, "ident"], [kph])
                for oc in range(8):
                    g = oc // 2
                    w = POOL_W[g]
                    if first:
                        src, ksrc = sF()
                        if G.kind == "s":
                            cp(src[:, 1:16], phh[:, oc * 16:oc * 16 + 15], [kph], [ksrc])
                        else:
                            memset("pool", src[:, 0:16], 0.0, [ksrc])
                        cp(src[:, 16:16 + Ts], G.xres[:, oc, sc0:sc0 + Ts], [kx(oc, ti)], [ksrc], eng="pool")
                        cur = src[:, 16 - 15:16 + Ts]
                        kcur = [ksrc]
                    else:
                        cur = G.xres[:, oc, sc0 - 15:sc0 + Ts]
                        kcur = [kx(oc, ti), kx(oc, ti - 1)]
                    xt = cur[:, 15:15 + Ts]
                    lo = 15
                    step = 1
                    while step < w:
                        nxt, kn = sF()
                        n = lo - step + Ts
                        tt(nxt[:, 0:n], cur[:, step:step + n], cur[:, 0:n], ALU.add, kcur, [kn])
                        cur, kcur, lo = nxt[:, 0:n], [kn], lo - step
                        step *= 2
                    sw = cur[:, lo:lo + Ts]
                    outp = G.xbf[:, oc, sc0:sc0 + Ts]
                    stt(outp, sw, 1.0 / w, xt, ALU.mult, ALU.subtract, kcur + [kx(oc, ti)] + ([ksrc] if first else []), [kb(oc, ti)])
                    if G.kind == "p" and ti == 0:
                        fx, kf = sF()
                        tt(fx[:, 0:16], sw[:, 0:16], invc[:, g, :], ALU.mult, kcur + ["invc"], [kf])
                        tt(outp[:, 0:16], fx[:, 0:16], xt[:, 0:16], ALU.subtract, [kf, kx(oc, ti)] + ([ksrc] if first else []), [kb(oc, ti)])

        for ti in range(G.nt):
            c0 = ti * W

            def ychunk(oc):
                g = oc // 2
                py, kpy = PS(4, 6)
                for kc in range(2):
                    mm(py[:, 0:W], G.poolw[:, kc, g, (oc % 2) * 128:(oc % 2 + 1) * 128], G.xbf[:, 2 * g + kc, c0:c0 + W],
                       kc == 0, kc == 1, [("A", "poolw"), kb(2 * g + kc, ti)], [kpy])
                return py, kpy
            ln_tile(G, l, 0, ti, ychunk, yscale=lambda oc: pscv[:, p, oc:oc + 1])

    def mixer_ab(G, l):
        m = l // 2
        W, L = G.W, G.L
        flush_pending(G)
        nch = W // L
        mid = L // 2 - 1
        if G.kind == "s":
            for s in range(G.nseq):
                dma("sp", Sst_box[0][:, s, :, :], state_hgrn[m, G.seqs[s]].rearrange("h k v -> k h v"), (), [("S", s)])
        else:
            memset("pool", Sst_box[0][:, 0, :, :], 0.0, [("S", 0)])
        if G.kind == "s":
            for s in range(G.nseq):
                b = G.seqs[s]
                pass
        for ti in range(G.nt):
            c0 = ti * W
            aq_sb = {}; sig_sb = {}; sg_sb = {}; vtok = {}
            order = [0, 2, 6, 4, 1, 3, 7, 5, 8, 9, 10, 11, 12, 13]
            for g in order:
                wr, kw = sRing()
                dma("sp", wr, win_bf[m, g], wkeys("win", m), [kw])
                sec, hh0 = g // 2, (g % 2) * 2
                if sec in (0, 1, 3, 4, 5):
                    for hi in range(2):
                        h = hh0 + hi
                        pt, kp = PS()
                        for kc in range(8):
                            mm(pt[:, 0:W], wr[:, kc, hi * 128:(hi + 1) * 128], G.xbf[:, kc, c0:c0 + W], kc == 0, kc == 7, [kw, kb(kc, ti)], [kp])
                        if sec == 0:
                            t_, k_ = FX(hi); act(t_[:, 0:W], pt[:, 0:W], AF.Copy, [kp], [k_]); aq_sb[h] = (t_, k_)
                        elif sec == 1:
                            t_, k_ = FX(2 + hi); act(t_[:, 0:W], pt[:, 0:W], AF.Sigmoid, [kp], [k_]); sig_sb[h] = (t_, k_)
                        elif sec == 3:
                            t_, k_ = BX(hi); act(t_[:, 0:W], pt[:, 0:W], AF.Silu, [kp], [k_]); sg_sb[h] = (t_, k_)
                        elif sec == 4:
                            cp(G.qT[:, h, c0:c0 + W], pt[:, 0:W], [kp], [("A", "qT", h, ti)])
                        elif G.kind == "p":
                            act(G.kT[:, h, c0:c0 + W], pt[:, 0:W], AF.Copy, [kp], [("A", "kT", h, ti)])
                        else:
                            cp(G.knew[:, h, 0:W], pt[:, 0:W], [kp], [("A", "knew", h)])
                if sec in (2, 5, 6):
                    for bi, (co, nb, s, t0) in enumerate(G.blocks[ti]):
                        pt, kp = PS()
                        for kc in range(8):
                            mm(pt[0:nb, 0:256], G.xbf[:, kc, c0 + co:c0 + co + nb], wr[:, kc, :], kc == 0, kc == 7, [kw, kb(kc, ti)], [kp])
                        if sec == 2:
                            t_, k_ = BX(2 + bi // 2)
                            tv = t_[:, (bi % 2) * 256:(bi % 2 + 1) * 256]
                            vtok[(g % 2, bi)] = (tv, k_)
                            act(tv[0:nb, :], pt[0:nb, 0:256], AF.Copy, [kp], [k_])
                        else:
                            st, ks_ = sStage()
                            act(st[0:nb, 0:256], pt[0:nb, 0:256], AF.Copy, [kp], [ks_])
                            outt = G.k_out if sec == 5 else G.v_out
                            dma("pool", outt[m, G.seqs[s], t0:t0 + nb, (g % 2) * 256:(g % 2 + 1) * 256], st[0:nb, 0:256], [ks_], [("kvout",)])
                            if sec == 6:
                                if G.kind == "p":
                                    vt = t0 // 128
                                    cp(G.V[0:nb, vt, (g % 2) * 256:(g % 2 + 1) * 256], st[0:nb, 0:256], [ks_], [("A", "V", vt)])
                                else:
                                    cp(G.vnew[0:nb, s, (g % 2) * 256:(g % 2 + 1) * 256], st[0:nb, 0:256], [ks_], [("A", "vnew", s)])
                if g in (4, 5):
                    for hi in range(2):
                        h = (g % 2) * 2 + hi
                        hgrn_tile_head(G, m, ti, h, aq_sb[h], sig_sb[h], sg_sb[h], vtok, hi, g % 2)
            for h in range(4):
                cp(G.xbf[:, h, c0:c0 + W], hout[:, h, 0:W], [("hout", h)], [kb(h, ti)], eng="pool")
        for s in range(G.nseq):
            dma("pool", G.h_out[m, G.seqs[s]].rearrange("h k v -> k h v"), Sst_box[0][:, s, :, :], [("S", s)], [("hout",)])
        attention(G, m)
        for ti in range(G.nt):
            c0 = ti * W

            wst = {}

            def ychunk(oc, c0=c0, ti=ti, wst=wst):
                py, kpy = PS(4, 6)
                if oc % 2 == 0:
                    wst["w"] = sRing()
                    dma("sp", wst["w"][0], wout_bf[m, oc // 2], wkeys("wout", m), [wst["w"][1]])
                wr, kw = wst["w"]
                for kc in range(8):
                    mm(py[:, 0:W], wr[:, kc, (oc % 2) * 128:(oc % 2 + 1) * 128], G.xbf[:, kc, c0:c0 + W], kc == 0, kc == 7,
                       [kw, kb(kc, ti)], [kpy])
                return py, kpy
            ln_tile(G, l, 0, ti, ychunk)

    def hgrn_tile_head(G, m, ti, h, aq, sig, sg, vtok, hi, gp):
        W, L = G.W, G.L
        nch = W // L
        mid = L // 2 - 1
        c0 = ti * W
        aq_t, kaq = aq
        sig_t, ksig = sig
        sg_t, ksg = sg
        lf, klf = FX(4)
        act(lf[:, 0:W], sig_t[:, 0:W], AF.Ln, [ksig, "lb", "oml"], [klf], bias=lbv[:, m, h:h + 1], scale=omlv[:, m, h:h + 1])
        kin = sig_t
        ts(kin[:, 0:W], sig_t[:, 0:W], nomlv[:, m, h:h + 1], omlv[:, m, h:h + 1], ALU.mult, ALU.add, [ksig, "oml", "noml"], [ksig])
        bt, kbt = FX(5)
        P.add("dve", lambda e: e.tensor_tensor_scan(out=bt[:, 0:W], data0=G.reset[:, 0:W], data1=lf[:, 0:W], initial=0.0,
                                                    op0=ALU.mult, op1=ALU.add), [klf, "resetp", "resets"], [kbt])
        b3 = bt[:, 0:W].rearrange("p (c t) -> p c t", c=nch)
        dd = lf
        d3 = dd[:, 0:W].rearrange("p (c t) -> p c t", c=nch)
        tt(d3, b3, b3[:, :, mid:mid + 1].to_broadcast([128, nch, L]), ALU.subtract, [kbt, klf], [klf])
        sidx = hgrn_tile_head.cnt % 4; hgrn_tile_head.cnt += 1
        em = smallv[:, 3 * sidx + 0, 0:nch]; el = smallv[:, 3 * sidx + 1, 0:nch]; eb = smallv[:, 3 * sidx + 2, 0:nch]
        ksm = ("small", sidx)
        act(em, b3[:, :, mid], AF.Exp, [kbt], [ksm])
        act(el, d3[:, :, L - 1], AF.Exp, [klf, ksm], [ksm])
        act(eb, b3[:, :, L - 1], AF.Exp, [kbt, ksm], [ksm])
        e1 = bt
        act(e1[:, 0:W], dd[:, 0:W], AF.Exp, [klf, kbt, ksm], [kbt])
        qt, kqt = BX(4)
        tt(qt[:, 0:W], aq_t[:, 0:W], e1[:, 0:W], ALU.mult, [kaq, kbt], [kqt])
        act(e1[:, 0:W], dd[:, 0:W], AF.Exp, [klf, kbt, kqt], [kbt], scale=-1.0)
        kt, kkt = BX(5)
        tt(kt[:, 0:W], kin[:, 0:W], e1[:, 0:W], ALU.mult, [ksig, kbt], [kkt])
        oacc = aq_t
        koacc = kaq
        for bi, (co, nb, s, t0) in enumerate(G.blocks[ti]):
            vt_t, kvt = vtok[(gp, bi)]
            vv = vt_t[0:nb, hi * 128:(hi + 1) * 128]
            ptk, kptk = PS()
            ptkb = ptk[:, 0:256].bitcast(BF16)
            tr(ptkb[0:nb, 0:128], kt[:, co:co + nb], identb, [kkt, "identb"], [kptk])
            b7, kktok = BX(7)
            ktok = b7[:, (bi % 2) * 256:(bi % 2) * 256 + 128]
            cp(ktok[0:nb, 0:128], ptkb[0:nb, 0:128], [kptk], [kktok])
            psc, kpsc = PS()
            mm(psc[0:nb, 0:nb], kt[:, co:co + nb], qt[:, co:co + nb], True, True, [kkt, kqt], [kpsc])
            scm, kscm = b7[:, (bi % 2) * 256 + 128:(bi % 2) * 256 + 256], kktok
            tt(scm[0:nb, 0:nb], psc[0:nb, 0:nb], G.maskH[0:nb, 0:nb], ALU.mult, [kpsc, "maskp", "masks"], [kscm])
            po, kpo = PS()
            mm(po[:, 0:nb], vv, scm[0:nb, 0:nb], True, True, [kvt, kscm], [kpo])
            act(oacc[:, co:co + nb], po[:, 0:nb], AF.Copy, [kpo, kqt], [koacc])
            for ci in range(nb // L):
                c = (co + ci * L) // L
                S = Sst_box[0][:, s, h, :]
                kS = ("S", s)
                act(Sbf, S, AF.Copy, [kS, ksm], ["Sbf"], scale=em[:, c:c + 1])
                px, kpx = PS()
                mm(px[:, 0:L], Sbf, qt[:, co + ci * L:co + (ci + 1) * L], True, True, ["Sbf", kqt], [kpx])
                tt(oacc[:, co + ci * L:co + (ci + 1) * L], oacc[:, co + ci * L:co + (ci + 1) * L], px[:, 0:L], ALU.add, [koacc, kpx], [koacc])
                pu, kpu = PS()
                mm(pu[:, 0:128], ktok[ci * L:(ci + 1) * L, 0:128], vv[ci * L:(ci + 1) * L, :], True, True, [kktok, kvt], [kpu])
                tmp, ktmp = FX(6)
                ts(tmp[:, 0:128], pu[:, 0:128], el[:, c:c + 1], None, ALU.mult, None, [kpu, ksm], [ktmp])
                stt(S, S, eb[:, c:c + 1], tmp[:, 0:128], ALU.mult, ALU.add, [kS, ksm, ktmp], [kS])
        sq, ksq = BX(6)
        act(sq[:, 0:W], oacc[:, 0:W], AF.Square, [koacc], [ksq])
        pss, kpss = PS()
        mm(pss[:, 0:W], onesb, sq[:, 0:W], True, True, [ksq, "onesb"], [kpss])
        rs, krs = FX(6)
        rsqrt(rs[:, 0:W], pss[:, 0:W], 1, [kpss], [krs])
        stt(rs[:, 0:W], oacc[:, 0:W], hgv[:, m, h:h + 1], rs[:, 0:W], ALU.mult, ALU.mult, [koacc, krs, "hg"], [krs])
        tt(hout[:, h, 0:W], rs[:, 0:W], sg_t[:, 0:W], ALU.mult, [krs, ksg], [("hout", h)])
    hgrn_tile_head.cnt = 0

    def attention(G, m):
        T, Pn = G.T, G.P
        QW = min(512, T)
        nq = T // QW
        for s in range(G.nseq):
            b = G.seqs[s]
            scol = s * T
            if G.kind == "s":
                for kt_ in range(G.nkh):
                    st, ks_ = sStage()
                    dma("sp", st[:, 0:512], cache_k[m, b, kt_ * 128:(kt_ + 1) * 128, :], (), [ks_])
                    pt, kp = PS(4, 8)
                    for h in range(4):
                        tr(pt[:, h * 128:(h + 1) * 128], st[:, h * 128:(h + 1) * 128], ident, [ks_, "ident"], [kp])
                    act(G.kT[:, :, kt_ * 128:(kt_ + 1) * 128], pt[:, :].rearrange("p (h k) -> p h k", h=4), AF.Copy, [kp], [("A", "kTh", kt_)])
                    st2, ks2 = sStage()
                    dma("sp", st2[:, 0:512], cache_v[m, b, kt_ * 128:(kt_ + 1) * 128, :], (), [ks2])
                    cp(G.V[:, kt_, :], st2[:, 0:512], [ks2], [("A", "Vh", kt_)], eng="pool")
                for h in range(4):
                    cp(G.kT[:, h, Pn:Pn + T], G.knew[:, h, scol:scol + T], [("A", "knew", h)], [("A", "kTn", h)], eng="pool")
                cp(G.V[0:T, G.nkh, :], G.vnew[0:T, s, :], [("A", "vnew", s)], [("A", "Vn")], eng="pool")
            ktiles = []
            for kt_ in range(G.nkh):
                ktiles.append((kt_ * 128, 128, kt_, True))
            if G.kind == "p":
                for kt_ in range(T // 128):
                    ktiles.append((kt_ * 128, 128, kt_, False))
            else:
                ktiles.append((Pn, T, G.nkh, False))
            for h in range(4):
                for qi in range(nq):
                    q0 = qi * QW
                    ti = (scol + q0) // G.W
                    accs = [(psb[i], ("ps", i)) for i in range(4)]
                    vis = []
                    for (k0, nk, vt, hist) in ktiles:
                        if hist:
                            vis.append((k0, nk, vt, hist, 0))
                        else:
                            krel = k0 - Pn
                            if krel >= q0 + QW:
                                continue
                            vis.append((k0, nk, vt, hist, max(0, krel - q0)))
                    for idx, (k0, nk, vt, hist, qs) in enumerate(vis):
                        first, last = idx == 0, idx == len(vis) - 1
                        nqc = QW - qs
                        qsl = slice(scol + q0 + qs, scol + q0 + QW)
                        if G.kind == "p":
                            kkeys = [("A", "kT", h, (k0) // G.W)]
                            vkeys = [("A", "V", vt)]
                        else:
                            kkeys = [("A", "kTh", vt)] if hist else [("A", "kTn", h)]
                            vkeys = [("A", "Vh", vt)] if hist else [("A", "Vn")]
                        pp = []
                        for half in range(2):
                            psc, kpsc = PS(4, 8)
                            pr = slice(half * 64, (half + 1) * 64)
                            mm(psc[0:nk, 0:nqc], G.kT[pr, h, k0:k0 + nk], G.qT[pr, h, qsl], True, True,
                               kkeys + [("A", "qT", h, ti)], [kpsc])
                            pt_, kpt = sB()
                            act(pt_[0:nk, 0:nqc], psc[0:nk, 0:nqc], AF.Exp, [kpsc], [kpt], scale=0.125)
                            if G.kind == "p":
                                if not hist:
                                    krel = k0 - Pn
                                    for qb in range(qs, QW, 128):
                                        d = krel - (q0 + qb)
                                        if d in (0, -128):
                                            di = 0 if d == 0 else 1
                                            sl = slice(qb - qs, qb - qs + 128)
                                            tt(pt_[0:nk, sl], pt_[0:nk, sl], Ep[0:nk, di, h, :], ALU.mult, [kpt, "Ep"], [kpt])
                            else:
                                if hist and k0 == Pn - 128:
                                    tt(pt_[0:nk, 0:nqc], pt_[0:nk, 0:nqc], Esh[0:nk, h, :], ALU.mult, [kpt, "Es"], [kpt])
                                if not hist:
                                    tt(pt_[0:nk, 0:nqc], pt_[0:nk, 0:nqc], Esn[0:nk, h, :], ALU.mult, [kpt, "Es"], [kpt])
                            pp.append((pt_, kpt))
                        for half in range(2):
                            pt_, kpt = pp[half]
                            (po, kpo), (psm, kpsm) = accs[2 * half], accs[2 * half + 1]
                            mm(po[:, qs:QW], G.V[0:nk, vt, h * 128:(h + 1) * 128], pt_[0:nk, 0:nqc], first, last, vkeys + [kpt], [kpo])
                            mm(psm[:, qs:QW], onesb[0:nk, :], pt_[0:nk, 0:nqc], first, last, ["onesb", kpt], [kpsm])
                    t1, k1 = FX(0); t2, k2 = FX(1)
                    P.add("dve", lambda e, o=t1, i=accs[1][0]: e.reciprocal(out=o[:, 0:QW], in_=i[:, 0:QW]), [accs[1][1]], [k1])
                    tt(t1[:, 0:QW], accs[0][0][:, 0:QW], t1[:, 0:QW], ALU.mult, [accs[0][1], k1], [k1])
                    P.add("dve", lambda e, o=t2, i=accs[3][0]: e.reciprocal(out=o[:, 0:QW], in_=i[:, 0:QW]), [accs[3][1]], [k2])
                    tt(t2[:, 0:QW], accs[2][0][:, 0:QW], t2[:, 0:QW], ALU.mult, [accs[2][1], k2], [k2])
                    stt(t1[:, 0:QW], t2[:, 0:QW], neglam[:, m:m + 1], t1[:, 0:QW], ALU.mult, ALU.add, [k1, k2, "nl"], [k1])
                    sq, ksq = sB()
                    act(sq[:, 0:QW], t1[:, 0:QW], AF.Square, [k1], [ksq])
                    pss, kpss = PS(4, 8)
                    mm(pss[:, 0:QW], onesb, sq[:, 0:QW], True, True, [ksq, "onesb"], [kpss])
                    rsqrt(t2[:, 0:QW], pss[:, 0:QW], 1, [kpss, k2], [k2])
                    P.add("dve", lambda e, o=G.xbf[:, 4 + h, scol + q0:scol + q0 + QW], a=t1, r=t2, g_=dgv[:, m:m + 1]:
                          e.scalar_tensor_tensor(out=o, in0=a[:, 0:QW], scalar=g_, in1=r[:, 0:QW], op0=ALU.mult, op1=ALU.mult),
                          [k1, k2, "dg"], [kb(4 + h, ti)])

    import os
    LEVEL = int(os.environ.get("KLEVEL", "99"))
    KL = int(os.environ.get("KLAYERS", "4"))
    KG = os.environ.get("KGROUPS", "ps")
    setup()
    cast_order = [("win", 0), ("wout", 0), ("wup", 0), ("wdn", 0), ("poolw", 0), ("wup", 1), ("wdn", 1),
                  ("win", 1), ("wout", 1), ("wup", 2), ("wdn", 2), ("poolw", 1), ("wup", 3), ("wdn", 3)]
    castf = {"win": cast_win, "wout": cast_wout, "wup": cast_wup, "wdn": cast_wdn, "poolw": cast_poolw}
    if LEVEL >= 1:
        for nm, i in cast_order[:5]:
            castf[nm](i)
    cast_later = {0: cast_order[5:7], 1: cast_order[7:12], 2: cast_order[12:]}

    groups = [("p", [i]) for i in range(NPS)]
    if NSS > 0:
        groups.append(("s", list(range(NSS))))
    groups = [g_ for g_ in groups if g_[0] in KG]
    for gi, (kind, ids) in enumerate(groups):
        if LEVEL < 2:
            break
        barrier()
        G = make_group(kind, ids)
        if kind == "s":
            G.o_knew = carve(4 * G.NT * 2); G.knew = view(G.o_knew, BF16, (4, G.NT))
            G.o_vnew = carve(G.nseq * 512 * 2); G.vnew = view(G.o_vnew, BF16, (G.nseq, 512))
        load_x(G)
        for l in range(DEPTH if LEVEL >= 3 else 0):
            if l >= KL:
                break
            if gi == 0:
                for nm, i in cast_later.get(l, []):
                    castf[nm](i)
            if G.alias:
                arena_barrier()
            if l % 2 == 0:
                mixer_ab(G, l)
            else:
                mixer_pool(G, l)
            if G.alias:
                arena_barrier()
            ffn(G, l)
        store_y(G)
    barrier()

    P.finalize(sems)
    with nc.Block() as block:
        @block.sync
        def _(e):
            P.run_stream("sp", e)

        @block.scalar
        def _(e):
            P.run_stream("act", e)

        @block.vector
        def _(e):
            P.run_stream("dve", e)

        @block.gpsimd
        def _(e):
            P.run_stream("pool", e)

        @block.tensor
        def _(e):
            P.run_stream("pe", e)
    es.close()
    return nc, len(P.ops)


_CACHE = {}


def kernel(**inp):
    import os
    NCORES = int(os.environ.get("KCORES", "8"))
    x_prompt = np.asarray(inp["x_prompt"], np.float32)
    x_sample = np.asarray(inp["x_sample"], np.float32)
    B, SEQ, _ = x_prompt.shape
    DB, DS, _ = x_sample.shape
    PAST = inp["cache_k"].shape[2]
    NPS, NSS = B // NCORES, DB // NCORES
    key = (NPS, SEQ, NSS, DS, PAST)
    if key not in _CACHE:
        _CACHE[key] = build(*key)
    nc, nops = _CACHE[key]
    consts = host_consts(NSS, DS)
    ck = np.asarray(inp["cache_k"], np.float32).reshape(2, DB, PAST, 512)
    cv = np.asarray(inp["cache_v"], np.float32).reshape(2, DB, PAST, 512)
    sh = np.asarray(inp["state_hgrn"], np.float32)
    sp_ = np.asarray(inp["state_pool"], np.float32)
    sc = np.asarray(inp["state_ffn_conv"], np.float32)
    shared = {k: np.ascontiguousarray(np.asarray(inp[k], np.float32)) for k in
              ("w_in_ab", "w_out_ab", "lb_logits", "hgrn_norm_g", "diff_norm_g", "rel_bias", "pool_w", "pool_scale",
               "ffn_w_up", "ffn_conv_w", "ffn_conv_b", "ffn_w_down", "ln1_g", "ln1_b", "ln2_g", "ln2_b")}
    shared["diff_lambda"] = np.ascontiguousarray(np.asarray(inp["diff_lambda"], np.float32).reshape(2, 256))
    shared.update(consts)
    in_maps = []
    for c in range(NCORES):
        ps, ss = slice(c * NPS, (c + 1) * NPS), slice(c * NSS, (c + 1) * NSS)
        d = dict(shared)
        d["x_p"] = np.ascontiguousarray(x_prompt[ps])
        d["x_s"] = np.ascontiguousarray(x_sample[ss])
        d["cache_k"] = np.ascontiguousarray(ck[:, ss])
        d["cache_v"] = np.ascontiguousarray(cv[:, ss])
        d["state_hgrn"] = np.ascontiguousarray(sh[:, ss])
        d["state_pool"] = np.ascontiguousarray(sp_[:, ss])
        d["state_conv"] = np.ascontiguousarray(sc[:, ss])
        in_maps.append(d)
    res = run_bass_kernel_spmd(nc, in_maps, core_ids=list(range(NCORES)))
    R = res.results

    def cat(name, axis):
        return np.concatenate([np.asarray(r[name], np.float32) for r in R], axis=axis)
    y_p = cat("y_p", 0)
    y_s = cat("y_s", 0)
    k_p = cat("k_p", 1).reshape(2, B, SEQ, 4, 128)
    v_p = cat("v_p", 1).reshape(2, B, SEQ, 4, 128)
    h_p = cat("h_p", 1)
    pl_p = cat("pl_p", 1)
    cv_p = cat("cv_p", 1)
    k_s = cat("k_s", 1).reshape(2, DB, DS, 4, 128)
    v_s = cat("v_s", 1).reshape(2, DB, DS, 4, 128)
    h_s = cat("h_s", 1)
    pl_s = cat("pl_s", 1)
    cv_s = cat("cv_s", 1)
    return (y_p, y_s, k_p, v_p, h_p, pl_p, cv_p, k_s, v_s, h_s, pl_s, cv_s)
```

```python
import math
from contextlib import ExitStack
import numpy as np
import concourse.bass as bass
import concourse.mybir as mybir
from concourse.bass_utils import run_bass_kernel_spmd

F32, BF16 = mybir.dt.float32, mybir.dt.bfloat16
AF = mybir.ActivationFunctionType
ALU = mybir.AluOpType

D = 1024
NCH = 8
DFF = 2816
NJ = 22
INC = 3584
DEPTH = 4
ALPHA = (2 * DEPTH) ** 0.25
INV_ALPHA = 1.0 / ALPHA
LN_EPS_S = 1e-5 / (ALPHA * ALPHA)
RMS_EPS = 1e-6
NREL = 383
POOL_W = (2, 4, 8, 16)


class Op:
    __slots__ = ("eng", "fn", "deps", "dma", "needed", "tok")


class Prog:
    NS = {"sp": 24, "pool": 12}

    def __init__(self):
        self.ops = []
        self.lw = {}
        self.rd = {}

    def add(self, eng, fn, reads=(), writes=(), dma=False):
        op = Op()
        op.eng, op.fn, op.dma, op.needed, op.tok = eng, fn, dma, False, None
        deps = {}
        def _exp(keys):
            out_ = []
            for k in keys:
                if isinstance(k, tuple) and len(k) == 2 and k[0] == "stg":
                    out_ += [("stg", k[1], 0), ("stg", k[1], 1)]
                else:
                    out_.append(k)
            return out_
        reads = _exp(reads) + ["EPOCH"]
        writes = _exp(writes)
        if eng != "pe":
            writes += [k for k in reads if isinstance(k, tuple) and k[0] == "ps" and k not in writes]
        if any(isinstance(k, tuple) and k[0] == "A" for k in reads + writes):
            reads.append("ARENA")
        for k in reads:
            w = self.lw.get(k)
            if w is not None:
                deps[id(w)] = (w, "raw")
        for k in writes:
            w = self.lw.get(k)
            if w is not None and id(w) not in deps:
                deps[id(w)] = (w, "waw")
            for r in self.rd.get(k, ()):
                if id(r) not in deps:
                    deps[id(r)] = (r, "war")
        out = []
        for d, kind in deps.values():
            if d is op:
                continue
            if (not d.dma) and (not dma) and d.eng == eng:
                if eng == "pe" or kind != "raw":
                    continue
            out.append(d)
        op.deps = out
        for k in reads:
            self.rd.setdefault(k, []).append(op)
        for k in writes:
            self.lw[k] = op
            self.rd[k] = []
        self.ops.append(op)
        return op

    def finalize(self, sems):
        cnt = {}
        dcount = {"sp": 0, "pool": 0}
        dops = {"sp": [], "pool": []}
        for op in self.ops:
            for d in op.deps:
                d.needed = True
        for op in self.ops:
            if op.dma:
                q = op.eng
                i = dcount[q]
                ns = self.NS[q]
                op.tok = (sems["dma_" + q][i % ns], 16 * (i // ns + 1))
                if i >= ns:
                    op.deps.append(dops[q][i - ns])
                dops[q].append(op)
                dcount[q] = i + 1
            elif op.needed:
                cnt[op.eng] = cnt.get(op.eng, 0) + 1
                op.tok = (sems[op.eng], cnt[op.eng])

    def run_stream(self, name, e):
        waited = {}
        for op in self.ops:
            if op.eng != name:
                continue
            req = {}
            for d in op.deps:
                s, v = d.tok
                k = id(s)
                if k not in req or req[k][1] < v:
                    req[k] = (s, v)
            for k, (s, v) in req.items():
                if waited.get(k, 0) < v:
                    e.wait_ge(s, v)
                    waited[k] = v
            ins = op.fn(e)
            if op.dma:
                ins.then_inc(op.tok[0], 16)
            elif op.needed:
                ins.then_inc(op.tok[0], 1)


def rel_bucket_np(rel):
    half, max_exact = 16, 8
    base = np.where(rel > 0, half, 0)
    n = np.abs(rel)
    nf = np.maximum(n, 1).astype(np.float32)
    large = max_exact + (np.log(nf / np.float32(max_exact)) / np.float32(math.log(128 / max_exact))
                         * (half - max_exact)).astype(np.int32)
    large = np.minimum(large, half - 1)
    return base + np.where(n < max_exact, n, large)


def host_consts(nss, ds):
    c = {}
    c["c_ident"] = np.eye(128, dtype=np.float32)
    s = np.arange(128)[:, None]
    t = np.arange(128)[None, :]
    c["c_maskp"] = ((s // 64 == t // 64) & (s <= t)).astype(np.float32)
    s = np.arange(ds)[:, None]
    t = np.arange(ds)[None, :]
    c["c_masks"] = (s <= t).astype(np.float32)
    c["c_resetp"] = (np.arange(512) % 64 != 0).astype(np.float32)
    c["c_resets"] = (np.arange(nss * ds) % ds != 0).astype(np.float32)
    k = np.arange(128)[:, None]
    q = np.arange(128)[None, :]
    c["c_vis0"] = (k // 64 <= q // 64).astype(np.float32)
    rel = 127 - np.arange(NREL)
    b = rel_bucket_np(rel.astype(np.int32))
    oh = np.zeros((32, NREL), np.float32)
    oh[b, np.arange(NREL)] = 1.0
    c["c_oht"] = oh
    invc = np.zeros((4, 16), np.float32)
    for g, w in enumerate(POOL_W):
        invc[g] = 1.0 / np.minimum(w, np.arange(16) + 1)
    c["c_invc"] = invc
    return c


def build(NPS, SEQ, NSS, DS, PAST):
    nc = bass.Bass("TRN2", target_bir_lowering=False)
    P = Prog()

    def din(name, shape):
        return nc.dram_tensor(name, list(shape), F32, kind="ExternalInput")

    def dout(name, shape):
        return nc.dram_tensor(name, list(shape), F32, kind="ExternalOutput")

    def dint(name, shape, dt):
        return nc.dram_tensor(name, list(shape), dt, kind="Internal")

    NTS = NSS * DS
    x_p = din("x_p", [NPS, SEQ, D]).ap()
    x_s = din("x_s", [NSS, DS, D]).ap()
    cache_k = din("cache_k", [2, NSS, PAST, 512]).ap()
    cache_v = din("cache_v", [2, NSS, PAST, 512]).ap()
    state_hgrn = din("state_hgrn", [2, NSS, 4, 128, 128]).ap()
    state_pool = din("state_pool", [2, NSS, 15, D]).ap()
    state_conv = din("state_conv", [4, NSS, 2, DFF]).ap()
    w_in = din("w_in_ab", [2, D, INC]).ap()
    w_out = din("w_out_ab", [2, D, D]).ap()
    lb_logits = din("lb_logits", [2, 512]).ap()
    hgrn_norm_g = din("hgrn_norm_g", [2, 512]).ap()
    diff_lambda = din("diff_lambda", [2, 256]).ap()
    diff_norm_g = din("diff_norm_g", [2, 128]).ap()
    rel_bias = din("rel_bias", [32, 4]).ap()
    pool_w = din("pool_w", [2, 4, 256, 256]).ap()
    pool_scale = din("pool_scale", [2, D]).ap()
    w_up = din("ffn_w_up", [4, D, 2 * DFF]).ap()
    conv_w = din("ffn_conv_w", [4, 3, DFF]).ap()
    conv_b = din("ffn_conv_b", [4, DFF]).ap()
    w_down = din("ffn_w_down", [4, DFF, D]).ap()
    ln1_g = din("ln1_g", [4, D]).ap()
    ln1_b = din("ln1_b", [4, D]).ap()
    ln2_g = din("ln2_g", [4, D]).ap()
    ln2_b = din("ln2_b", [4, D]).ap()
    c_ident = din("c_ident", [128, 128]).ap()
    c_maskp = din("c_maskp", [128, 128]).ap()
    c_masks = din("c_masks", [DS, DS]).ap()
    c_resetp = din("c_resetp", [512]).ap()
    c_resets = din("c_resets", [NTS]).ap()
    c_vis0 = din("c_vis0", [128, 128]).ap()
    c_oht = din("c_oht", [32, NREL]).ap()
    c_invc = din("c_invc", [4, 16]).ap()
    y_p = dout("y_p", [NPS, SEQ, D]).ap()
    y_s = dout("y_s", [NSS, DS, D]).ap()
    k_p = dout("k_p", [2, NPS, SEQ, 512]).ap()
    v_p = dout("v_p", [2, NPS, SEQ, 512]).ap()
    h_p = dout("h_p", [2, NPS, 4, 128, 128]).ap()
    pl_p = dout("pl_p", [2, NPS, 15, D]).ap()
    cv_p = dout("cv_p", [4, NPS, 2, DFF]).ap()
    k_s = dout("k_s", [2, NSS, DS, 512]).ap()
    v_s = dout("v_s", [2, NSS, DS, 512]).ap()
    h_s = dout("h_s", [2, NSS, 4, 128, 128]).ap()
    pl_s = dout("pl_s", [2, NSS, 15, D]).ap()
    cv_s = dout("cv_s", [4, NSS, 2, DFF]).ap()
    win_bf = dint("win_bf", [2, 14, 128, 8, 256], BF16).ap()
    wout_bf = dint("wout_bf", [2, 4, 128, 8, 256], BF16).ap()
    poolw_bf = dint("poolw_bf", [2, 128, 2, 4, 256], BF16).ap()
    wup_bf = dint("wup_bf", [4, NJ, 128, 8, 256], BF16).ap()
    wdn_bf = dint("wdn_bf", [4, 4, 128, NJ, 256], BF16).ap()
    evm_t = dint("evm", [4, 128, NREL], F32)
    evm = evm_t.ap()

    es = ExitStack()
    ARENA_BYTES = 207 * 1024
    arena = es.enter_context(nc.sbuf_tensor("arena", [128, ARENA_BYTES // 4], F32))
    psb = [es.enter_context(nc.psum_tensor(f"ps{i}", [128, 512], F32)) for i in range(8)]
    sems = {}
    for n in ("pe", "act", "dve", "pool", "sp"):
        sems[n] = es.enter_context(nc.semaphore("sem_" + n))
    for q in ("sp", "pool"):
        sems["dma_" + q] = [es.enter_context(nc.semaphore(f"dsem_{q}{i}")) for i in range(Prog.NS[q])]

    top = [0]

    def carve(nbytes):
        o = top[0]
        top[0] = (o + nbytes + 63) // 64 * 64
        assert top[0] <= ARENA_BYTES, ("arena overflow", top[0])
        return o

    def view(off, dt, shape, parts=128):
        n = int(np.prod(shape))
        esz = 4 if dt == F32 else 2
        nb = n * esz
        assert off % 4 == 0
        a = arena[0:parts, off // 4: off // 4 + (nb + 3) // 4]
        if dt == BF16:
            a = a.bitcast(BF16)
            if a.shape[1] != n:
                a = a[:, 0:n]
        if len(shape) == 2:
            a = a.rearrange("p (a b) -> p a b", a=shape[0])
        elif len(shape) == 3:
            a = a.rearrange("p (a b c) -> p a b c", a=shape[0], b=shape[1])
        return a

    o_ident = carve(512); ident = view(o_ident, F32, (128,))
    o_identb = carve(256); identb = view(o_identb, BF16, (128,))
    o_onesb = carve(256); onesb = view(o_onesb, BF16, (128,))
    o_maskp = carve(256); maskp = view(o_maskp, BF16, (128,))
    o_masks = carve(64); masks = view(o_masks, BF16, (DS,))
    o_resetp = carve(1024); resetp = view(o_resetp, BF16, (512,))
    o_resets = carve(2 * NTS + 4); resets = view(o_resets, BF16, (NTS,))
    o_Ep = carve(2 * 4 * 256); Ep = view(o_Ep, BF16, (2, 4, 128))
    o_Esh = carve(4 * DS * 2); Esh = view(o_Esh, BF16, (4, DS))
    o_Esn = carve(4 * DS * 2); Esn = view(o_Esn, BF16, (4, DS))
    o_lng = carve(4 * 2 * 8 * 4); lng = view(o_lng, F32, (4, 2, 8))
    o_lnb = carve(4 * 2 * 8 * 4); lnb = view(o_lnb, F32, (4, 2, 8))
    o_cw = carve(4 * 3 * NJ * 4); cw = view(o_cw, F32, (4, 3, NJ))
    o_cb = carve(4 * NJ * 4); cbias = view(o_cb, F32, (4, NJ))
    o_lb = carve(2 * 4 * 4); lbv = view(o_lb, F32, (2, 4))
    o_oml = carve(2 * 4 * 4); omlv = view(o_oml, F32, (2, 4))
    o_noml = carve(2 * 4 * 4); nomlv = view(o_noml, F32, (2, 4))
    o_hg = carve(2 * 4 * 4); hgv = view(o_hg, F32, (2, 4))
    o_dg = carve(2 * 4); dgv = view(o_dg, F32, (2,))
    o_nl = carve(2 * 4); neglam = view(o_nl, F32, (2,))
    o_psc = carve(2 * 8 * 4); pscv = view(o_psc, F32, (2, 8))
    o_invc = carve(4 * 16 * 4); invc = view(o_invc, F32, (4, 16))
    o_epsc = carve(8); epsc = view(o_epsc, F32, (2,))
    o_tmpc = carve(2304); tmpc = view(o_tmpc, F32, (576,))
    NSM = max(1, NSS)
    Sst_box = [None]
    o_Sb = carve(256); Sbf = view(o_Sb, BF16, (128,))
    o_atail = carve(NJ * NSM * 2 * 4); atail = view(o_atail, F32, (NJ, NSM, 2))
    o_small = carve(12 * 8 * 4); smallv = view(o_small, F32, (12, 8))
    o_hout = carve(4 * 512 * 2); hout = view(o_hout, BF16, (4, 512))
    NF, NB = 7, 8
    FW = 528
    o_sf = [carve(FW * 4) for _ in range(NF)]
    o_sb = [carve(512 * 2) for _ in range(NB)]
    o_stg = [carve(4096) for _ in range(2)]
    NR = 3
    o_ring = [carve(4096) for _ in range(NR)]
    base_top = top[0]
    print("base_top", base_top)

    sf_i, sb_i, stg_i, ring_i = [0], [0], [0], [0]

    def sF():
        i = sf_i[0] % NF; sf_i[0] += 1
        return view(o_sf[i], F32, (FW,)), ("sf", i)

    def sB():
        i = sb_i[0] % NB; sb_i[0] += 1
        return view(o_sb[i], BF16, (512,)), ("sb", i)

    def FX(i):
        return view(o_sf[i], F32, (FW,)), ("sf", i)

    def BX(i):
        return view(o_sb[i], BF16, (512,)), ("sb", i)

    def sFr(lo, hi):
        n = hi - lo
        i = lo + sf_i[0] % n; sf_i[0] += 1
        return view(o_sf[i], F32, (FW,)), ("sf", i)

    def sStage():
        i = stg_i[0] % 2; stg_i[0] += 1
        return view(o_stg[i], F32, (1024,)), ("stg", i)

    sh_i = [0]

    def sHalf():
        i = sh_i[0] % 4; sh_i[0] += 1
        return view(o_stg[i // 2] + (i % 2) * 2048, F32, (512,)), ("stg", i // 2, i % 2)

    def sRing():
        i = ring_i[0] % NR; ring_i[0] += 1
        return view(o_ring[i], BF16, (8, 256)), ("ring", i)

    ps_i = [0]

    def PS(lo=0, hi=8):
        n = hi - lo
        i = lo + ps_i[0] % n; ps_i[0] += 1
        return psb[i], ("ps", i)

    def act(out, in_, func, reads, writes, bias=None, scale=None, accum=None):
        kw = {}
        if bias is not None: kw["bias"] = bias
        if scale is not None: kw["scale"] = scale
        if accum is not None: kw["accum_out"] = accum
        P.add("act", lambda e: e.activation(out=out, in_=in_, func=func, **kw), reads, writes)

    def tt(out, in0, in1, op, reads, writes, eng="dve"):
        P.add(eng, lambda e: e.tensor_tensor(out=out, in0=in0, in1=in1, op=op), reads, writes)

    def ts(out, in0, s1, s2, op0, op1, reads, writes, eng="dve"):
        if op1 is None:
            P.add(eng, lambda e: e.tensor_scalar(out=out, in0=in0, scalar1=s1, scalar2=None, op0=op0), reads, writes)
        else:
            P.add(eng, lambda e: e.tensor_scalar(out=out, in0=in0, scalar1=s1, scalar2=s2, op0=op0, op1=op1), reads, writes)

    def stt(out, in0, scalar, in1, op0, op1, reads, writes):
        P.add("dve", lambda e: e.scalar_tensor_tensor(out=out, in0=in0, scalar=scalar, in1=in1, op0=op0, op1=op1), reads, writes)

    def cp(out, in_, reads, writes, eng="dve"):
        P.add(eng, lambda e: e.tensor_copy(out=out, in_=in_), reads, writes)

    def mm(out, lhsT, rhs, start, stop, reads, writes):
        P.add("pe", lambda e: e.matmul(out, lhsT=lhsT, rhs=rhs, start=start, stop=stop), reads, writes)

    def tr(out, in_, idn, reads, writes):
        P.add("pe", lambda e: e.transpose(out, in_, idn), reads, writes)

    def dma(q, out, in_, reads, writes, slow=False):
        if slow:
            P.add(q, lambda e: e.dma_start(out=out, in_=in_, allow_slow_non_contiguous=True), reads, writes, dma=True)
        else:
            P.add(q, lambda e: e.dma_start(out=out, in_=in_), reads, writes, dma=True)

    def memset(eng, ap, val, writes):
        P.add(eng, lambda e: e.memset(ap, val), (), writes)

    def wkeys(nm, i):
        n = {"win": 8, "wout": 8, "poolw": 8, "wup": 16, "wdn": NJ}[nm]
        return [(nm, i, k) for k in range(n)]

    def rsqrt(out, in_, which, reads, writes):
        act(out, in_, AF.Sqrt, list(reads) + ["epsc"], writes, bias=epsc[:, which:which + 1])
        P.add("dve", lambda e: e.reciprocal(out=out, in_=out), writes, writes)

    def barrier():
        P.add("pool", lambda e: e.memset(tmpc[0:1, 0:1], 0.0), (), ["EPOCH", "ARENA"])

    def arena_barrier():
        P.add("pool", lambda e: e.memset(tmpc[0:1, 0:1], 0.0), (), ["ARENA"])

    def cast_win(m):
        for kc in range(8):
            src = w_in[m, kc * 128:(kc + 1) * 128, :].rearrange("p (g c) -> p g c", g=14)
            dst = win_bf[m, :, :, kc, :].rearrange("g p c -> p g c")
            dma("pool", dst, src, (), [("win", m, kc)])

    def cast_wout(m):
        for kc in range(8):
            dma("pool", wout_bf[m, :, :, kc, :].rearrange("q p c -> p q c"),
                w_out[m, kc * 128:(kc + 1) * 128, :].rearrange("p (q c) -> p q c", q=4), (), [("wout", m, kc)])

    def cast_poolw(p):
        for g in range(4):
            for kc in range(2):
                dma("pool", poolw_bf[p, :, kc, g, :], pool_w[p, g, kc * 128:(kc + 1) * 128, :], (), [("poolw", p, g * 2 + kc)])

    def cast_wup(l):
        for kc in range(8):
            for ab in range(2):
                src = w_up[l, kc * 128:(kc + 1) * 128, ab * DFF:(ab + 1) * DFF].rearrange("p (j c) -> p j c", j=NJ)
                dst = wup_bf[l, :, :, kc, ab * 128:(ab + 1) * 128].rearrange("j p c -> p j c")
                dma("pool", dst, src, (), [("wup", l, kc * 2 + ab)])

    def cast_wdn(l):
        for j in range(NJ):
            src = w_down[l, j * 128:(j + 1) * 128, :].rearrange("p (h c) -> p h c", h=4)
            dst = wdn_bf[l, :, :, j, :].rearrange("h p c -> p h c")
            dma("pool", dst, src, (), [("wdn", l, j)])

    def setup():
        dma("sp", ident, c_ident, (), ["ident"])
        t1, k1 = sF()
        dma("sp", t1[:, 0:128], c_maskp, (), [k1])
        cp(maskp, t1[:, 0:128], [k1], ["maskp"])
        cp(identb, ident, ["ident"], ["identb"])
        memset("dve", onesb, 1.0, ["onesb"])
        memset("dve", epsc[:, 0:1], LN_EPS_S, ["epsc"])
        memset("dve", epsc[:, 1:2], 128.0 * RMS_EPS, ["epsc"])
        t2, k2 = sF()
        dma("sp", t2[0:DS, 0:DS], c_masks, (), [k2])
        cp(masks[0:DS, :], t2[0:DS, 0:DS], [k2], ["masks"])
        t3, k3 = sF()
        dma("sp", t3[:, 0:512], c_resetp.partition_broadcast(128), (), [k3])
        cp(resetp, t3[:, 0:512], [k3], ["resetp"])
        t4, k4 = sF()
        dma("sp", t4[:, 0:NTS], c_resets.partition_broadcast(128), (), [k4])
        cp(resets, t4[:, 0:NTS], [k4], ["resets"])
        dma("sp", invc, bass.AP(c_invc.tensor, 0, [[0, 128], [16, 4], [1, 16]]), (), ["invc"])
        for l in range(4):
            for w, (g_, b_) in enumerate(((ln1_g, ln1_b), (ln2_g, ln2_b))):
                dma("sp", lng[:, l, w, :], g_[l].rearrange("(c p) -> p c", p=128), (), ["lnp"], slow=True)
                dma("sp", lnb[:, l, w, :], b_[l].rearrange("(c p) -> p c", p=128), (), ["lnp"], slow=True)
            for r in range(3):
                dma("sp", cw[:, l, r, :], conv_w[l, r].rearrange("(j p) -> p j", p=128), (), ["cw"], slow=True)
            dma("sp", cbias[:, l, :], conv_b[l].rearrange("(j p) -> p j", p=128), (), ["cw"], slow=True)
        for m in range(2):
            dma("sp", hgv[:, m, :], hgrn_norm_g[m].rearrange("(h p) -> p h", p=128), (), ["hg"], slow=True)
            dma("sp", dgv[:, m:m + 1], diff_norm_g[m].rearrange("(p o) -> p o", o=1), (), ["dg"], slow=True)
            dma("sp", pscv[:, m, :], pool_scale[m].rearrange("(c p) -> p c", p=128), (), ["psc"], slow=True)
        ts(hgv, hgv, math.sqrt(128.0), None, ALU.mult, None, ["hg"], ["hg"])
        for m in range(2):
            lam_init = 0.8 - 0.6 * math.exp(-0.3 * (2 * m))
            ts(dgv[:, m:m + 1], dgv[:, m:m + 1], math.sqrt(128.0) * (1.0 - lam_init), None, ALU.mult, None, ["dg"], ["dg"])
        ts(pscv, pscv, INV_ALPHA, None, ALU.mult, None, ["psc"], ["psc"])
        l0 = tmpc[:, 0:4]; l1 = tmpc[:, 4:8]
        dma("sp", l0, lb_logits[0].rearrange("(h p) -> p h", p=128), (), ["tmpc"], slow=True)
        dma("sp", l1, lb_logits[1].rearrange("(h p) -> p h", p=128), (), ["tmpc"], slow=True)
        tt(tmpc[:, 8:12], l1, l0, ALU.subtract, ["tmpc"], ["tmpc2"])
        memset("dve", lbv[:, 0, :], 0.0, ["lb"])
        act(lbv[:, 1, :], tmpc[:, 8:12], AF.Sigmoid, ["tmpc2", "lb"], ["lb"])
        ts(omlv, lbv, -1.0, 1.0, ALU.mult, ALU.add, ["lb"], ["oml"])
        ts(nomlv, omlv, -1.0, None, ALU.mult, None, ["oml"], ["noml"])
        for m in range(2):
            lam_init = 0.8 - 0.6 * math.exp(-0.3 * (2 * m))
            lp = tmpc[:, 16:16 + 256]
            dma("sp", lp, diff_lambda[m].partition_broadcast(128), ["tmpcl"], ["tmpcl"])
            pr = tmpc[:, 300:300 + 128]
            tt(pr.rearrange("p (a b) -> p a b", a=2), lp.rearrange("p (a t b) -> p a t b", a=2, t=2)[:, :, 0, :],
               lp.rearrange("p (a t b) -> p a t b", a=2, t=2)[:, :, 1, :], ALU.mult, ["tmpcl"], ["tmpcp"])
            sm_ = tmpc[:, 440:442]
            P.add("dve", lambda e, o=sm_, i=pr: e.tensor_reduce(out=o, in_=i.rearrange("p (a b) -> p a b", a=2),
                                                                 axis=mybir.AxisListType.X, op=ALU.add), ["tmpcp"], ["tmpcs"])
            ex = tmpc[:, 444:446]
            act(ex, sm_, AF.Exp, ["tmpcs"], ["tmpce"])
            tt(neglam[:, m:m + 1], ex[:, 1:2], ex[:, 0:1], ALU.subtract, ["tmpce"], ["nl"])
            ts(neglam[:, m:m + 1], neglam[:, m:m + 1], -lam_init, None, ALU.add, None, ["nl"], ["nl", "tmpcl"])
        rb = tmpc[0:32, 500:504]
        oht, ko = sStage()
        dma("sp", rb, rel_bias, (), ["rb"])
        dma("sp", oht[0:32, 0:NREL], c_oht, (), [ko])
        for h in range(4):
            rbb, kr = sF()
            cp(rbb[0:32, 0:128], rb[:, h:h + 1].to_broadcast([32, 128]), ["rb"], [kr])
            pt, kp = PS()
            mm(pt[:, 0:NREL], rbb[0:32, 0:128], oht[0:32, 0:NREL], True, True, [kr, ko], [kp])
            ngc = tmpc[:, 510 + h:511 + h]
            ts(ngc, pt[:, NREL - 1:NREL], -1.0, None, ALU.mult, None, [kp], [("ngc", h)])
            ev, ke = sF()
            act(ev[:, 0:NREL], pt[:, 0:NREL], AF.Exp, [kp, ("ngc", h)], [ke], bias=ngc)
            dma("sp", evm[h], ev[:, 0:NREL], [ke], [("evm", h)])
        vis, kv = sStage()
        dma("sp", vis[:, 0:128], c_vis0, (), [kv])
        for h in range(4):
            for di, d in enumerate((0, -128)):
                et, ke = sF()
                src = bass.AP(evm_t, h * 128 * NREL + (127 - d), [[NREL - 1, 128], [1, 128]])
                dma("sp", et[:, 0:128], src, [("evm", h)], [ke])
                if d == 0:
                    tt(Ep[:, di, h, :], et[:, 0:128], vis[:, 0:128], ALU.mult, [ke, kv], ["Ep"])
                else:
                    cp(Ep[:, di, h, :], et[:, 0:128], [ke], ["Ep"])
            et, ke = sF()
            src = bass.AP(evm_t, h * 128 * NREL + (127 + 128), [[NREL - 1, 128], [1, DS]])
            dma("sp", et[:, 0:DS], src, [("evm", h)], [ke])
            cp(Esh[:, h, :], et[:, 0:DS], [ke], ["Es"])
            et, ke = sF()
            src = bass.AP(evm_t, h * 128 * NREL + 127, [[NREL - 1, DS], [1, DS]])
            dma("sp", et[0:DS, 0:DS], src, [("evm", h)], [ke])
            cp(Esn[0:DS, h, :], et[0:DS, 0:DS], [ke], ["Es"])

    class Grp:
        pass

    def make_group(kind, seq_ids):
        G = Grp()
        G.pending = []
        G.kind = kind
        G.seqs = list(seq_ids)
        G.nseq = len(G.seqs)
        G.T = SEQ if kind == "p" else DS
        G.P = 0 if kind == "p" else PAST
        G.NT = G.nseq * G.T
        G.W = min(512, G.NT)
        G.nt = G.NT // G.W
        G.L = min(64, G.T)
        G.blocks = []
        for ti in range(G.nt):
            bl = []
            if kind == "p":
                for j in range(G.W // 128):
                    bl.append((128 * j, 128, 0, ti * G.W + 128 * j))
            else:
                for s in range(G.nseq):
                    bl.append((G.T * s, G.T, s, 0))
            G.blocks.append(bl)
        G.ns_t = 1 if kind == "p" else G.nseq
        G.Ts = G.W // G.ns_t
        top[0] = base_top
        G.o_xres = carve(8 * G.NT * 4)
        G.o_xbf = carve(8 * G.NT * 2)
        G.o_S = carve(G.nseq * 4 * 128 * 4)
        Sst_box[0] = view(G.o_S, F32, (G.nseq, 4, 128))
        KT = G.P + G.T
        G.KT = KT
        G.nkh = G.P // 128
        if kind == "p":
            a0 = top[0]
            G.o_qT = carve(4 * G.NT * 2); G.o_kT = carve(4 * KT * 2); G.o_V = carve((G.T // 128) * 512 * 2)
            G.o_poolw = G.o_qT
            a1 = top[0]
            top[0] = a0
            G.o_gT = carve(NJ * G.W * 2); G.o_wdn = carve(2 * NJ * 256 * 2)
            top[0] = max(top[0], a1)
            G.alias = True
        else:
            G.o_qT = carve(4 * G.NT * 2); G.o_kT = carve(4 * KT * 2); G.o_V = carve((G.nkh + 1) * 512 * 2)
            G.o_poolw = carve(2 * 4 * 256 * 2)
            G.o_gT = carve(NJ * G.W * 2); G.o_wdn = carve(2 * NJ * 256 * 2)
            G.alias = False
        print("group", kind, "arena top", top[0], "of", ARENA_BYTES)
        G.xres = view(G.o_xres, F32, (8, G.NT))
        G.xbf = view(G.o_xbf, BF16, (8, G.NT))
        G.qT = view(G.o_qT, BF16, (4, G.NT))
        G.kT = view(G.o_kT, BF16, (4, KT))
        G.V = view(G.o_V, BF16, ((G.T // 128) if kind == "p" else (G.nkh + 1), 512))
        G.poolw = view(G.o_poolw, BF16, (2, 4, 256))
        G.gT = view(G.o_gT, BF16, (NJ, G.W))
        G.wdn = view(G.o_wdn, BF16, (2, NJ, 256))
        if kind == "p":
            G.x_in, G.y_out, G.k_out, G.v_out, G.h_out, G.pl_out, G.cv_out = x_p, y_p, k_p, v_p, h_p, pl_p, cv_p
            G.maskH, G.reset = maskp, resetp
        else:
            G.x_in, G.y_out, G.k_out, G.v_out, G.h_out, G.pl_out, G.cv_out = x_s, y_s, k_s, v_s, h_s, pl_s, cv_s
            G.maskH, G.reset = masks, resets
        return G

    def kx(oc, ti): return ("xres", oc, ti)
    def kb(oc, ti): return ("A", "xbf", oc, ti)

    def load_x(G):
        import os
        KSUB = int(os.environ.get("KSUB", "9"))
        for ti in range(G.nt):
            c0 = ti * G.W
            banks = [PS() for _ in range(8)]
            for (co, nb, s, t0) in G.blocks[ti]:
                st, ks_ = sStage()
                dma("sp", st[0:nb, :], G.x_in[G.seqs[s], t0:t0 + nb, :], (), [ks_])
                for oc in range(8 if KSUB >= 1 else 0):
                    pb_, kp = banks[oc]
                    tr(pb_[:, co:co + nb], st[0:nb, oc * 128:(oc + 1) * 128], ident[0:nb, 0:nb], [ks_, "ident"], [kp])
            for oc in range(8 if KSUB >= 2 else 0):
                pb_, kp = banks[oc]
                act(G.xres[:, oc, c0:c0 + G.W], pb_[:, 0:G.W], AF.Copy, [kp], [kx(oc, ti)])
                if KSUB == 3:
                    cp(G.xbf[:, oc, c0:c0 + G.W], G.xres[:, oc, c0:c0 + G.W], [kx(oc, ti)], [kb(oc, ti)])
                elif KSUB == 5:
                    cp(G.xbf[:, oc, c0:c0 + G.W], pb_[:, 0:G.W], [kp], [("xbftest", oc, ti)])
                elif KSUB == 6:
                    cp(G.xbf[:, oc, c0:c0 + G.W], pb_[:, 0:G.W], [kp, kx(oc, ti)], [kb(oc, ti)])
                elif KSUB == 7:
                    t_, k_ = sF()
                    cp(t_[:, 0:G.W], pb_[:, 0:G.W], [kp], [k_])
                elif KSUB > 7:
                    cp(G.xbf[:, oc, c0:c0 + G.W], pb_[:, 0:G.W], [kp], [kb(oc, ti)])

    def store_y(G):
        import os
        KSUB = int(os.environ.get("KSUB", "9"))
        if KSUB < 4:
            return
        flush_pending(G)
        for ti in range(G.nt):
            c0 = ti * G.W
            for (co, nb, s, t0) in G.blocks[ti]:
                st, ks_ = sStage()
                for half in range(2):
                    pb_, kp = PS()
                    for o4 in range(4):
                        oc = half * 4 + o4
                        tr(pb_[0:nb, o4 * 128:(o4 + 1) * 128], G.xres[:, oc, c0 + co:c0 + co + nb], ident, [kx(oc, ti), "ident"], [kp])
                    if half == 0:
                        act(st[0:nb, 0:512], pb_[0:nb, :], AF.Copy, [kp], [ks_])
                    else:
                        cp(st[0:nb, 512:1024], pb_[0:nb, :], [kp], [ks_])
                dma("pool", G.y_out[G.seqs[s], t0:t0 + nb, :], st[0:nb, :], [ks_], [("yout",)])

    def pop_pending(G, n=1):
        for _ in range(n):
            if G.pending:
                G.pending.pop(0)[1]()

    def flush_pending(G, tile=None):
        keep = []
        for (t_id, fn) in G.pending:
            if tile is None or t_id == tile:
                fn()
            else:
                keep.append((t_id, fn))
        G.pending = keep

    def ln_tile(G, l, which, ti, ychunk, yscale=None):
        W = G.W
        c0 = ti * W
        s1, k1 = psb[6], ("ps", 6)
        s2, k2 = psb[7], ("ps", 7)
        prev_stats = None
        for oc in range(8):
            yp, ky = ychunk(oc)
            if prev_stats is not None:
                prev_stats()
            xr = G.xres[:, oc, c0:c0 + W]
            sc = INV_ALPHA if yscale is None else yscale(oc)
            stt(xr, yp[:, 0:W], sc, xr, ALU.mult, ALU.add, [ky, kx(oc, ti)] + ([] if yscale is None else ["psc"]), [kx(oc, ti)])
            rb_, kr = sB()
            act(rb_[:, 0:W], xr, AF.Copy, [kx(oc, ti)], [kr])
            sq_, kq = sB()
            act(sq_[:, 0:W], xr, AF.Square, [kx(oc, ti)], [kq])
            def prev_stats(oc=oc, rb_=rb_, sq_=sq_, kr=kr, kq=kq):
                mm(s1[:, 0:W], onesb, rb_[:, 0:W], oc == 0, oc == 7, [kr, "onesb"], [k1])
                mm(s2[:, 0:W], onesb, sq_[:, 0:W], oc == 0, oc == 7, [kq, "onesb"], [k2])
            pop_pending(G, 1)
        prev_stats()
        flush_pending(G)
        mean, km = FX(0)
        ts(mean[:, 0:W], s1[:, 0:W], 1.0 / D, None, ALU.mult, None, [k1], [km])
        msq, kq2 = FX(1)
        tt(msq[:, 0:W], mean[:, 0:W], mean[:, 0:W], ALU.mult, [km], [kq2])
        rstd, krs = msq, kq2
        stt(rstd[:, 0:W], s2[:, 0:W], 1.0 / D, msq[:, 0:W], ALU.mult, ALU.subtract, [k2, kq2], [krs])
        rsqrt(rstd[:, 0:W], rstd[:, 0:W], 0, [krs], [krs])
        for oc in range(8):
            def partB(oc=oc, mean=mean, rstd=rstd, km=km, krs=krs):
                xr = G.xres[:, oc, c0:c0 + W]
                t_, kt = sFr(2, NF)
                tt(t_[:, 0:W], xr, mean[:, 0:W], ALU.subtract, [kx(oc, ti), km], [kt])
                tt(t_[:, 0:W], t_[:, 0:W], rstd[:, 0:W], ALU.mult, [kt, krs], [kt])
                act(xr, t_[:, 0:W], AF.Identity, [kt, "lnp"], [kx(oc, ti)], bias=lnb[:, l, which, oc:oc + 1], scale=lng[:, l, which, oc:oc + 1])
                act(G.xbf[:, oc, c0:c0 + W], t_[:, 0:W], AF.Identity, [kt, "lnp"], [kb(oc, ti)], bias=lnb[:, l, which, oc:oc + 1], scale=lng[:, l, which, oc:oc + 1])
            G.pending.append((ti, partB))

    def ffn(G, l):
        W, ns, Ts = G.W, G.ns_t, G.Ts
        if G.kind == "s":
            for s in range(G.nseq):
                st, ks_ = sStage()
                for r in range(2):
                    dma("sp", st[r * NJ:(r + 1) * NJ, 0:128], state_conv[l, G.seqs[s], r].rearrange("(j p) -> j p", p=128), (), [ks_])
                pt, kp = PS(4, 6)
                tr(pt[:, 0:2 * NJ], st[0:2 * NJ, 0:128], ident[0:2 * NJ, 0:2 * NJ], [ks_, "ident"], [kp])
                cp(atail[:, :, s, :].rearrange("p j r -> p r j"), pt[:, 0:2 * NJ].rearrange("p (r j) -> p r j", r=2), [kp], ["atail"])
        else:
            memset("pool", atail, 0.0, ["atail"])
        for ti in range(G.nt):
            c0 = ti * W
            flush_pending(G, tile=ti)
            prev_s2 = None
            for j in range(NJ):
                wr, kw = sRing()
                dma("sp", wr, wup_bf[l, j], wkeys("wup", l), [kw])
                pa, ka = PS(0, 8)
                pb_, kpb = PS(0, 8)
                for kc in range(8):
                    mm(pa[:, 0:W], wr[:, kc, 0:128], G.xbf[:, kc, c0:c0 + W], kc == 0, kc == 7, [kw, kb(kc, ti)], [ka])
                for kc in range(8):
                    mm(pb_[:, 0:W], wr[:, kc, 128:256], G.xbf[:, kc, c0:c0 + W], kc == 0, kc == 7, [kw, kb(kc, ti)], [kpb])
                a_sb, kas = sFr(2, NF)
                a3 = a_sb[:, 0:ns * (Ts + 2)].rearrange("p (s t) -> p s t", s=ns)
                cp(a3[:, :, 0:2], atail[:, j, 0:ns, :], ["atail"], [kas], eng="pool")
                act(a3[:, :, 2:2 + Ts], pa[:, 0:W].rearrange("p (s t) -> p s t", s=ns), AF.Copy, [ka, kas], [kas])
                t0_, k0 = sHalf()
                t03 = t0_[:, 0:W].rearrange("p (s t) -> p s t", s=ns)
                act(t0_[:, 0:W], pa[:, 0:W], AF.Identity, [ka, "cw"], [k0], bias=cbias[:, l, j:j + 1], scale=cw[:, l, 2, j:j + 1])
                stt(t03, a3[:, :, 1:1 + Ts], cw[:, l, 1, j:j + 1], t03, ALU.mult, ALU.add, [kas, k0, "cw"], [k0])
                stt(t03, a3[:, :, 0:Ts], cw[:, l, 0, j:j + 1], t03, ALU.mult, ALU.add, [kas, k0, "cw"], [k0])
                cp(atail[:, j, 0:ns, :], a3[:, :, Ts:Ts + 2], [kas, "atail"], ["atail"], eng="pool")
                if prev_s2 is not None:
                    prev_s2()

                def prev_s2(j=j, t0_=t0_, k0=k0, pb_=pb_, kpb=kpb):
                    act(t0_[:, 0:W], t0_[:, 0:W], AF.Silu, [k0], [k0])
                    tt(G.gT[:, j, 0:W], t0_[:, 0:W], pb_[:, 0:W], ALU.mult, [k0, kpb], [("A", "gT", j)])
                pop_pending(G, 1)
            prev_s2()
            def ychunk(oc, l=l):
                py, kpy = PS(4, 6)
                qd, o2 = oc // 2, oc % 2
                if o2 == 0:
                    dma("sp", G.wdn[:, qd % 2], wdn_bf[l, qd], wkeys("wdn", l), [("A", "wdn", qd % 2)])
                for j in range(NJ):
                    mm(py[:, 0:W], G.wdn[:, qd % 2, j, o2 * 128:(o2 + 1) * 128], G.gT[:, j, 0:W], j == 0, j == NJ - 1,
                       [("A", "wdn", qd % 2), ("A", "gT", j)], [kpy])
                return py, kpy
            ln_tile(G, l, 1, ti, ychunk)
        for s in range(G.nseq):
            t44, k44 = sFr(3, NF)
            cp(t44[:, 0:2 * NJ].rearrange("p (r j) -> p r j", r=2), atail[:, :, s, :].rearrange("p j r -> p r j"), ["atail"], [k44])
            pt, kp = PS(4, 6)
            tr(pt[0:2 * NJ, 0:128], t44[:, 0:2 * NJ], ident, [k44, "ident"], [kp])
            st, ks_ = sStage()
            act(st[0:2 * NJ, 0:128], pt[0:2 * NJ, 0:128], AF.Copy, [kp], [ks_])
            for r in range(2):
                dma("pool", G.cv_out[l, G.seqs[s], r].rearrange("(j p) -> j p", p=128), st[r * NJ:(r + 1) * NJ, 0:128], [ks_], [("cvout",)])

    def mixer_pool(G, l):
        p = l // 2
        W = G.W
        flush_pending(G)
        for s in range(G.nseq):
            cend = (s + 1) * G.T
            tl = (cend - 1) // W
            st, ks_ = sStage()
            for half in range(2):
                pb_, kp = PS(4, 6)
                for o4 in range(4):
                    oc = half * 4 + o4
                    tr(pb_[0:15, o4 * 128:(o4 + 1) * 128], G.xres[:, oc, cend - 15:cend], ident, [kx(oc, tl), "ident"], [kp])
                if half == 0:
                    act(st[0:15, 0:512], pb_[0:15, :], AF.Copy, [kp], [ks_])
                else:
                    cp(st[0:15, 512:1024], pb_[0:15, :], [kp], [ks_])
            dma("pool", G.pl_out[p, G.seqs[s]], st[0:15, :], [ks_], [("plout",)])
        dma("sp", G.poolw, poolw_bf[p], wkeys("poolw", p), [("A", "poolw")])
        for ti in range(G.nt):
            c0 = ti * W
            for s in range(G.ns_t):
                sc0 = c0 + s * G.Ts
                Ts = G.Ts
                first = (G.kind == "s") or (ti == 0)
                if G.kind == "s":
                    sth, ksh = sStage()
                    dma("sp", sth[0:15, :], state_pool[p, G.seqs[s]], (), [ksh])
                    phh, kph = PS(6, 8)
                    for oc in range(8):
                        tr(phh[:, oc * 16:oc * 16 + 15], sth[0:15, oc * 128:(oc + 1) * 128], ident[0:15, 0:15], [ksh, "ident"], [kph])
                for oc in range(8):
                    g = oc // 2
                    w = POOL_W[g]
                    if first:
                        src, ksrc = sF()
                        if G.kind == "s":
                            cp(src[:, 1:16], phh[:, oc * 16:oc * 16 + 15], [kph], [ksrc])
                        else:
                            memset("pool", src[:, 0:16], 0.0, [ksrc])
                        cp(src[:, 16:16 + Ts], G.xres[:, oc, sc0:sc0 + Ts], [kx(oc, ti)], [ksrc], eng="pool")
                        cur = src[:, 16 - 15:16 + Ts]
                        kcur = [ksrc]
                    else:
                        cur = G.xres[:, oc, sc0 - 15:sc0 + Ts]
                        kcur = [kx(oc, ti), kx(oc, ti - 1)]
                    xt = cur[:, 15:15 + Ts]
                    lo = 15
                    step = 1
                    while step < w:
                        nxt, kn = sF()
                        n = lo - step + Ts
                        tt(nxt[:, 0:n], cur[:, step:step + n], cur[:, 0:n], ALU.add, kcur, [kn])
                        cur, kcur, lo = nxt[:, 0:n], [kn], lo - step
                        step *= 2
                    sw = cur[:, lo:lo + Ts]
                    outp = G.xbf[:, oc, sc0:sc0 + Ts]
                    stt(outp, sw, 1.0 / w, xt, ALU.mult, ALU.subtract, kcur + [kx(oc, ti)] + ([ksrc] if first else []), [kb(oc, ti)])
                    if G.kind == "p" and ti == 0:
                        fx, kf = sF()
                        tt(fx[:, 0:16], sw[:, 0:16], invc[:, g, :], ALU.mult, kcur + ["invc"], [kf])
                        tt(outp[:, 0:16], fx[:, 0:16], xt[:, 0:16], ALU.subtract, [kf, kx(oc, ti)] + ([ksrc] if first else []), [kb(oc, ti)])

        for ti in range(G.nt):
            c0 = ti * W

            def ychunk(oc):
                g = oc // 2
                py, kpy = PS(4, 6)
                for kc in range(2):
                    mm(py[:, 0:W], G.poolw[:, kc, g, (oc % 2) * 128:(oc % 2 + 1) * 128], G.xbf[:, 2 * g + kc, c0:c0 + W],
                       kc == 0, kc == 1, [("A", "poolw"), kb(2 * g + kc, ti)], [kpy])
                return py, kpy
            ln_tile(G, l, 0, ti, ychunk, yscale=lambda oc: pscv[:, p, oc:oc + 1])

    def mixer_ab(G, l):
        m = l // 2
        W, L = G.W, G.L
        flush_pending(G)
        nch = W // L
        mid = L // 2 - 1
        if G.kind == "s":
            for s in range(G.nseq):
                dma("sp", Sst_box[0][:, s, :, :], state_hgrn[m, G.seqs[s]].rearrange("h k v -> k h v"), (), [("S", s)])
        else:
            memset("pool", Sst_box[0][:, 0, :, :], 0.0, [("S", 0)])
        if G.kind == "s":
            for s in range(G.nseq):
                b = G.seqs[s]
                pass
        for ti in range(G.nt):
            c0 = ti * W
            aq_sb = {}; sig_sb = {}; sg_sb = {}; vtok = {}
            order = [0, 2, 6, 4, 1, 3, 7, 5, 8, 9, 10, 11, 12, 13]
            for g in order:
                wr, kw = sRing()
                dma("sp", wr, win_bf[m, g], wkeys("win", m), [kw])
                sec, hh0 = g // 2, (g % 2) * 2
                if sec in (0, 1, 3, 4, 5):
                    for hi in range(2):
                        h = hh0 + hi
                        pt, kp = PS()
                        for kc in range(8):
                            mm(pt[:, 0:W], wr[:, kc, hi * 128:(hi + 1) * 128], G.xbf[:, kc, c0:c0 + W], kc == 0, kc == 7, [kw, kb(kc, ti)], [kp])
                        if sec == 0:
                            t_, k_ = FX(hi); act(t_[:, 0:W], pt[:, 0:W], AF.Copy, [kp], [k_]); aq_sb[h] = (t_, k_)
                        elif sec == 1:
                            t_, k_ = FX(2 + hi); act(t_[:, 0:W], pt[:, 0:W], AF.Sigmoid, [kp], [k_]); sig_sb[h] = (t_, k_)
                        elif sec == 3:
                            t_, k_ = BX(hi); act(t_[:, 0:W], pt[:, 0:W], AF.Silu, [kp], [k_]); sg_sb[h] = (t_, k_)
                        elif sec == 4:
                            cp(G.qT[:, h, c0:c0 + W], pt[:, 0:W], [kp], [("A", "qT", h, ti)])
                        elif G.kind == "p":
                            act(G.kT[:, h, c0:c0 + W], pt[:, 0:W], AF.Copy, [kp], [("A", "kT", h, ti)])
                        else:
                            cp(G.knew[:, h, 0:W], pt[:, 0:W], [kp], [("A", "knew", h)])
                if sec in (2, 5, 6):
                    for bi, (co, nb, s, t0) in enumerate(G.blocks[ti]):
                        pt, kp = PS()
                        for kc in range(8):
                            mm(pt[0:nb, 0:256], G.xbf[:, kc, c0 + co:c0 + co + nb], wr[:, kc, :], kc == 0, kc == 7, [kw, kb(kc, ti)], [kp])
                        if sec == 2:
                            t_, k_ = BX(2 + bi // 2)
                            tv = t_[:, (bi % 2) * 256:(bi % 2 + 1) * 256]
                            vtok[(g % 2, bi)] = (tv, k_)
                            act(tv[0:nb, :], pt[0:nb, 0:256], AF.Copy, [kp], [k_])
                        else:
                            st, ks_ = sStage()
                            act(st[0:nb, 0:256], pt[0:nb, 0:256], AF.Copy, [kp], [ks_])
                            outt = G.k_out if sec == 5 else G.v_out
                            dma("pool", outt[m, G.seqs[s], t0:t0 + nb, (g % 2) * 256:(g % 2 + 1) * 256], st[0:nb, 0:256], [ks_], [("kvout",)])
                            if sec == 6:
                                if G.kind == "p":
                                    vt = t0 // 128
                                    cp(G.V[0:nb, vt, (g % 2) * 256:(g % 2 + 1) * 256], st[0:nb, 0:256], [ks_], [("A", "V", vt)])
                                else:
                                    cp(G.vnew[0:nb, s, (g % 2) * 256:(g % 2 + 1) * 256], st[0:nb, 0:256], [ks_], [("A", "vnew", s)])
                if g in (4, 5):
                    for hi in range(2):
                        h = (g % 2) * 2 + hi
                        hgrn_tile_head(G, m, ti, h, aq_sb[h], sig_sb[h], sg_sb[h], vtok, hi, g % 2)
            for h in range(4):
                cp(G.xbf[:, h, c0:c0 + W], hout[:, h, 0:W], [("hout", h)], [kb(h, ti)], eng="pool")
        for s in range(G.nseq):
            dma("pool", G.h_out[m, G.seqs[s]].rearrange("h k v -> k h v"), Sst_box[0][:, s, :, :], [("S", s)], [("hout",)])
        attention(G, m)
        for ti in range(G.nt):
            c0 = ti * W

            wst = {}

            def ychunk(oc, c0=c0, ti=ti, wst=wst):
                py, kpy = PS(4, 6)
                if oc % 2 == 0:
                    wst["w"] = sRing()
                    dma("sp", wst["w"][0], wout_bf[m, oc // 2], wkeys("wout", m), [wst["w"][1]])
                wr, kw = wst["w"]
                for kc in range(8):
                    mm(py[:, 0:W], wr[:, kc, (oc % 2) * 128:(oc % 2 + 1) * 128], G.xbf[:, kc, c0:c0 + W], kc == 0, kc == 7,
                       [kw, kb(kc, ti)], [kpy])
                return py, kpy
            ln_tile(G, l, 0, ti, ychunk)

    def hgrn_tile_head(G, m, ti, h, aq, sig, sg, vtok, hi, gp):
        W, L = G.W, G.L
        nch = W // L
        mid = L // 2 - 1
        c0 = ti * W
        aq_t, kaq = aq
        sig_t, ksig = sig
        sg_t, ksg = sg
        lf, klf = FX(4)
        act(lf[:, 0:W], sig_t[:, 0:W], AF.Ln, [ksig, "lb", "oml"], [klf], bias=lbv[:, m, h:h + 1], scale=omlv[:, m, h:h + 1])
        kin = sig_t
        ts(kin[:, 0:W], sig_t[:, 0:W], nomlv[:, m, h:h + 1], omlv[:, m, h:h + 1], ALU.mult, ALU.add, [ksig, "oml", "noml"], [ksig])
        bt, kbt = FX(5)
        P.add("dve", lambda e: e.tensor_tensor_scan(out=bt[:, 0:W], data0=G.reset[:, 0:W], data1=lf[:, 0:W], initial=0.0,
                                                    op0=ALU.mult, op1=ALU.add), [klf, "resetp", "resets"], [kbt])
        b3 = bt[:, 0:W].rearrange("p (c t) -> p c t", c=nch)
        dd = lf
        d3 = dd[:, 0:W].rearrange("p (c t) -> p c t", c=nch)
        tt(d3, b3, b3[:, :, mid:mid + 1].to_broadcast([128, nch, L]), ALU.subtract, [kbt, klf], [klf])
        sidx = hgrn_tile_head.cnt % 4; hgrn_tile_head.cnt += 1
        em = smallv[:, 3 * sidx + 0, 0:nch]; el = smallv[:, 3 * sidx + 1, 0:nch]; eb = smallv[:, 3 * sidx + 2, 0:nch]
        ksm = ("small", sidx)
        act(em, b3[:, :, mid], AF.Exp, [kbt], [ksm])
        act(el, d3[:, :, L - 1], AF.Exp, [klf, ksm], [ksm])
        act(eb, b3[:, :, L - 1], AF.Exp, [kbt, ksm], [ksm])
        e1 = bt
        act(e1[:, 0:W], dd[:, 0:W], AF.Exp, [klf, kbt, ksm], [kbt])
        qt, kqt = BX(4)
        tt(qt[:, 0:W], aq_t[:, 0:W], e1[:, 0:W], ALU.mult, [kaq, kbt], [kqt])
        act(e1[:, 0:W], dd[:, 0:W], AF.Exp, [klf, kbt, kqt], [kbt], scale=-1.0)
        kt, kkt = BX(5)
        tt(kt[:, 0:W], kin[:, 0:W], e1[:, 0:W], ALU.mult, [ksig, kbt], [kkt])
        oacc = aq_t
        koacc = kaq
        for bi, (co, nb, s, t0) in enumerate(G.blocks[ti]):
            vt_t, kvt = vtok[(gp, bi)]
            vv = vt_t[0:nb, hi * 128:(hi + 1) * 128]
            ptk, kptk = PS()
            ptkb = ptk[:, 0:256].bitcast(BF16)
            tr(ptkb[0:nb, 0:128], kt[:, co:co + nb], identb, [kkt, "identb"], [kptk])
            b7, kktok = BX(7)
            ktok = b7[:, (bi % 2) * 256:(bi % 2) * 256 + 128]
            cp(ktok[0:nb, 0:128], ptkb[0:nb, 0:128], [kptk], [kktok])
            psc, kpsc = PS()
            mm(psc[0:nb, 0:nb], kt[:, co:co + nb], qt[:, co:co + nb], True, True, [kkt, kqt], [kpsc])
            scm, kscm = b7[:, (bi % 2) * 256 + 128:(bi % 2) * 256 + 256], kktok
            tt(scm[0:nb, 0:nb], psc[0:nb, 0:nb], G.maskH[0:nb, 0:nb], ALU.mult, [kpsc, "maskp", "masks"], [kscm])
            po, kpo = PS()
            mm(po[:, 0:nb], vv, scm[0:nb, 0:nb], True, True, [kvt, kscm], [kpo])
            act(oacc[:, co:co + nb], po[:, 0:nb], AF.Copy, [kpo, kqt], [koacc])
            for ci in range(nb // L):
                c = (co + ci * L) // L
                S = Sst_box[0][:, s, h, :]
                kS = ("S", s)
                act(Sbf, S, AF.Copy, [kS, ksm], ["Sbf"], scale=em[:, c:c + 1])
                px, kpx = PS()
                mm(px[:, 0:L], Sbf, qt[:, co + ci * L:co + (ci + 1) * L], True, True, ["Sbf", kqt], [kpx])
                tt(oacc[:, co + ci * L:co + (ci + 1) * L], oacc[:, co + ci * L:co + (ci + 1) * L], px[:, 0:L], ALU.add, [koacc, kpx], [koacc])
                pu, kpu = PS()
                mm(pu[:, 0:128], ktok[ci * L:(ci + 1) * L, 0:128], vv[ci * L:(ci + 1) * L, :], True, True, [kktok, kvt], [kpu])
                tmp, ktmp = FX(6)
                ts(tmp[:, 0:128], pu[:, 0:128], el[:, c:c + 1], None, ALU.mult, None, [kpu, ksm], [ktmp])
                stt(S, S, eb[:, c:c + 1], tmp[:, 0:128], ALU.mult, ALU.add, [kS, ksm, ktmp], [kS])
        sq, ksq = BX(6)
        act(sq[:, 0:W], oacc[:, 0:W], AF.Square, [koacc], [ksq])
        pss, kpss = PS()
        mm(pss[:, 0:W], onesb, sq[:, 0:W], True, True, [ksq, "onesb"], [kpss])
        rs, krs = FX(6)
        rsqrt(rs[:, 0:W], pss[:, 0:W], 1, [kpss], [krs])
        stt(rs[:, 0:W], oacc[:, 0:W], hgv[:, m, h:h + 1], rs[:, 0:W], ALU.mult, ALU.mult, [koacc, krs, "hg"], [krs])
        tt(hout[:, h, 0:W], rs[:, 0:W], sg_t[:, 0:W], ALU.mult, [krs, ksg], [("hout", h)])
    hgrn_tile_head.cnt = 0

    def attention(G, m):
        T, Pn = G.T, G.P
        QW = min(512, T)
        nq = T // QW
        for s in range(G.nseq):
            b = G.seqs[s]
            scol = s * T
            if G.kind == "s":
                for kt_ in range(G.nkh):
                    st, ks_ = sStage()
                    dma("sp", st[:, 0:512], cache_k[m, b, kt_ * 128:(kt_ + 1) * 128, :], (), [ks_])
                    pt, kp = PS(4, 8)
                    for h in range(4):
                        tr(pt[:, h * 128:(h + 1) * 128], st[:, h * 128:(h + 1) * 128], ident, [ks_, "ident"], [kp])
                    act(G.kT[:, :, kt_ * 128:(kt_ + 1) * 128], pt[:, :].rearrange("p (h k) -> p h k", h=4), AF.Copy, [kp], [("A", "kTh", kt_)])
                    st2, ks2 = sStage()
                    dma("sp", st2[:, 0:512], cache_v[m, b, kt_ * 128:(kt_ + 1) * 128, :], (), [ks2])
                    cp(G.V[:, kt_, :], st2[:, 0:512], [ks2], [("A", "Vh", kt_)], eng="pool")
                for h in range(4):
                    cp(G.kT[:, h, Pn:Pn + T], G.knew[:, h, scol:scol + T], [("A", "knew", h)], [("A", "kTn", h)], eng="pool")
                cp(G.V[0:T, G.nkh, :], G.vnew[0:T, s, :], [("A", "vnew", s)], [("A", "Vn")], eng="pool")
            ktiles = []
            for kt_ in range(G.nkh):
                ktiles.append((kt_ * 128, 128, kt_, True))
            if G.kind == "p":
                for kt_ in range(T // 128):
                    ktiles.append((kt_ * 128, 128, kt_, False))
            else:
                ktiles.append((Pn, T, G.nkh, False))
            for h in range(4):
                for qi in range(nq):
                    q0 = qi * QW
                    ti = (scol + q0) // G.W
                    accs = [(psb[i], ("ps", i)) for i in range(4)]
                    vis = []
                    for (k0, nk, vt, hist) in ktiles:
                        if hist:
                            vis.append((k0, nk, vt, hist, 0))
                        else:
                            krel = k0 - Pn
                            if krel >= q0 + QW:
                                continue
                            vis.append((k0, nk, vt, hist, max(0, krel - q0)))
                    prev_pv = None
                    for idx, (k0, nk, vt, hist, qs) in enumerate(vis):
                        first, last = idx == 0, idx == len(vis) - 1
                        nqc = QW - qs
                        qsl = slice(scol + q0 + qs, scol + q0 + QW)
                        if G.kind == "p":
                            kkeys = [("A", "kT", h, (k0) // G.W)]
                            vkeys = [("A", "V", vt)]
                        else:
                            kkeys = [("A", "kTh", vt)] if hist else [("A", "kTn", h)]
                            vkeys = [("A", "Vh", vt)] if hist else [("A", "Vn")]
                        pp = []
                        for half in range(2):
                            psc, kpsc = PS(4, 8)
                            pr = slice(half * 64, (half + 1) * 64)
                            mm(psc[0:nk, 0:nqc], G.kT[pr, h, k0:k0 + nk], G.qT[pr, h, qsl], True, True,
                               kkeys + [("A", "qT", h, ti)], [kpsc])
                            pt_, kpt = sB()
                            act(pt_[0:nk, 0:nqc], psc[0:nk, 0:nqc], AF.Exp, [kpsc], [kpt], scale=0.125)
                            if G.kind == "p":
                                if not hist:
                                    krel = k0 - Pn
                                    for qb in range(qs, QW, 128):
                                        d = krel - (q0 + qb)
                                        if d in (0, -128):
                                            di = 0 if d == 0 else 1
                                            sl = slice(qb - qs, qb - qs + 128)
                                            tt(pt_[0:nk, sl], pt_[0:nk, sl], Ep[0:nk, di, h, :], ALU.mult, [kpt, "Ep"], [kpt])
                            else:
                                if hist and k0 == Pn - 128:
                                    tt(pt_[0:nk, 0:nqc], pt_[0:nk, 0:nqc], Esh[0:nk, h, :], ALU.mult, [kpt, "Es"], [kpt])
                                if not hist:
                                    tt(pt_[0:nk, 0:nqc], pt_[0:nk, 0:nqc], Esn[0:nk, h, :], ALU.mult, [kpt, "Es"], [kpt])
                            pp.append((pt_, kpt))
                        if prev_pv is not None:
                            prev_pv()

                        def prev_pv(pp=pp, nk=nk, vt=vt, qs=qs, nqc=nqc, first=first, last=last, vkeys=vkeys):
                            for half in range(2):
                                pt_, kpt = pp[half]
                                (po, kpo), (psm, kpsm) = accs[2 * half], accs[2 * half + 1]
                                mm(po[:, qs:QW], G.V[0:nk, vt, h * 128:(h + 1) * 128], pt_[0:nk, 0:nqc], first, last, vkeys + [kpt], [kpo])
                                mm(psm[:, qs:QW], onesb[0:nk, :], pt_[0:nk, 0:nqc], first, last, ["onesb", kpt], [kpsm])
                    prev_pv()
                    t1, k1 = FX(0); t2, k2 = FX(1)
                    P.add("dve", lambda e, o=t1, i=accs[1][0]: e.reciprocal(out=o[:, 0:QW], in_=i[:, 0:QW]), [accs[1][1]], [k1])
                    tt(t1[:, 0:QW], accs[0][0][:, 0:QW], t1[:, 0:QW], ALU.mult, [accs[0][1], k1], [k1])
                    P.add("dve", lambda e, o=t2, i=accs[3][0]: e.reciprocal(out=o[:, 0:QW], in_=i[:, 0:QW]), [accs[3][1]], [k2])
                    tt(t2[:, 0:QW], accs[2][0][:, 0:QW], t2[:, 0:QW], ALU.mult, [accs[2][1], k2], [k2])
                    stt(t1[:, 0:QW], t2[:, 0:QW], neglam[:, m:m + 1], t1[:, 0:QW], ALU.mult, ALU.add, [k1, k2, "nl"], [k1])
                    sq, ksq = sB()
                    act(sq[:, 0:QW], t1[:, 0:QW], AF.Square, [k1], [ksq])
                    pss, kpss = PS(4, 8)
                    mm(pss[:, 0:QW], onesb, sq[:, 0:QW], True, True, [ksq, "onesb"], [kpss])
                    rsqrt(t2[:, 0:QW], pss[:, 0:QW], 1, [kpss, k2], [k2])
                    P.add("dve", lambda e, o=G.xbf[:, 4 + h, scol + q0:scol + q0 + QW], a=t1, r=t2, g_=dgv[:, m:m + 1]:
                          e.scalar_tensor_tensor(out=o, in0=a[:, 0:QW], scalar=g_, in1=r[:, 0:QW], op0=ALU.mult, op1=ALU.mult),
                          [k1, k2, "dg"], [kb(4 + h, ti)])

    import os
    LEVEL = int(os.environ.get("KLEVEL", "99"))
    KL = int(os.environ.get("KLAYERS", "4"))
    KG = os.environ.get("KGROUPS", "ps")
    setup()
    cast_order = [("win", 0), ("wout", 0), ("wup", 0), ("wdn", 0), ("poolw", 0), ("wup", 1), ("wdn", 1),
                  ("win", 1), ("wout", 1), ("wup", 2), ("wdn", 2), ("poolw", 1), ("wup", 3), ("wdn", 3)]
    castf = {"win": cast_win, "wout": cast_wout, "wup": cast_wup, "wdn": cast_wdn, "poolw": cast_poolw}
    if LEVEL >= 1:
        for nm, i in cast_order[:5]:
            castf[nm](i)
    cast_later = {0: cast_order[5:7], 1: cast_order[7:12], 2: cast_order[12:]}

    groups = [("p", [i]) for i in range(NPS)]
    if NSS > 0:
        groups.append(("s", list(range(NSS))))
    groups = [g_ for g_ in groups if g_[0] in KG]
    for gi, (kind, ids) in enumerate(groups):
        if LEVEL < 2:
            break
        barrier()
        G = make_group(kind, ids)
        if kind == "s":
            G.o_knew = carve(4 * G.NT * 2); G.knew = view(G.o_knew, BF16, (4, G.NT))
            G.o_vnew = carve(G.nseq * 512 * 2); G.vnew = view(G.o_vnew, BF16, (G.nseq, 512))
        load_x(G)
        for l in range(DEPTH if LEVEL >= 3 else 0):
            if l >= KL:
                break
            if gi == 0:
                for nm, i in cast_later.get(l, []):
                    castf[nm](i)
            if G.alias:
                arena_barrier()
            if l % 2 == 0:
                mixer_ab(G, l)
            else:
                mixer_pool(G, l)
            if G.alias:
                arena_barrier()
            ffn(G, l)
        store_y(G)
    barrier()

    P.finalize(sems)
    with nc.Block() as block:
        @block.sync
        def _(e):
            P.run_stream("sp", e)

        @block.scalar
        def _(e):
            P.run_stream("act", e)

        @block.vector
        def _(e):
            P.run_stream("dve", e)

        @block.gpsimd
        def _(e):
            P.run_stream("pool", e)

        @block.tensor
        def _(e):
            P.run_stream("pe", e)
    es.close()
    return nc, len(P.ops)


_CACHE = {}


def kernel(**inp):
    import os
    NCORES = int(os.environ.get("KCORES", "8"))
    x_prompt = np.asarray(inp["x_prompt"], np.float32)
    x_sample = np.asarray(inp["x_sample"], np.float32)
    B, SEQ, _ = x_prompt.shape
    DB, DS, _ = x_sample.shape
    PAST = inp["cache_k"].shape[2]
    NPS, NSS = B // NCORES, DB // NCORES
    key = (NPS, SEQ, NSS, DS, PAST)
    if key not in _CACHE:
        _CACHE[key] = build(*key)
    nc, nops = _CACHE[key]
    consts = host_consts(NSS, DS)
    ck = np.asarray(inp["cache_k"], np.float32).reshape(2, DB, PAST, 512)
    cv = np.asarray(inp["cache_v"], np.float32).reshape(2, DB, PAST, 512)
    sh = np.asarray(inp["state_hgrn"], np.float32)
    sp_ = np.asarray(inp["state_pool"], np.float32)
    sc = np.asarray(inp["state_ffn_conv"], np.float32)
    shared = {k: np.ascontiguousarray(np.asarray(inp[k], np.float32)) for k in
              ("w_in_ab", "w_out_ab", "lb_logits", "hgrn_norm_g", "diff_norm_g", "rel_bias", "pool_w", "pool_scale",
               "ffn_w_up", "ffn_conv_w", "ffn_conv_b", "ffn_w_down", "ln1_g", "ln1_b", "ln2_g", "ln2_b")}
    shared["diff_lambda"] = np.ascontiguousarray(np.asarray(inp["diff_lambda"], np.float32).reshape(2, 256))
    shared.update(consts)
    in_maps = []
    for c in range(NCORES):
        ps, ss = slice(c * NPS, (c + 1) * NPS), slice(c * NSS, (c + 1) * NSS)
        d = dict(shared)
        d["x_p"] = np.ascontiguousarray(x_prompt[ps])
        d["x_s"] = np.ascontiguousarray(x_sample[ss])
        d["cache_k"] = np.ascontiguousarray(ck[:, ss])
        d["cache_v"] = np.ascontiguousarray(cv[:, ss])
        d["state_hgrn"] = np.ascontiguousarray(sh[:, ss])
        d["state_pool"] = np.ascontiguousarray(sp_[:, ss])
        d["state_conv"] = np.ascontiguousarray(sc[:, ss])
        in_maps.append(d)
    res = run_bass_kernel_spmd(nc, in_maps, core_ids=list(range(NCORES)))
    R = res.results

    def cat(name, axis):
        return np.concatenate([np.asarray(r[name], np.float32) for r in R], axis=axis)
    y_p = cat("y_p", 0)
    y_s = cat("y_s", 0)
    k_p = cat("k_p", 1).reshape(2, B, SEQ, 4, 128)
    v_p = cat("v_p", 1).reshape(2, B, SEQ, 4, 128)
    h_p = cat("h_p", 1)
    pl_p = cat("pl_p", 1)
    cv_p = cat("cv_p", 1)
    k_s = cat("k_s", 1).reshape(2, DB, DS, 4, 128)
    v_s = cat("v_s", 1).reshape(2, DB, DS, 4, 128)
    h_s = cat("h_s", 1)
    pl_s = cat("pl_s", 1)
    cv_s = cat("cv_s", 1)
    return (y_p, y_s, k_p, v_p, h_p, pl_p, cv_p, k_s, v_s, h_s, pl_s, cv_s)
```
